# Optimizing a Trainium2 kernel written in Bass

```python
import math
import jax, jax.numpy as jnp
from jax import lax
import numpy as np

D_MODEL = 1024
BATCH = 1
SEQ = 16384
DEPTH = 4

GRID_W = 64
CTX_LEN = 256
HEAD_DIM = 64
A_HEADS = 8
A_KV_HEADS = 2
B_HEADS = 8
NB_KH_MAX = 8
NB_KW = 16
C_HEADS = 16
C_KV_HEADS = 2
C_WINDOW = 128
Q_BLOCK = 128
D_FF = 2816
CONV_W = 3
ROPE_THETA = 10000.0
LN_EPS = 1e-5
RMS_EPS = 1e-6
NEG = -1e30
DN_ALPHA = (2 * DEPTH) ** 0.25
DN_BETA = (8 * DEPTH) ** -0.25
N_EVEN = (DEPTH + 1) // 2
N_ODD = DEPTH // 2
A_Q = A_HEADS * HEAD_DIM
A_KV = A_KV_HEADS * HEAD_DIM
B_W = B_HEADS * HEAD_DIM
EVEN_IN = A_Q + 2 * A_KV + 3 * B_W
EVEN_MIX = A_Q + B_W
C_Q = C_HEADS * HEAD_DIM
C_KV = C_KV_HEADS * HEAD_DIM
ODD_IN = C_Q + 2 * C_KV
ODD_MIX = C_Q

kernel_name = 'hybrid_dit_gqa_natten_swa_convffn'


def layer_norm(x, g, b):
    xf = x.astype(jnp.float32)
    mu = jnp.mean(xf, -1, keepdims=True)
    var = jnp.mean(jnp.square(xf - mu), -1, keepdims=True)
    return ((xf - mu) * lax.rsqrt(var + LN_EPS) * g + b).astype(x.dtype)


def rms_norm(x, g):
    xf = x.astype(jnp.float32)
    return (xf * lax.rsqrt(jnp.mean(xf * xf, -1, keepdims=True) + RMS_EPS) * g).astype(x.dtype)


def axial_rope_tables(n_tokens):
    t = jnp.arange(n_tokens, dtype=jnp.int32)
    row = (t // GRID_W).astype(jnp.float32)
    col = (t % GRID_W).astype(jnp.float32)
    half = HEAD_DIM // 2
    inv_freq = ROPE_THETA ** (-jnp.arange(0, half, 2, dtype=jnp.float32) / half)
    ang_r = row[:, None] * inv_freq
    ang_c = col[:, None] * inv_freq
    ang = jnp.concatenate([ang_r, ang_r, ang_c, ang_c], -1)
    return jnp.cos(ang), jnp.sin(ang)


def apply_rope(x, cos, sin):
    x1, x2, x3, x4 = jnp.split(x, 4, axis=-1)
    rot = jnp.concatenate([-x2, x1, -x4, x3], -1)
    return (x * cos[None, :, None, :] + rot * sin[None, :, None, :]).astype(x.dtype)


def heads(z, n):
    return z.reshape(z.shape[0], z.shape[1], n, HEAD_DIM)


def modulate(x, shift, scale):
    return x * (1 + scale) + shift


def gqa_scores(q, k):
    return jnp.einsum('bqkgd,btkd->bkgqt', q, k).astype(jnp.float32)


def gqa_out(p, v):
    return jnp.einsum('bkgqt,btkd->bqkgd', p.astype(v.dtype), v)


def global_gqa(q_lat, k_lat, v_lat, q_ctx, k_ctx, v_ctx, ctx_out):
    B, S, H, d = q_lat.shape
    KV = k_lat.shape[2]
    G = H // KV
    scale = d ** -0.5
    k_all = jnp.concatenate([k_ctx, k_lat], 1)
    v_all = jnp.concatenate([v_ctx, v_lat], 1)
    qb = (q_lat * scale).reshape(B, S // Q_BLOCK, Q_BLOCK, KV, G, d).swapaxes(0, 1)

    def block(q):
        p = jax.nn.softmax(gqa_scores(q, k_all), axis=-1)
        return gqa_out(p, v_all)

    o = lax.map(block, qb)
    y_lat = o.swapaxes(0, 1).reshape(B, S, H * d)
    y_ctx = None
    if ctx_out:
        qc = (q_ctx * scale).reshape(B, q_ctx.shape[1], KV, G, d)
        p = jax.nn.softmax(gqa_scores(qc, k_ctx), axis=-1)
        y_ctx = gqa_out(p, v_ctx).reshape(B, q_ctx.shape[1], H * d)
    return y_lat, y_ctx


def neighbourhood_attention(q_lat, k_lat, v_lat, q_ctx, k_ctx, v_ctx, rpb, ctx_out):
    B, S, H, d = q_lat.shape
    rows = S // GRID_W
    kh = min(NB_KH_MAX, rows)
    rb = Q_BLOCK // GRID_W
    scale = d ** -0.5
    col = jnp.arange(GRID_W)
    cs = jnp.clip(col - NB_KW // 2, 0, GRID_W - NB_KW)
    kcol = cs[:, None] + jnp.arange(NB_KW)
    dcol = kcol - col[:, None]
    qb = (q_lat * scale).reshape(B, rows // rb, rb, GRID_W, H, d).swapaxes(0, 1)
    n_nb = kh * NB_KW

    def block(args):
        qblk, blk = args
        row = blk * rb + jnp.arange(rb)
        rs = jnp.clip(row - kh // 2, 0, rows - kh)
        krow = rs[:, None] + jnp.arange(kh)
        drow = krow - row[:, None]
        idx = krow[:, None, :, None] * GRID_W + kcol[None, :, None, :]
        kn = k_lat[:, idx]
        vn = v_lat[:, idx]
        bias = rpb[:, drow[:, None, :, None] + NB_KH_MAX - 1, dcol[None, :, None, :] + NB_KW - 1]
        s_nb = jnp.einsum('brwhd,brwyxhd->bhrwyx', qblk, kn).astype(jnp.float32) + bias[None]
        s_nb = s_nb.reshape(B, H, rb, GRID_W, n_nb)
        s_ctx = jnp.einsum('brwhd,bthd->bhrwt', qblk, k_ctx).astype(jnp.float32)
        p = jax.nn.softmax(jnp.concatenate([s_nb, s_ctx], -1), axis=-1)
        p_nb = p[..., :n_nb].reshape(B, H, rb, GRID_W, kh, NB_KW).astype(vn.dtype)
        p_ctx = p[..., n_nb:].astype(v_ctx.dtype)
        return (jnp.einsum('bhrwyx,brwyxhd->brwhd', p_nb, vn)
                + jnp.einsum('bhrwt,bthd->brwhd', p_ctx, v_ctx))

    o = lax.map(block, (qb, jnp.arange(rows // rb)))
    y_lat = o.swapaxes(0, 1).reshape(B, S, H * d)
    y_ctx = None
    if ctx_out:
        s = jnp.einsum('bqhd,bthd->bhqt', q_ctx * scale, k_ctx).astype(jnp.float32)
        p = jax.nn.softmax(s, axis=-1).astype(v_ctx.dtype)
        y_ctx = jnp.einsum('bhqt,bthd->bqhd', p, v_ctx).reshape(B, q_ctx.shape[1], H * d)
    return y_lat, y_ctx


def window_gqa_sink(q_lat, k_lat, v_lat, q_ctx, k_ctx, v_ctx, sink, ctx_out):
    B, S, H, d = q_lat.shape
    KV = k_lat.shape[2]
    G = H // KV
    C = k_ctx.shape[1]
    scale = d ** -0.5
    span = Q_BLOCK + 2 * C_WINDOW
    pad = ((0, 0), (C_WINDOW, C_WINDOW), (0, 0), (0, 0))
    kp = jnp.pad(k_lat, pad)
    vp = jnp.pad(v_lat, pad)
    sink5 = sink.astype(jnp.float32).reshape(1, KV, G, 1, 1)
    qb = (q_lat * scale).reshape(B, S // Q_BLOCK, Q_BLOCK, KV, G, d).swapaxes(0, 1)
    qpos = jnp.arange(Q_BLOCK)
    kpos = jnp.arange(span) - C_WINDOW
    sink_q = jnp.broadcast_to(sink5, (B, KV, G, Q_BLOCK, 1))

    def block(args):
        q, blk = args
        start = blk * Q_BLOCK
        kb = lax.dynamic_slice_in_dim(kp, start, span, axis=1)
        vb = lax.dynamic_slice_in_dim(vp, start, span, axis=1)
        abs_k = start + kpos
        valid = ((jnp.abs(kpos[None, :] - qpos[:, None]) <= C_WINDOW)
                 & ((abs_k >= 0) & (abs_k < S))[None, :])
        s_loc = jnp.where(valid, gqa_scores(q, kb), NEG)
        s_ctx = gqa_scores(q, k_ctx)
        p = jax.nn.softmax(jnp.concatenate([s_loc, s_ctx, sink_q], -1), axis=-1)
        return gqa_out(p[..., :span], vb) + gqa_out(p[..., span:span + C], v_ctx)

    o = lax.map(block, (qb, jnp.arange(S // Q_BLOCK)))
    y_lat = o.swapaxes(0, 1).reshape(B, S, H * d)
    y_ctx = None
    if ctx_out:
        qc = (q_ctx * scale).reshape(B, C, KV, G, d)
        s = jnp.concatenate([gqa_scores(qc, k_ctx), jnp.broadcast_to(sink5, (B, KV, G, C, 1))], -1)
        p = jax.nn.softmax(s, axis=-1)
        y_ctx = gqa_out(p[..., :C], v_ctx).reshape(B, C, H * d)
    return y_lat, y_ctx


def even_project(h, w_in, q_gain, k_gain):
    qa, ka, va, qb, kb, vb = jnp.split(
        h @ w_in, [A_Q, A_Q + A_KV, A_Q + 2 * A_KV, A_Q + 2 * A_KV + B_W, A_Q + 2 * A_KV + 2 * B_W], axis=-1)
    return (rms_norm(heads(qa, A_HEADS), q_gain), rms_norm(heads(ka, A_KV_HEADS), k_gain),
            heads(va, A_KV_HEADS), heads(qb, B_HEADS), heads(kb, B_HEADS), heads(vb, B_HEADS))


def even_mixer(h_lat, h_ctx, w_in, w_out, q_gain, k_gain, rpb, cos, sin, ctx_out):
    qa, ka, va, qb, kb, vb = even_project(h_lat, w_in, q_gain, k_gain)
    qa_c, ka_c, va_c, qb_c, kb_c, vb_c = even_project(h_ctx, w_in, q_gain, k_gain)
    qa = apply_rope(qa, cos, sin)
    ka = apply_rope(ka, cos, sin)
    ya, ya_c = global_gqa(qa, ka, va, qa_c, ka_c, va_c, ctx_out)
    yb, yb_c = neighbourhood_attention(qb, kb, vb, qb_c, kb_c, vb_c, rpb, ctx_out)
    y_lat = jnp.concatenate([ya, yb], -1) @ w_out
    y_ctx = (jnp.concatenate([ya_c, yb_c], -1) @ w_out) if ctx_out else None
    return y_lat, y_ctx


def odd_project(h, w_in):
    q, k, v = jnp.split(h @ w_in, [C_Q, C_Q + C_KV], axis=-1)
    return heads(q, C_HEADS), heads(k, C_KV_HEADS), heads(v, C_KV_HEADS)


def odd_mixer(h_lat, h_ctx, w_in, w_out, sink, cos, sin, ctx_out):
    q, k, v = odd_project(h_lat, w_in)
    qc, kc, vc = odd_project(h_ctx, w_in)
    q = apply_rope(q, cos, sin)
    k = apply_rope(k, cos, sin)
    y, yc = window_gqa_sink(q, k, v, qc, kc, vc, sink, ctx_out)
    return y @ w_out, ((yc @ w_out) if ctx_out else None)


def conv_ffn(h, w_up, conv_w, conv_b, w_down):
    u = h @ w_up
    L = u.shape[1]
    r = CONV_W // 2
    up = jnp.pad(u, ((0, 0), (r, r), (0, 0)))
    acc = conv_b
    for j in range(CONV_W):
        acc = acc + up[:, j:j + L] * conv_w[j]
    a, g = jnp.split(acc, 2, axis=-1)
    return (jax.nn.silu(g) * a) @ w_down


def setup_inputs(seed: int = 0) -> dict:
    key = jax.random.key(seed)
    ks = jax.random.split(key, 24)
    nrm = jax.random.normal
    f32 = jnp.float32
    return {
        'x': nrm(ks[0], (BATCH, SEQ, D_MODEL), f32),
        'c': nrm(ks[1], (BATCH, D_MODEL), f32),
        'ctx': nrm(ks[2], (BATCH, CTX_LEN, D_MODEL), f32),
        'c_ctx': nrm(ks[3], (D_MODEL,), f32),
        'ada_w': nrm(ks[4], (DEPTH, D_MODEL, 6 * D_MODEL), f32) * (0.5 * D_MODEL ** -0.5),
        'ada_b': nrm(ks[5], (DEPTH, 6 * D_MODEL), f32) * 0.02,
        'ln_g': 1.0 + 0.02 * nrm(ks[6], (DEPTH, 2, D_MODEL), f32),
        'ln_b': 0.02 * nrm(ks[7], (DEPTH, 2, D_MODEL), f32),
        'ev_w_in': nrm(ks[8], (N_EVEN, D_MODEL, EVEN_IN), f32) * D_MODEL ** -0.5,
        'ev_w_out': nrm(ks[9], (N_EVEN, EVEN_MIX, D_MODEL), f32) * (EVEN_MIX ** -0.5 * DN_BETA),
        'ev_q_gain': 1.0 + 0.02 * nrm(ks[10], (N_EVEN, HEAD_DIM), f32),
        'ev_k_gain': 1.0 + 0.02 * nrm(ks[11], (N_EVEN, HEAD_DIM), f32),
        'ev_rpb': 0.5 * nrm(ks[12], (N_EVEN, B_HEADS, 2 * NB_KH_MAX - 1, 2 * NB_KW - 1), f32),
        'od_w_in': nrm(ks[13], (N_ODD, D_MODEL, ODD_IN), f32) * D_MODEL ** -0.5,
        'od_w_out': nrm(ks[14], (N_ODD, ODD_MIX, D_MODEL), f32) * (ODD_MIX ** -0.5 * DN_BETA),
        'od_sink': nrm(ks[15], (N_ODD, C_HEADS), f32),
        'ffn_w_up': nrm(ks[16], (DEPTH, D_MODEL, 2 * D_FF), f32) * D_MODEL ** -0.5,
        'ffn_conv_w': nrm(ks[17], (DEPTH, CONV_W, 2 * D_FF), f32) * CONV_W ** -0.5,
        'ffn_conv_b': 0.02 * nrm(ks[18], (DEPTH, 2 * D_FF), f32),
        'ffn_w_down': nrm(ks[19], (DEPTH, D_FF, D_MODEL), f32) * (D_FF ** -0.5 * DN_BETA),
    }


def reference(x, c, ctx, c_ctx, ada_w, ada_b, ln_g, ln_b, ev_w_in, ev_w_out, ev_q_gain, ev_k_gain, ev_rpb,
              od_w_in, od_w_out, od_sink, ffn_w_up, ffn_conv_w, ffn_conv_b, ffn_w_down):
    S = x.shape[1]
    cos, sin = axial_rope_tables(S)
    x_lat, x_ctx = x, ctx
    for l in range(DEPTH):
        ctx_out = l < DEPTH - 1
        i = l // 2
        m_lat = [m[:, None, :] for m in jnp.split(jax.nn.silu(c) @ ada_w[l] + ada_b[l], 6, axis=-1)]
        m_ctx = jnp.split(jax.nn.silu(c_ctx) @ ada_w[l] + ada_b[l], 6, axis=-1)
        h_lat = modulate(x_lat, m_lat[0], m_lat[1])
        h_ctx = modulate(x_ctx, m_ctx[0], m_ctx[1])
        if l % 2 == 0:
            y_lat, y_ctx = even_mixer(h_lat, h_ctx, ev_w_in[i], ev_w_out[i], ev_q_gain[i], ev_k_gain[i],
                                      ev_rpb[i], cos, sin, ctx_out)
        else:
            y_lat, y_ctx = odd_mixer(h_lat, h_ctx, od_w_in[i], od_w_out[i], od_sink[i], cos, sin, ctx_out)
        x_lat = layer_norm(DN_ALPHA * x_lat + m_lat[2] * y_lat, ln_g[l, 0], ln_b[l, 0])
        f_lat = conv_ffn(modulate(x_lat, m_lat[3], m_lat[4]), ffn_w_up[l], ffn_conv_w[l], ffn_conv_b[l], ffn_w_down[l])
        x_lat = layer_norm(DN_ALPHA * x_lat + m_lat[5] * f_lat, ln_g[l, 1], ln_b[l, 1])
        if ctx_out:
            x_ctx = layer_norm(DN_ALPHA * x_ctx + m_ctx[2] * y_ctx, ln_g[l, 0], ln_b[l, 0])
            f_ctx = conv_ffn(modulate(x_ctx, m_ctx[3], m_ctx[4]), ffn_w_up[l], ffn_conv_w[l], ffn_conv_b[l], ffn_w_down[l])
            x_ctx = layer_norm(DN_ALPHA * x_ctx + m_ctx[5] * f_ctx, ln_g[l, 1], ln_b[l, 1])
    return x_lat
```

```python
import numpy as np
import ml_dtypes
import concourse.bass as bass
import concourse.mybir as mybir
from concourse.bass_utils import run_bass_kernel_spmd

F32 = mybir.dt.float32
BF16 = mybir.dt.bfloat16
ALU = mybir.AluOpType
AF = mybir.ActivationFunctionType
AX = mybir.AxisListType
NPBF = ml_dtypes.bfloat16

NCORES = 8
D = 1024
S = 16384
DEPTH = 4
GRID_W = 64
CTX = 256
HD = 64
D_FF = 2816
LN_EPS = 1e-5
RMS_EPS = 1e-6
DN_ALPHA = (2 * DEPTH) ** 0.25
TL = 16
TC = 2
TT = TL + TC
NTOK = TT * 128
LTOK = TL * 128


class Buf:
    __slots__ = ("name", "t", "lw", "rd", "psum")

    def __init__(self, name, t, psum=False):
        self.name = name
        self.t = t
        self.psum = psum
        self.lw = None
        self.rd = {}

    def __getitem__(self, k):
        return self.t[k]


class Sync:
    ENGS = ("pe", "act", "dve", "pool", "sp")

    def __init__(self):
        self.needed = set()

    def begin(self, nc, dry):
        self.nc = nc
        self.dry = dry
        self.idx = {e: 0 for e in self.ENGS}
        self.marks = {e: 0 for e in self.ENGS}
        self.markval = {}
        self.seen = {e: {o: 0 for o in self.ENGS} for e in self.ENGS}
        self.seen_dma = {}
        self.n_wait = 0
        self.ndsem = 0
        self.dsem_cnt = {}
        if not dry:
            self.eng = {"pe": nc.tensor, "act": nc.scalar, "dve": nc.vector,
                        "pool": nc.gpsimd, "sp": nc.sync}
            self.sem = {e: nc.alloc_semaphore("s_" + e) for e in self.ENGS}
            self.dsems = []

    def _deps(self, reads, writes):
        deps = set()
        for b in reads:
            if b.lw is not None:
                deps.add(b.lw)
            if b.psum:
                for r in b.rd.values():
                    deps.add(r)
        for b in writes:
            if b.lw is not None:
                deps.add(b.lw)
            for r in b.rd.values():
                deps.add(r)
        return deps

    def _post(self, me, reads, writes):
        key = me[0] if me[0] != "dma" else ("dma", me[1])
        for b in reads:
            b.rd[key] = me
        for b in writes:
            b.lw = me
            b.rd = {}

    def op(self, e, fn, reads=(), writes=()):
        self._waits(e, self._deps(reads, writes))
        me = (e, self.idx[e])
        self.idx[e] += 1
        if me in self.needed:
            self.marks[e] += 1
            self.markval[me] = self.marks[e]
        if not self.dry:
            ins = fn()
            if me in self.needed:
                ins.then_inc(self.sem[e], 1)
        self._post(me, reads, writes)
        return me

    def _waits(self, e, deps):
        best = {}
        for d in deps:
            if d[0] == "dma":
                self._dma_wait(e, d)
                continue
            pe_, i_ = d
            if pe_ == e and e == "pe":
                continue
            if pe_ not in best or best[pe_] < i_:
                best[pe_] = i_
        for pe_, i_ in best.items():
            if self.dry:
                self.needed.add((pe_, i_))
                continue
            v = self.markval[(pe_, i_)]
            if self.seen[e][pe_] >= v:
                continue
            self.seen[e][pe_] = v
            self.eng[e].wait_ge(self.sem[pe_], v)
            self.n_wait += 1

    def new_dma_sem(self):
        k = self.ndsem
        self.ndsem += 1
        self.dsem_cnt[k] = 0
        if not self.dry:
            self.dsems.append(self.nc.alloc_semaphore("d%d" % k))
        return k

    def _dma_wait(self, e, d):
        k = d[1]
        v = self.dsem_cnt[k]
        if self.seen_dma.get((e, k), 0) >= v:
            return
        self.seen_dma[(e, k)] = v
        if not self.dry:
            self.eng[e].wait_ge(self.dsems[k], v)
            self.n_wait += 1

    def dma(self, e, k, fn, reads=(), writes=()):
        self._waits(e, self._deps(reads, writes))
        self.dsem_cnt[k] += 16
        me = ("dma", k, self.dsem_cnt[k])
        if not self.dry:
            fn().then_inc(self.dsems[k], 16)
        self._post(me, reads, writes)
        return me

    def wait_all(self, e, bufs):
        self._waits(e, self._deps((), bufs))


class Prog:
    def __init__(self, body):
        self.body = body
        self.sy = Sync()
        self.outs = []
        self._build(True)
        self._build(False)

    def _build(self, dry):
        self.dry = dry
        self.nc = None if dry else bass.Bass("TRN2", target_bir_lowering=False)
        self.sy.begin(self.nc, dry)
        self.outs = []
        self.body(self)
        self.sy.wait_all("sp", self.outs)
        self.sy.wait_all("pool", self.outs)

    def dram(self, name, shape, dt, kind="Internal"):
        b = Buf(name, None if self.dry else self.nc.dram_tensor(name, list(shape), dt, kind=kind).ap())
        if kind == "ExternalOutput":
            self.outs.append(b)
        return b

    def inp(self, name, shape, dt=F32):
        return self.dram(name, shape, dt, "ExternalInput")

    def out(self, name, shape, dt=F32):
        return self.dram(name, shape, dt, "ExternalOutput")

    def sb(self, name, shape, dt):
        return Buf(name, None if self.dry else self.nc.alloc_sbuf_tensor(name, list(shape), dt).ap())

    def ps(self, name, shape, dt=F32):
        return Buf(name, None if self.dry else self.nc.alloc_psum_tensor(name, list(shape), dt).ap(), psum=True)

    def sbs(self, name, n, shape, dt):
        return [self.sb("%s%d" % (name, i), shape, dt) for i in range(n)]

    def pss(self, name, n, shape, dt=F32):
        return [self.ps("%s%d" % (name, i), shape, dt) for i in range(n)]

    def run(self, in_maps):
        res = run_bass_kernel_spmd(self.nc, in_maps, core_ids=list(range(NCORES)))
        return res.results


def cap(buf, off, dims):
    return bass.AP(buf.t.tensor, off, [list(d) for d in dims])


def body_mod(p):
    nc, sy = p.nc, p.sy
    v = p.inp("v", [128, 8])
    aw = p.inp("aw", [D, 6 * D])
    ab = p.inp("ab", [1, 6 * D])
    m = p.out("m", [1, 6 * D])
    vs = p.sb("vs", [128, 8], F32)
    sv = p.sb("sv", [128, 8], F32)
    abs_ = p.sb("abs", [1, 6 * D], F32)
    ms = p.sb("ms", [1, 6 * D], F32)
    wts = p.sbs("wt", 2, [128, 8, 512], F32)
    pm = p.pss("pm", 2, [1, 512])
    d0 = sy.new_dma_sem()
    dw = [sy.new_dma_sem() for _ in range(2)]
    sy.dma("sp", d0, lambda: nc.sync.dma_start(out=vs[:, :], in_=v[:, :]), reads=[v], writes=[vs])
    sy.dma("sp", d0, lambda: nc.sync.dma_start(out=abs_[:, :], in_=ab[:, :]), reads=[ab], writes=[abs_])
    sy.op("act", lambda: nc.scalar.activation(out=sv[:, :], in_=vs[:, :], func=AF.Silu), reads=[vs], writes=[sv])
    for j in range(12):
        wt = wts[j % 2]
        sy.dma("sp", dw[j % 2], lambda j=j, wt=wt: nc.sync.dma_start(
            out=wt[:, :, :], in_=aw.t[:, j * 512:(j + 1) * 512].rearrange("(k p) n -> p k n", p=128)),
            reads=[aw], writes=[wt])
        pj = pm[j % 2]
        for k in range(8):
            sy.op("pe", lambda k=k, wt=wt, pj=pj: nc.tensor.matmul(
                pj[:, :], lhsT=sv[:, k:k + 1], rhs=wt[:, k, :], start=(k == 0), stop=(k == 7)),
                reads=[sv, wt], writes=[pj])
        sl = slice(j * 512, (j + 1) * 512)
        sy.op("dve", lambda pj=pj, sl=sl: nc.vector.tensor_tensor(
            out=ms[:, sl], in0=pj[:, :], in1=abs_[:, sl], op=ALU.add), reads=[pj, abs_], writes=[ms])
        if j in (2, 3, 8, 9):
            sy.op("dve", lambda sl=sl: nc.vector.tensor_scalar_add(out=ms[:, sl], in0=ms[:, sl], scalar1=1.0),
                  reads=[ms], writes=[ms])
    sy.dma("sp", d0, lambda: nc.sync.dma_start(out=m[:, :], in_=ms[:, :]), reads=[ms], writes=[m])


def make_body_proj(even):
    G = 4 if even else 8
    NQ = G * 128
    NW = 2304 if even else 1280
    HR = 2 * G + 2
    nblk = (NW + 511) // 512

    def body(p):
        nc, sy = p.nc, p.sy
        x = p.inp("x", [NTOK, D])
        w = p.inp("w", [D, NW])
        modv = p.inp("modv", [128, 8, 4])
        cosd = p.inp("cos", [128, TL, 64])
        sind = p.inp("sin", [128, TL, 64])
        identd = p.inp("ident", [128, 128])
        qT = p.out("qT", [128, TT, G, 128], BF16)
        kT = p.out("kT", [128, NTOK], BF16)
        vo = p.out("v", [NTOK, 128], BF16)
        if even:
            gains = p.inp("gains", [128, 2, 64])
            qbT = p.out("qbT", [128, 4, NTOK], BF16)
            kbT = p.out("kbT", [128, 4, NTOK], BF16)
            vbo = p.out("vb", [NTOK, 512], BF16)

        wb = p.sb("wb", [128, 8, NW], BF16)
        mods = p.sb("mods", [128, 8, 4], F32)
        coss = p.sb("coss", [128, TL, 64], F32)
        sins = p.sb("sins", [128, TL, 64], F32)
        ident = p.sb("ident_s", [128, 128], F32)
        identb = p.sb("identb", [128, 128], BF16)
        qT_s = p.sb("qT_s", [128, TT, G, 128], BF16)
        kT_s = p.sb("kT_s", [128, NTOK], BF16)
        if even:
            gs = p.sb("gs", [128, 2, 64], F32)
            qbT_s = p.sb("qbT_s", [128, 4, NTOK], BF16)
            kbT_s = p.sb("kbT_s", [128, 4, NTOK], BF16)
        xs = p.sbs("xs", 2, [128, D], F32)
        hT = p.sbs("hT", 2, [128, 8, 128], BF16)
        qkv = p.sbs("qkv", 2, [128, NW], F32)
        qkvb = p.sbs("qkvb", 3, [128, NW], BF16)
        t1 = p.sb("t1", [128, HR * 64], F32)
        t2 = p.sb("t2", [128, HR * 64], F32)
        ss = p.sb("ss", [128, 16], F32)
        pTs = p.pss("pT", 2, [128, 4, 128], F32)
        pq = p.pss("pq", 2, [128, 512], F32)
        ptr = p.pss("ptr", 2, [128, 8, 128], BF16)

        d0 = sy.new_dma_sem()
        dx = [sy.new_dma_sem() for _ in range(2)]
        dv = [sy.new_dma_sem() for _ in range(3)]
        dout = sy.new_dma_sem()
        sy.dma("sp", d0, lambda: nc.sync.dma_start(out=mods[:, :, :], in_=modv[:, :, :]), reads=[modv], writes=[mods])
        sy.dma("sp", d0, lambda: nc.sync.dma_start(out=coss[:, :, :], in_=cosd[:, :, :]), reads=[cosd], writes=[coss])
        sy.dma("sp", d0, lambda: nc.sync.dma_start(out=sins[:, :, :], in_=sind[:, :, :]), reads=[sind], writes=[sins])
        sy.dma("sp", d0, lambda: nc.sync.dma_start(out=ident[:, :], in_=identd[:, :]), reads=[identd], writes=[ident])
        if even:
            sy.dma("sp", d0, lambda: nc.sync.dma_start(out=gs[:, :, :], in_=gains[:, :, :]), reads=[gains], writes=[gs])
        dwt = sy.new_dma_sem()
        sy.dma("pool", dwt, lambda: nc.gpsimd.dma_start(
            out=wb[:, :, :], in_=w.t.rearrange("(k p) n -> p k n", p=128)), reads=[w], writes=[wb])
        sy.op("dve", lambda: nc.vector.tensor_copy(out=identb[:, :], in_=ident[:, :]), reads=[ident], writes=[identb])

        def load_x(t):
            b = xs[t % 2]
            sy.dma("sp", dx[t % 2], lambda: nc.sync.dma_start(out=b[:, :], in_=x[t * 128:(t + 1) * 128, :]),
                   reads=[x], writes=[b])

        import os
        LV = int(os.environ.get("PLV", "9"))
        NT_ = int(os.environ.get("PNT", str(TT)))
        load_x(0)
        for t in range(NT_):
            is_ctx = t >= TL
            mo = 2 if is_ctx else 0
            if t + 1 < NT_:
                load_x(t + 1)
            xb, hb, qb_, bb = xs[t % 2], hT[t % 2], qkv[t % 2], qkvb[t % 3]
            for c in range(8):
                sy.op("pe", lambda c=c, xb=xb: nc.tensor.transpose(pTs[c // 4][:, c % 4, :], xb[:, c * 128:(c + 1) * 128], ident[:, :]),
                      reads=[xb, ident], writes=[pTs[c // 4]])
            for c in range(8):
                pTc = pTs[c // 4]
                if c < 4:
                    sy.op("dve", lambda c=c, hb=hb, pTc=pTc: nc.vector.tensor_scalar(
                        out=hb[:, c, :], in0=pTc[:, c % 4, :], scalar1=mods[:, c, mo:mo + 1], scalar2=mods[:, c, mo + 1:mo + 2],
                        op0=ALU.mult, op1=ALU.add), reads=[pTc, mods], writes=[hb])
                else:
                    sy.op("act", lambda c=c, hb=hb, pTc=pTc: nc.scalar.activation(
                        out=hb[:, c, :], in_=pTc[:, c % 4, :], func=AF.Identity, bias=mods[:, c, mo + 1:mo + 2],
                        scale=mods[:, c, mo:mo + 1]), reads=[pTc, mods], writes=[hb])
            for j in range(nblk):
                n0, n1 = j * 512, min(NW, (j + 1) * 512)
                pj = pq[j % 2]
                for c in range(8):
                    sy.op("pe", lambda c=c, hb=hb, pj=pj, n0=n0, n1=n1: nc.tensor.matmul(
                        pj[:, 0:n1 - n0], lhsT=hb[:, c, :], rhs=wb[:, c, n0:n1], start=(c == 0), stop=(c == 7)),
                        reads=[hb, wb], writes=[pj])
                if j % 2 == 0:
                    sy.op("act", lambda pj=pj, n0=n0, n1=n1, qb_=qb_: nc.scalar.copy(out=qb_[:, n0:n1], in_=pj[:, 0:n1 - n0]),
                          reads=[pj], writes=[qb_])
                else:
                    sy.op("dve", lambda pj=pj, n0=n0, n1=n1, qb_=qb_: nc.vector.tensor_copy(out=qb_[:, n0:n1], in_=pj[:, 0:n1 - n0]),
                          reads=[pj], writes=[qb_])
            if LV < 2:
                continue
            if even and LV >= 3:
                sy.op("dve", lambda qb_=qb_: nc.vector.tensor_tensor(out=t1[:, 0:640], in0=qb_[:, 0:640], in1=qb_[:, 0:640], op=ALU.mult),
                      reads=[qb_], writes=[t1])
                sy.op("dve", lambda: nc.vector.tensor_reduce(out=ss[:, 0:10], in_=cap(t1, 0, [[HR * 64, 128], [64, 10], [1, 64]]),
                                                             axis=AX.X, op=ALU.add), reads=[t1], writes=[ss])
                sy.op("dve", lambda: nc.vector.tensor_scalar_add(out=ss[:, 0:10], in0=ss[:, 0:10], scalar1=64.0 * RMS_EPS), reads=[ss], writes=[ss])
                sy.op("act", lambda: nc.scalar.sqrt(out=ss[:, 0:10], in_=ss[:, 0:10]), reads=[ss], writes=[ss])
                sy.op("dve", lambda: nc.vector.reciprocal(out=ss[:, 0:10], in_=ss[:, 0:10]), reads=[ss], writes=[ss])
                sy.op("dve", lambda: nc.vector.tensor_scalar_mul(out=ss[:, 8:10], in0=ss[:, 8:10], scalar1=8.0), reads=[ss], writes=[ss])
                sy.op("dve", lambda qb_=qb_: nc.vector.tensor_tensor(
                    out=cap(qb_, 0, [[NW, 128], [64, 10], [1, 64]]), in0=cap(qb_, 0, [[NW, 128], [64, 10], [1, 64]]),
                    in1=cap(ss, 0, [[16, 128], [1, 10], [0, 64]]), op=ALU.mult), reads=[qb_, ss], writes=[qb_])
                sy.op("dve", lambda qb_=qb_: nc.vector.tensor_tensor(
                    out=cap(qb_, 0, [[NW, 128], [64, 8], [1, 64]]), in0=cap(qb_, 0, [[NW, 128], [64, 8], [1, 64]]),
                    in1=cap(gs, 0, [[128, 128], [0, 8], [1, 64]]), op=ALU.mult), reads=[qb_, gs], writes=[qb_])
                sy.op("dve", lambda qb_=qb_: nc.vector.tensor_tensor(
                    out=cap(qb_, 512, [[NW, 128], [64, 2], [1, 64]]), in0=cap(qb_, 512, [[NW, 128], [64, 2], [1, 64]]),
                    in1=cap(gs, 64, [[128, 128], [0, 2], [1, 64]]), op=ALU.mult), reads=[qb_, gs], writes=[qb_])
            if not is_ctx and LV >= 4:
                W_ = HR * 64
                sy.op("dve", lambda qb_=qb_, t=t: nc.vector.tensor_tensor(
                    out=cap(t1, 0, [[W_, 128], [64, HR], [1, 64]]), in0=cap(qb_, 0, [[NW, 128], [64, HR], [1, 64]]),
                    in1=cap(coss, t * 64, [[TL * 64, 128], [0, HR], [1, 64]]), op=ALU.mult), reads=[qb_, coss], writes=[t1])
                for hf in range(2):
                    sy.op("dve", lambda qb_=qb_, t=t, hf=hf: nc.vector.tensor_tensor(
                        out=cap(t2, hf * 16, [[W_, 128], [64, HR], [32, 2], [1, 16]]),
                        in0=cap(qb_, (1 - hf) * 16, [[NW, 128], [64, HR], [32, 2], [1, 16]]),
                        in1=cap(sins, t * 64 + hf * 16, [[TL * 64, 128], [0, HR], [32, 2], [1, 16]]), op=ALU.mult),
                        reads=[qb_, sins], writes=[t2])
                sy.op("dve", lambda qb_=qb_: nc.vector.tensor_tensor(out=qb_[:, 0:W_], in0=t1[:, 0:W_], in1=t2[:, 0:W_], op=ALU.add),
                      reads=[t1, t2], writes=[qb_])
            sy.op("act", lambda qb_=qb_, bb=bb: nc.scalar.activation(
                out=cap(bb, 0, [[NW, 128], [64, 2], [128, G], [1, 64]]),
                in_=cap(qb_, 0, [[NW, 128], [G * 64, 2], [64, G], [1, 64]]), func=AF.Copy, scale=(1.0 if even else 0.125)),
                reads=[qb_], writes=[bb])
            if even:
                sy.op("act", lambda qb_=qb_, bb=bb: nc.scalar.copy(out=bb[:, NQ:768], in_=qb_[:, NQ:768]), reads=[qb_], writes=[bb])
                sy.op("act", lambda qb_=qb_, bb=bb: nc.scalar.activation(out=bb[:, 768:1280], in_=qb_[:, 768:1280], func=AF.Copy, scale=0.125),
                      reads=[qb_], writes=[bb])
                sy.op("act", lambda qb_=qb_, bb=bb: nc.scalar.copy(out=bb[:, 1280:NW], in_=qb_[:, 1280:NW]), reads=[qb_], writes=[bb])
            else:
                sy.op("act", lambda qb_=qb_, bb=bb: nc.scalar.copy(out=bb[:, NQ:NW], in_=qb_[:, NQ:NW]), reads=[qb_], writes=[bb])
            dvk = dv[t % 3]
            sy.dma("sp", dvk, lambda bb=bb, t=t: nc.sync.dma_start(out=vo[t * 128:(t + 1) * 128, :], in_=bb[:, NQ + 128:NQ + 256]),
                   reads=[bb], writes=[vo])
            if even:
                sy.dma("sp", dvk, lambda bb=bb, t=t: nc.sync.dma_start(out=vbo[t * 128:(t + 1) * 128, :], in_=bb[:, 1792:2304]),
                       reads=[bb], writes=[vbo])
            if LV < 5:
                continue
            pa = ptr[t % 2]
            for g in range(G):
                sy.op("pe", lambda g=g, bb=bb, pa=pa: nc.tensor.transpose(
                    pa[:, g, :], bb[:, g * 128:(g + 1) * 128], identb[:, :]),
                    reads=[bb, identb], writes=[pa])
            sy.op("dve", lambda pa=pa, t=t: nc.vector.tensor_copy(out=qT_s[:, t, :, :], in_=pa[:, 0:G, :]),
                  reads=[pa], writes=[qT_s])
            if LV == 5:
                continue
            if even:
                pb = ptr[(t + 1) % 2]
                sy.op("pe", lambda bb=bb, pb=pb: nc.tensor.transpose(pb[:, 0, :], bb[:, NQ:NQ + 128], identb[:, :]),
                      reads=[bb, identb], writes=[pb])
                for c in range(4):
                    sy.op("pe", lambda c=c, bb=bb, pb=pb: nc.tensor.transpose(pb[:, 1 + c, :], bb[:, 768 + c * 128:768 + (c + 1) * 128], identb[:, :]),
                          reads=[bb, identb], writes=[pb])
                if LV != 7:
                    sy.op("act", lambda pb=pb, t=t: nc.scalar.copy(out=kT_s[:, t * 128:(t + 1) * 128], in_=pb[:, 0, :]),
                          reads=[pb], writes=[kT_s])
                if LV != 8:
                    sy.op("dve", lambda pb=pb, t=t: nc.vector.tensor_copy(out=qbT_s[:, :, t * 128:(t + 1) * 128], in_=pb[:, 1:5, :]),
                          reads=[pb], writes=[qbT_s])
                if LV in (7, 8):
                    continue
                if LV == 6:
                    continue
                for c in range(4):
                    sy.op("pe", lambda c=c, bb=bb, pa=pa: nc.tensor.transpose(pa[:, 4 + c, :], bb[:, 1280 + c * 128:1280 + (c + 1) * 128], identb[:, :]),
                          reads=[bb, identb], writes=[pa])
                sy.op("act", lambda pa=pa, t=t: nc.scalar.copy(out=kbT_s[:, :, t * 128:(t + 1) * 128], in_=pa[:, 4:8, :]),
                      reads=[pa], writes=[kbT_s])
            else:
                pb = ptr[(t + 1) % 2]
                sy.op("pe", lambda bb=bb, pb=pb: nc.tensor.transpose(pb[:, 0, :], bb[:, NQ:NQ + 128], identb[:, :]),
                      reads=[bb, identb], writes=[pb])
                sy.op("act", lambda pb=pb, t=t: nc.scalar.copy(out=kT_s[:, t * 128:(t + 1) * 128], in_=pb[:, 0, :]),
                      reads=[pb], writes=[kT_s])
        sy.dma("sp", dout, lambda: nc.sync.dma_start(out=qT[:, :, :, :], in_=qT_s[:, :, :, :]), reads=[qT_s], writes=[qT])
        sy.dma("sp", dout, lambda: nc.sync.dma_start(out=kT[:, :], in_=kT_s[:, :]), reads=[kT_s], writes=[kT])
        if even:
            sy.dma("sp", dout, lambda: nc.sync.dma_start(out=qbT[:, :, :], in_=qbT_s[:, :, :]), reads=[qbT_s], writes=[qbT])
            sy.dma("sp", dout, lambda: nc.sync.dma_start(out=kbT[:, :, :], in_=kbT_s[:, :, :]), reads=[kbT_s], writes=[kbT])

    return body


_PROGS = {}


def get_prog(name, body):
    if name not in _PROGS:
        _PROGS[name] = Prog(body)
    return _PROGS[name]


def make_body_attn(kind):
    G = {"A": 4, "C": 8, "B": 4}[kind]
    H = {"A": 8, "C": 16, "B": 8}[kind]
    if kind == "A":
        NKT = 128 + TC
    elif kind == "C":
        NKT = 1 + TL + 1 + TC
    else:
        NKT = 2 + TL + 2 + TC
    KC = 1 if kind != "B" else 4
    VH = 2 if kind != "B" else 8

    def body(p):
        nc, sy = p.nc, p.sy
        qT = p.inp("qT", [128, TT, G, 128], BF16)
        kT = p.inp("kT", [128, KC, NKT * 128], BF16)
        va = p.inp("va", [128, NKT, VH, 128], BF16)
        identd = p.inp("ident", [128, 128])
        yT = p.out("yT", [64, TT, H, 128], BF16)
        q_s = p.sb("q_s", [128, TT, G, 128], BF16)
        k_s = p.sb("k_s", [128, KC, NKT * 128], BF16)
        v_s = p.sb("v_s", [128, NKT, VH, 128], BF16)
        y_s = p.sb("y_s", [64, TT, H, 128], BF16)
        ident = p.sb("ident_s", [128, 128], F32)
        identb = p.sb("identb", [128, 128], BF16)
        pts = p.sbs("pt", 4, [128, 512], BF16)
        rc = p.sbs("rc", 2, [128, 512], F32)
        rs = p.sbs("rs", 2, [64, 512], F32)
        psS = p.pss("psS", 4, [128, 512], F32)
        psO = p.pss("psO", 2, [128, 512], F32)
        d0 = sy.new_dma_sem()
        dk = sy.new_dma_sem()
        dvs = sy.new_dma_sem()
        dout = sy.new_dma_sem()
        sy.dma("sp", d0, lambda: nc.sync.dma_start(out=q_s[:, :, :, :], in_=qT[:, :, :, :]), reads=[qT], writes=[q_s])
        sy.dma("sp", d0, lambda: nc.sync.dma_start(out=ident[:, :], in_=identd[:, :]), reads=[identd], writes=[ident])
        step = 26 if kind == "A" else NKT
        for k0 in range(0, NKT, step):
            k1 = min(NKT, k0 + step)
            sy.dma("sp", dk, lambda k0=k0, k1=k1: nc.sync.dma_start(out=k_s[:, :, k0 * 128:k1 * 128], in_=kT[:, :, k0 * 128:k1 * 128]),
                   reads=[kT], writes=[k_s])
            sy.dma("sp", dvs, lambda k0=k0, k1=k1: nc.sync.dma_start(out=v_s[:, k0:k1, :, :], in_=va[:, k0:k1, :, :]),
                   reads=[va], writes=[v_s])
        sy.op("dve", lambda: nc.vector.tensor_copy(out=identb[:, :], in_=ident[:, :]), reads=[ident], writes=[identb])
        if kind == "C":
            maskd = p.inp("masks", [128, 4, 512])
            sinkd = p.inp("sink", [1, 16])
            mask_s = p.sb("mask_s", [128, 4, 512], BF16)
            sk = p.sb("sk", [128, 16], F32)
            ske = p.sb("ske", [128, 16], F32)
            dm = sy.new_dma_sem()
            sy.dma("pool", dm, lambda: nc.gpsimd.dma_start(out=mask_s[:, :, :], in_=maskd[:, :, :]), reads=[maskd], writes=[mask_s])
            sy.dma("sp", d0, lambda: nc.sync.dma_start(out=sk[:, :], in_=sinkd.t.partition_broadcast(128)), reads=[sinkd], writes=[sk])
            sy.op("act", lambda: nc.scalar.activation(out=ske[:, :], in_=sk[:, :], func=AF.Exp), reads=[sk], writes=[ske])
        if kind == "B":
            nbd = p.inp("nbias", [128, 5, 8, 768])
            nb_i = p.sb("nb_i", [128, 8, 768], BF16)
            nb_e = p.sb("nb_e", [128, 8, 768], BF16)
            dm = sy.new_dma_sem()
            dme = sy.new_dma_sem()
            sy.dma("pool", dm, lambda: nc.gpsimd.dma_start(out=nb_i[:, :, :], in_=nbd[:, 2, :, :]), reads=[nbd], writes=[nb_i])

        jobs = []
        for t in range(TT):
            is_ctx = t >= TL
            if kind == "A":
                kts = [(kt, None) for kt in (range(128, 130) if is_ctx else range(130))]
                for kv in range(2):
                    jobs.append(dict(t=t, pb=kv * 64, kc=0, q=(lambda t=t: q_s[:, t, :, :]), qoff=0, N=512, kts=kts, vh=kv,
                                     h0=kv * 4, nh=4, sink=None))
            elif kind == "C":
                if is_ctx:
                    kl = [(TL + 2, None), (TL + 3, None)]
                else:
                    mp = 2 if t == 0 else 0
                    mn = 3 if t == TL - 1 else 1
                    kl = [(t, mp), (t + 1, None), (t + 2, mn), (TL + 2, None), (TL + 3, None)]
                for kv in range(2):
                    for hf in range(2):
                        jobs.append(dict(t=t, pb=kv * 64, kc=0, qoff=hf * 4, N=512, kts=kl, vh=kv,
                                         h0=kv * 8 + hf * 4, nh=4, sink=kv * 8 + hf * 4))
            else:
                if is_ctx:
                    kl = [(TL + 4, None), (TL + 5, None)]
                else:
                    cls = 0 if t == 0 else 1 if t == 1 else 3 if t == TL - 2 else 4 if t == TL - 1 else 2
                    lo = t - 1 if t == TL - 1 else t
                    nk = 6 if t in (0, TL - 1) else 5
                    kl = [(lo + i, (cls, i)) for i in range(nk)] + [(TL + 4, None), (TL + 5, None)]
                for h in range(8):
                    jobs.append(dict(t=t, pb=(h % 2) * 64, kc=h // 2, qoff=h // 2, N=128, kts=kl, vh=h,
                                     h0=h, nh=1, sink=None))

        si = 0
        cur_cls = None
        for ji, jb in enumerate(jobs):
            t, pb_, N = jb["t"], jb["pb"], jb["N"]
            if kind == "B" and t < TL:
                cls_t = jb["kts"][0][1][0]
                if cls_t != 2 and cls_t != cur_cls:
                    cur_cls = cls_t
                    sy.dma("pool", dme, lambda c5=cls_t: nc.gpsimd.dma_start(out=nb_e[:, :, :], in_=nbd[:, c5, :, :]),
                           reads=[nbd], writes=[nb_e])
            po = psO[ji % 2]
            nkt = len(jb["kts"])
            qoff = jb["qoff"]
            for ki, (kt, bias) in enumerate(jb["kts"]):
                pS = psS[si % 4]
                pt = pts[si % 4]
                si += 1
                kc = jb["kc"]
                sy.op("pe", lambda pS=pS, kt=kt, kc=kc, t=t, qoff=qoff, bias=bias: nc.tensor.matmul(
                    pS[:, 0:N], lhsT=k_s[pb_:pb_ + 64, kc, kt * 128:(kt + 1) * 128],
                    rhs=cap(q_s, pb_ * TT * G * 128 + (t * G + qoff) * 128, [[TT * G * 128, 64], [1, N]]),
                    start=True, stop=(bias is None or kind == "C")), reads=[k_s, q_s], writes=[pS])
                if bias is not None:
                    if kind == "C":
                        pass
                    else:
                        cls, wi = bias
                        h = jb["vh"]
                        nbb = nb_i if cls == 2 else nb_e
                        sy.op("pe", lambda pS=pS, nbb=nbb, wi=wi, h=h: nc.tensor.matmul(
                            pS[:, 0:N], lhsT=identb[:, :], rhs=nbb[:, h, wi * 128:(wi + 1) * 128], start=False, stop=True),
                            reads=[identb, nbb], writes=[pS])
                sy.op("act", lambda pS=pS, pt=pt: nc.scalar.activation(out=pt[:, 0:N], in_=pS[:, 0:N], func=AF.Exp),
                      reads=[pS], writes=[pt])
                if kind == "C" and bias is not None:
                    sy.op("dve", lambda pt=pt, bias=bias: nc.vector.tensor_tensor(
                        out=pt[:, 0:N], in0=pt[:, 0:N], in1=mask_s[:, bias, :], op=ALU.mult), reads=[pt, mask_s], writes=[pt])
                last = (ki == nkt - 1)
                vh = jb["vh"]
                sy.op("pe", lambda po=po, pt=pt, kt=kt, vh=vh, ki=ki, last=last: nc.tensor.matmul(
                    po[:, 0:N], lhsT=v_s[:, kt, vh, :], rhs=pt[:, 0:N], start=(ki == 0), stop=last),
                    reads=[v_s, pt], writes=[po])
            r1, r2 = rc[ji % 2], rs[ji % 2]
            if jb["sink"] is not None:
                s0 = jb["sink"]
                sy.op("dve", lambda po=po, r1=r1, s0=s0: nc.vector.tensor_tensor(
                    out=cap(r1, 64 * 512, [[512, 64], [128, 4], [1, 128]]), in0=cap(po, 64 * 512, [[512, 64], [128, 4], [1, 128]]),
                    in1=cap(ske, 64 * 16 + s0, [[16, 64], [1, 4], [0, 128]]), op=ALU.add), reads=[po, ske], writes=[r1])
                sy.op("dve", lambda r1=r1: nc.vector.reciprocal(out=r1[64:128, 0:N], in_=r1[64:128, 0:N]),
                      reads=[r1], writes=[r1])
            else:
                sy.op("dve", lambda po=po, r1=r1: nc.vector.reciprocal(out=r1[64:128, 0:N], in_=po[64:128, 0:N]),
                      reads=[po], writes=[r1])
            sy.op("dve", lambda r1=r1, r2=r2: nc.vector.tensor_copy(out=r2[0:64, 0:N], in_=r1[64:128, 0:N]),
                  reads=[r1], writes=[r2])
            h0 = jb["h0"]
            sy.op("dve", lambda po=po, r2=r2, t=t, h0=h0: nc.vector.tensor_tensor(
                out=cap(y_s, (t * H + h0) * 128, [[TT * H * 128, 64], [1, N]]), in0=po[0:64, 0:N], in1=r2[0:64, 0:N], op=ALU.mult),
                reads=[po, r2], writes=[y_s])
        sy.dma("sp", dout, lambda: nc.sync.dma_start(out=yT[:, :, :, :], in_=y_s[:, :, :, :]), reads=[y_s], writes=[yT])

    return body


def ln_tile(p, nc, sy, z, tmp, st, gvec, bvec, out_t):
    sy.op("dve", lambda: nc.vector.memset(st[:, 0:2], 0.0), writes=[st])
    sy.op("act", lambda: nc.scalar.activation(out=tmp[:, :], in_=z[:, :], func=AF.Identity, accum_out=st[:, 0:1]),
          reads=[z, st], writes=[tmp, st])
    sy.op("act", lambda: nc.scalar.activation(out=tmp[:, :], in_=z[:, :], func=AF.Square, accum_out=st[:, 1:2]),
          reads=[z, st], writes=[tmp, st])
    sy.op("dve", lambda: nc.vector.tensor_scalar_mul(out=st[:, 2:3], in0=st[:, 0:1], scalar1=1.0 / D), reads=[st], writes=[st])
    sy.op("dve", lambda: nc.vector.tensor_tensor(out=st[:, 3:4], in0=st[:, 2:3], in1=st[:, 2:3], op=ALU.mult), reads=[st], writes=[st])
    sy.op("dve", lambda: nc.vector.scalar_tensor_tensor(out=st[:, 4:5], in0=st[:, 1:2], scalar=1.0 / D, in1=st[:, 3:4],
                                                        op0=ALU.mult, op1=ALU.subtract), reads=[st], writes=[st])
    sy.op("dve", lambda: nc.vector.tensor_scalar_add(out=st[:, 4:5], in0=st[:, 4:5], scalar1=LN_EPS), reads=[st], writes=[st])
    sy.op("act", lambda: nc.scalar.sqrt(out=st[:, 5:6], in_=st[:, 4:5]), reads=[st], writes=[st])
    sy.op("dve", lambda: nc.vector.reciprocal(out=st[:, 6:7], in_=st[:, 5:6]), reads=[st], writes=[st])
    sy.op("dve", lambda: nc.vector.tensor_scalar(out=tmp[:, :], in0=z[:, :], scalar1=st[:, 2:3], scalar2=st[:, 6:7],
                                                 op0=ALU.subtract, op1=ALU.mult), reads=[z, st], writes=[tmp])
    sy.op("pool", lambda: nc.gpsimd.tensor_tensor(out=tmp[:, :], in0=tmp[:, :], in1=gvec, op=ALU.mult), reads=[tmp], writes=[tmp])
    sy.op("dve", lambda: nc.vector.tensor_tensor(out=out_t[:, :], in0=tmp[:, :], in1=bvec, op=ALU.add), reads=[tmp], writes=[out_t])


def body_oproj(p):
    nc, sy = p.nc, p.sy
    yT = p.inp("yT", [64, TT, 16, 128], BF16)
    wo = p.inp("wo", [D, D])
    x = p.inp("x", [NTOK, D])
    vecs = p.inp("vecs", [128, 4, D])
    xo = p.out("xo", [NTOK, D])
    y_s = p.sb("y_s", [64, TT, 16, 128], BF16)
    wo_s = p.sb("wo_s", [64, 16, D], BF16)
    vs = p.sb("vs", [128, 4, D], F32)
    xs = p.sbs("xs", 2, [128, D], F32)
    zs = p.sbs("z", 2, [128, D], F32)
    tmps = p.sbs("tmp", 2, [128, D], F32)
    outs = p.sbs("ot", 2, [128, D], F32)
    sts = p.sbs("st", 2, [128, 8], F32)
    po = p.pss("po", 4, [128, 512], F32)
    d0 = sy.new_dma_sem()
    dw = sy.new_dma_sem()
    dx = [sy.new_dma_sem() for _ in range(2)]
    do = [sy.new_dma_sem() for _ in range(2)]
    sy.dma("sp", d0, lambda: nc.sync.dma_start(out=y_s[:, :, :, :], in_=yT[:, :, :, :]), reads=[yT], writes=[y_s])
    sy.dma("sp", d0, lambda: nc.sync.dma_start(out=vs[:, :, :], in_=vecs[:, :, :]), reads=[vecs], writes=[vs])
    sy.dma("pool", dw, lambda: nc.gpsimd.dma_start(out=wo_s[:, :, :], in_=wo.t.rearrange("(h d) n -> d h n", d=64)),
           reads=[wo], writes=[wo_s])

    def load_x(t):
        b = xs[t % 2]
        sy.dma("sp", dx[t % 2], lambda: nc.sync.dma_start(out=b[:, :], in_=x[t * 128:(t + 1) * 128, :]), reads=[x], writes=[b])

    load_x(0)
    for t in range(TT):
        if t + 1 < TT:
            load_x(t + 1)
        xb, z, tmp, ot, st = xs[t % 2], zs[t % 2], tmps[t % 2], outs[t % 2], sts[t % 2]
        gi = 1 if t >= TL else 0
        for nb in range(2):
            pj = po[(2 * t + nb) % 4]
            for h in range(16):
                sy.op("pe", lambda h=h, pj=pj, nb=nb, t=t: nc.tensor.matmul(
                    pj[:, :], lhsT=y_s[0:64, t, h, :], rhs=wo_s[0:64, h, nb * 512:(nb + 1) * 512], start=(h == 0), stop=(h == 15)),
                    reads=[y_s, wo_s], writes=[pj])
            sl = slice(nb * 512, (nb + 1) * 512)
            sy.op("dve", lambda pj=pj, sl=sl, tmp=tmp, gi=gi: nc.vector.tensor_tensor(
                out=tmp[:, sl], in0=pj[:, :], in1=vs[:, gi, sl], op=ALU.mult), reads=[pj, vs], writes=[tmp])
        sy.op("dve", lambda xb=xb, tmp=tmp, z=z: nc.vector.scalar_tensor_tensor(
            out=z[:, :], in0=xb[:, :], scalar=DN_ALPHA, in1=tmp[:, :], op0=ALU.mult, op1=ALU.add), reads=[xb, tmp], writes=[z])
        ln_tile(p, nc, sy, z, tmp, st, None if p.dry else vs[:, 2, :], None if p.dry else vs[:, 3, :], ot)
        sy.dma("sp", do[t % 2], lambda ot=ot, t=t: nc.sync.dma_start(out=xo[t * 128:(t + 1) * 128, :], in_=ot[:, :]),
               reads=[ot], writes=[xo])


NFF = D_FF // 128


def body_ffn(p):
    nc, sy = p.nc, p.sy
    x = p.inp("x", [NTOK, D])
    xhT = p.inp("xhT", [128, 8, 2])
    hmask = p.inp("hmask", [128, 2])
    wu = p.inp("wu", [D, 2 * D_FF])
    wd = p.inp("wd", [D_FF, D])
    cw = p.inp("cw", [128, 2 * NFF, 4])
    modv = p.inp("modv", [128, 8, 4])
    vecs = p.inp("vecs", [128, 4, D])
    identd = p.inp("ident", [128, 128])
    xo = p.out("xo", [NTOK, D])

    GT = 512
    hT_l = p.sb("hT_l", [128, 8, LTOK + 2], BF16)
    hT_c = p.sb("hT_c", [128, 8, CTX + 2], BF16)
    actT = p.sb("actT", [128, NFF, GT], BF16)
    wd_s = p.sb("wd_s", [128, NFF, D], BF16)
    wus = p.sbs("wu_s", 2, [128, 8, 256], BF16)
    ua = p.sb("ua", [128, GT + 2], F32)
    ug = p.sb("ug", [128, GT + 2], F32)
    ta = p.sb("ta", [128, GT], F32)
    tg = p.sb("tg", [128, GT], F32)
    sg = p.sb("sg", [128, GT], F32)
    tq = p.sb("tq", [128, GT], F32)
    cw_s = p.sb("cw_s", [128, 2 * NFF, 4], F32)
    mods = p.sb("mods", [128, 8, 4], F32)
    vs = p.sb("vs", [128, 4, D], F32)
    ident = p.sb("ident_s", [128, 128], F32)
    xh_s = p.sb("xh_s", [128, 8, 2], F32)
    hm_s = p.sb("hm_s", [128, 2], F32)
    xs = p.sbs("xs", 2, [128, D], F32)
    zs = p.sbs("z", 2, [128, D], F32)
    tmps = p.sbs("tmp", 2, [128, D], F32)
    outs = p.sbs("ot", 2, [128, D], F32)
    sts = p.sbs("st", 2, [128, 8], F32)
    pu = p.pss("pu", 4, [128, 512], F32)
    pus = p.pss("pus", 2, [128, 512], F32)
    pd = p.pss("pd", 2, [128, 512], F32)
    d0 = sy.new_dma_sem()
    dwd = sy.new_dma_sem()
    dwu = [sy.new_dma_sem() for _ in range(2)]
    dx = [sy.new_dma_sem() for _ in range(2)]
    do = [sy.new_dma_sem() for _ in range(2)]
    for dst, src in ((cw_s, cw), (mods, modv), (vs, vecs), (xh_s, xhT)):
        sy.dma("sp", d0, lambda dst=dst, src=src: nc.sync.dma_start(out=dst.t, in_=src.t), reads=[src], writes=[dst])
    sy.dma("sp", d0, lambda: nc.sync.dma_start(out=ident[:, :], in_=identd[:, :]), reads=[identd], writes=[ident])
    sy.dma("sp", d0, lambda: nc.sync.dma_start(out=hm_s[:, :], in_=hmask[:, :]), reads=[hmask], writes=[hm_s])
    sy.dma("pool", dwd, lambda: nc.gpsimd.dma_start(out=wd_s[:, :, :], in_=wd.t.rearrange("(j p) n -> p j n", p=128)),
           reads=[wd], writes=[wd_s])

    def load_x(t, eng_slot):
        b = xs[eng_slot % 2]
        sy.dma("sp", dx[eng_slot % 2], lambda: nc.sync.dma_start(out=b[:, :], in_=x[t * 128:(t + 1) * 128, :]), reads=[x], writes=[b])

    load_x(0, 0)
    for t in range(TT):
        if t + 1 < TT:
            load_x(t + 1, t + 1)
        xb = xs[t % 2]
        is_ctx = t >= TL
        mo = 2 if is_ctx else 0
        dstb = hT_c if is_ctx else hT_l
        c0 = 1 + (t - TL if is_ctx else t) * 128
        for c in range(8):
            pT = pu[c // 4 + 2 * (t % 2)]
            sy.op("pe", lambda c=c, xb=xb, pT=pT: nc.tensor.transpose(pT[:, (c % 4) * 128:(c % 4 + 1) * 128], xb[:, c * 128:(c + 1) * 128], ident[:, :]),
                  reads=[xb, ident], writes=[pT])
        for c in range(8):
            pT = pu[c // 4 + 2 * (t % 2)]
            if c < 4:
                sy.op("dve", lambda c=c, pT=pT, dstb=dstb, c0=c0, mo=mo: nc.vector.tensor_scalar(
                    out=dstb[:, c, c0:c0 + 128], in0=pT[:, (c % 4) * 128:(c % 4 + 1) * 128], scalar1=mods[:, c, mo:mo + 1],
                    scalar2=mods[:, c, mo + 1:mo + 2], op0=ALU.mult, op1=ALU.add), reads=[pT, mods], writes=[dstb])
            else:
                sy.op("act", lambda c=c, pT=pT, dstb=dstb, c0=c0, mo=mo: nc.scalar.activation(
                    out=dstb[:, c, c0:c0 + 128], in_=pT[:, (c % 4) * 128:(c % 4 + 1) * 128], func=AF.Identity,
                    bias=mods[:, c, mo + 1:mo + 2], scale=mods[:, c, mo:mo + 1]), reads=[pT, mods], writes=[dstb])
    for j, col in ((0, 0), (1, LTOK + 1)):
        sy.op("dve", lambda j=j: nc.vector.tensor_tensor(out=xh_s[:, :, j], in0=xh_s[:, :, j], in1=mods[:, :, 0], op=ALU.mult),
              reads=[xh_s, mods], writes=[xh_s])
        sy.op("dve", lambda j=j: nc.vector.tensor_tensor(out=xh_s[:, :, j], in0=xh_s[:, :, j], in1=mods[:, :, 1], op=ALU.add),
              reads=[xh_s, mods], writes=[xh_s])
        sy.op("dve", lambda j=j, col=col: nc.vector.tensor_scalar(out=hT_l[:, :, col], in0=xh_s[:, :, j], scalar1=hm_s[:, j:j + 1],
                                                                scalar2=None, op0=ALU.mult), reads=[xh_s, hm_s], writes=[hT_l])
    sy.op("dve", lambda: nc.vector.memset(hT_c[:, :, 0], 0.0), reads=[hT_c], writes=[hT_c])
    sy.op("dve", lambda: nc.vector.memset(hT_c[:, :, CTX + 1], 0.0), reads=[hT_c], writes=[hT_c])

    groups = [(hT_l, g * GT, GT, g * 4) for g in range(LTOK // GT)] + [(hT_c, 0, CTX, TL)]
    wi = 0
    for (hb, c0, gt, t0) in groups:
        blocks = [(0, min(512, gt + 2))]
        if gt + 2 > 512:
            blocks.append((512, gt + 2))
        for j in range(NFF):
            ws = wus[wi % 2]
            dk = dwu[wi % 2]
            wi += 1
            sy.dma("pool", dk, lambda ws=ws, j=j: nc.gpsimd.dma_start(
                out=ws[:, :, 0:128], in_=wu.t[:, j * 128:(j + 1) * 128].rearrange("(k p) n -> p k n", p=128)), reads=[wu], writes=[ws])
            sy.dma("pool", dk, lambda ws=ws, j=j: nc.gpsimd.dma_start(
                out=ws[:, :, 128:256], in_=wu.t[:, D_FF + j * 128:D_FF + (j + 1) * 128].rearrange("(k p) n -> p k n", p=128)),
                reads=[wu], writes=[ws])
            for bi, (b0, b1) in enumerate(blocks):
                for br, (ub, woff) in enumerate(((ua, 0), (ug, 128))):
                    pp = (pu[(2 * j + br) % 4] if bi == 0 else pus[br])
                    for k in range(8):
                        sy.op("pe", lambda k=k, pp=pp, ws=ws, woff=woff, b0=b0, b1=b1, hb=hb, c0=c0: nc.tensor.matmul(
                            pp[:, 0:b1 - b0], lhsT=ws[:, k, woff:woff + 128], rhs=hb[:, k, c0 + b0:c0 + b1],
                            start=(k == 0), stop=(k == 7)), reads=[ws, hb], writes=[pp])
                    sy.op("act", lambda pp=pp, ub=ub, b0=b0, b1=b1: nc.scalar.copy(out=ub[:, b0:b1], in_=pp[:, 0:b1 - b0]),
                          reads=[pp], writes=[ub])
            ch = j
            sy.op("dve", lambda ch=ch, gt=gt: nc.vector.tensor_scalar(
                out=ta[:, 0:gt], in0=ua[:, 1:gt + 1], scalar1=cw_s[:, ch, 1:2], scalar2=cw_s[:, ch, 3:4],
                op0=ALU.mult, op1=ALU.add), reads=[ua, cw_s], writes=[ta])
            sy.op("dve", lambda ch=ch, gt=gt: nc.vector.scalar_tensor_tensor(
                out=ta[:, 0:gt], in0=ua[:, 0:gt], scalar=cw_s[:, ch, 0:1], in1=ta[:, 0:gt],
                op0=ALU.mult, op1=ALU.add), reads=[ua, cw_s, ta], writes=[ta])
            sy.op("dve", lambda ch=ch, gt=gt: nc.vector.scalar_tensor_tensor(
                out=ta[:, 0:gt], in0=ua[:, 2:gt + 2], scalar=cw_s[:, ch, 2:3], in1=ta[:, 0:gt],
                op0=ALU.mult, op1=ALU.add), reads=[ua, cw_s, ta], writes=[ta])
            ch = NFF + j
            sy.op("pool", lambda ch=ch, gt=gt: nc.gpsimd.tensor_scalar(
                out=tg[:, 0:gt], in0=ug[:, 1:gt + 1], scalar1=cw_s[:, ch, 1:2], scalar2=cw_s[:, ch, 3:4],
                op0=ALU.mult, op1=ALU.add), reads=[ug, cw_s], writes=[tg])
            for tap in (0, 2):
                sy.op("pool", lambda ch=ch, gt=gt, tap=tap: nc.gpsimd.tensor_scalar(
                    out=tq[:, 0:gt], in0=ug[:, tap:gt + tap], scalar1=cw_s[:, ch, tap:tap + 1], scalar2=None,
                    op0=ALU.mult), reads=[ug, cw_s], writes=[tq])
                sy.op("pool", lambda gt=gt: nc.gpsimd.tensor_tensor(out=tg[:, 0:gt], in0=tg[:, 0:gt], in1=tq[:, 0:gt], op=ALU.add),
                      reads=[tg, tq], writes=[tg])
            sy.op("act", lambda gt=gt: nc.scalar.activation(out=sg[:, 0:gt], in_=tg[:, 0:gt], func=AF.Silu), reads=[tg], writes=[sg])
            sy.op("dve", lambda j=j, gt=gt: nc.vector.tensor_tensor(out=actT[:, j, 0:gt], in0=sg[:, 0:gt], in1=ta[:, 0:gt], op=ALU.mult),
                  reads=[sg, ta], writes=[actT])
        for ti in range(gt // 128):
            t = t0 + ti
            load_x(t, t)
            xb, z, tmp, ot, st = xs[t % 2], zs[t % 2], tmps[t % 2], outs[t % 2], sts[t % 2]
            gi = 1 if t >= TL else 0
            for nb in range(2):
                pj = pd[nb]
                for j in range(NFF):
                    sy.op("pe", lambda j=j, pj=pj, nb=nb, ti=ti: nc.tensor.matmul(
                        pj[:, :], lhsT=actT[:, j, ti * 128:(ti + 1) * 128], rhs=wd_s[:, j, nb * 512:(nb + 1) * 512],
                        start=(j == 0), stop=(j == NFF - 1)), reads=[actT, wd_s], writes=[pj])
                sl = slice(nb * 512, (nb + 1) * 512)
                sy.op("dve", lambda pj=pj, sl=sl, tmp=tmp, gi=gi: nc.vector.tensor_tensor(
                    out=tmp[:, sl], in0=pj[:, :], in1=vs[:, gi, sl], op=ALU.mult), reads=[pj, vs], writes=[tmp])
            sy.op("dve", lambda xb=xb, tmp=tmp, z=z: nc.vector.scalar_tensor_tensor(
                out=z[:, :], in0=xb[:, :], scalar=DN_ALPHA, in1=tmp[:, :], op0=ALU.mult, op1=ALU.add), reads=[xb, tmp], writes=[z])
            ln_tile(p, nc, sy, z, tmp, st, None if p.dry else vs[:, 2, :], None if p.dry else vs[:, 3, :], ot)
            sy.dma("sp", do[t % 2], lambda ot=ot, t=t: nc.sync.dma_start(out=xo[t * 128:(t + 1) * 128, :], in_=ot[:, :]),
                   reads=[ot], writes=[xo])


def _fm(v):
    return np.ascontiguousarray(np.asarray(v, np.float32).reshape(8, 128).T)


def _bc(v):
    return np.broadcast_to(np.asarray(v, np.float32)[None, :], (128, v.shape[-1]))


def _rope_tables(base):
    t = np.arange(base, base + LTOK)
    row = (t // GRID_W).astype(np.float32)
    col = (t % GRID_W).astype(np.float32)
    half = HD // 2
    inv = (np.float32(10000.0) ** (-np.arange(0, half, 2, dtype=np.float32) / np.float32(half))).astype(np.float32)
    ar = row[:, None] * inv
    ac = col[:, None] * inv
    ang = np.concatenate([ar, ar, ac, ac], -1).astype(np.float32)
    cos = np.cos(ang).astype(np.float32)
    sin = np.sin(ang).astype(np.float32)
    sgn = np.concatenate([-np.ones(16), np.ones(16), -np.ones(16), np.ones(16)]).astype(np.float32)
    pm = lambda a: np.ascontiguousarray(a.reshape(TL, 128, 64).transpose(1, 0, 2))
    return pm(cos), pm(sin * sgn)


NEGM = -30000.0


def _nbias_core(rpb, r):
    out = np.full((128, 5, 8, 6, 128), NEGM, np.float32)
    for cls, t in enumerate((0, 1, 5, TL - 2, TL - 1)):
        b = TL * r + t
        lo = t - 1 if t == TL - 1 else t
        nk = 6 if t in (0, TL - 1) else 5
        b0 = TL * r + lo - 2
        j = np.arange(128)[:, None, None]
        wi = np.arange(nk)[None, :, None]
        i = np.arange(128)[None, None, :]
        ktok = (b0 + wi) * 128 + j
        qtok = b * 128 + i
        krow, kcol = ktok // GRID_W, ktok % GRID_W
        row, col = qtok // GRID_W, qtok % GRID_W
        rs_ = np.clip(row - 4, 0, S // GRID_W - 8)
        cs_ = np.clip(col - 8, 0, GRID_W - 16)
        valid = (krow >= rs_) & (krow < rs_ + 8) & (kcol >= cs_) & (kcol < cs_ + 16) & (ktok >= 0) & (ktok < S)
        dr = np.clip(krow - row + 7, 0, 14)
        dc = np.clip(kcol - col + 15, 0, 30)
        vals = rpb[:, dr, dc]
        vals = np.where(valid[None], vals, np.float32(NEGM))
        out[:, cls, :, :nk, :] = vals.transpose(1, 0, 2, 3)
    return np.ascontiguousarray(out.reshape(128, 5, 8, 768))


def _cmasks_core(r):
    j = np.arange(128)[:, None]
    i = np.arange(128)[None, :]
    prev = np.where(j >= i, 1.0, 0.0).astype(np.float32)
    nxt = np.where(j <= i, 1.0, 0.0).astype(np.float32)
    allm = np.zeros((128, 128), np.float32)
    m = np.stack([prev, nxt, allm if r == 0 else prev, allm if r == NCORES - 1 else nxt], 1)
    return np.ascontiguousarray(np.broadcast_to(m[:, :, None, :], (128, 4, 4, 128)).reshape(128, 4, 512))


def _aug_v(v_tok, nh):
    n = v_tok.shape[0]
    a = np.ones((n, nh, 128), NPBF)
    a[:, :, :64] = v_tok.reshape(n, nh, 64)
    return np.ascontiguousarray(a.reshape(n // 128, 128, nh, 128).transpose(1, 0, 2, 3))


_IDENT = np.eye(128, dtype=np.float32)
_DBG = {}


def kernel(x, c, ctx, c_ctx, ada_w, ada_b, ln_g, ln_b, ev_w_in, ev_w_out, ev_q_gain, ev_k_gain, ev_rpb,
           od_w_in, od_w_out, od_sink, ffn_w_up, ffn_conv_w, ffn_conv_b, ffn_w_down):
    f32 = lambda a: np.ascontiguousarray(np.asarray(a, np.float32))
    x, c, ctx, c_ctx = f32(x), f32(c), f32(ctx), f32(c_ctx)
    ada_w, ada_b, ln_g, ln_b = f32(ada_w), f32(ada_b), f32(ln_g), f32(ln_b)
    R = range(NCORES)

    pm = get_prog("M", body_mod)
    res = pm.run([{"v": _fm((c[0] if r % 2 == 0 else c_ctx)), "aw": ada_w[r // 2], "ab": ada_b[r // 2][None, :]} for r in R])
    mod = [[res[2 * l + s]["m"].reshape(6, D) for s in range(2)] for l in range(DEPTH)]

    x_lat = x[0]
    x_ctx = ctx[0]
    for l in range(DEPTH):
        i = l // 2
        even = (l % 2 == 0)
        ml, mc = mod[l]
        x_loc = [np.concatenate([x_lat[r * LTOK:(r + 1) * LTOK], x_ctx], 0) for r in R]
        pp = get_prog("P%d" % even, make_body_proj(even))
        modv = np.ascontiguousarray(np.stack([_fm(ml[1]), _fm(ml[0]), _fm(mc[1]), _fm(mc[0])], -1))
        ims = []
        for r in R:
            cs, sn = _rope_tables(r * LTOK)
            im = {"x": x_loc[r], "w": f32(ev_w_in[i] if even else od_w_in[i]), "modv": modv, "cos": cs, "sin": sn, "ident": _IDENT}
            if even:
                im["gains"] = np.ascontiguousarray(np.broadcast_to(
                    np.stack([f32(ev_q_gain[i]), f32(ev_k_gain[i])], 0)[None], (128, 2, 64)))
            ims.append(im)
        pr = pp.run(ims)

        def halo_tok(key, nh_tiles, tokmajor):
            outl = []
            for r in R:
                ax = 0 if tokmajor else -1
                own = pr[r][key]
                take = lambda a, s0, s1: (a[s0:s1] if tokmajor else a[..., s0:s1])
                hw = nh_tiles * 128
                prev = take(pr[r - 1][key], LTOK - hw, LTOK) if r > 0 else np.zeros_like(take(own, 0, hw))
                nxt = take(pr[r + 1][key], 0, hw) if r < NCORES - 1 else np.zeros_like(take(own, 0, hw))
                outl.append(np.concatenate([prev, take(own, 0, LTOK), nxt, take(own, LTOK, NTOK)], ax))
            return outl

        if even:
            pa = get_prog("ATTA", make_body_attn("A"))
            k_all = np.concatenate([pr[r]["kT"][:, :LTOK] for r in R] + [pr[0]["kT"][:, LTOK:]], 1)[:, None, :]
            v_all = np.concatenate([pr[r]["v"][:LTOK] for r in R] + [pr[0]["v"][LTOK:]], 0)
            va = _aug_v(v_all, 2)
            ar = pa.run([{"qT": pr[r]["qT"], "kT": np.ascontiguousarray(k_all), "va": va, "ident": _IDENT} for r in R])
            pb = get_prog("ATTB", make_body_attn("B"))
            kh = halo_tok("kbT", 2, False)
            vh = halo_tok("vb", 2, True)
            rpb = f32(ev_rpb[i])
            br = pb.run([{"qT": np.ascontiguousarray(pr[r]["qbT"].reshape(128, 4, TT, 128).transpose(0, 2, 1, 3)),
                          "kT": np.ascontiguousarray(kh[r]), "va": _aug_v(vh[r], 8), "ident": _IDENT,
                          "nbias": _nbias_core(rpb, r)} for r in R])
            yT = [np.ascontiguousarray(np.concatenate([ar[r]["yT"], br[r]["yT"]], 2)) for r in R]
            wo = f32(ev_w_out[i])
        else:
            pc = get_prog("ATTC", make_body_attn("C"))
            kh = halo_tok("kT", 1, False)
            vh = halo_tok("v", 1, True)
            cims = [{"qT": pr[r]["qT"], "kT": np.ascontiguousarray(kh[r][:, None, :]), "va": _aug_v(vh[r], 2),
                     "ident": _IDENT, "masks": _cmasks_core(r), "sink": f32(od_sink[i])[None, :]} for r in R]
            _DBG["cims%d" % l] = cims
            cr = pc.run(cims)
            yT = [cr[r]["yT"] for r in R]
            _DBG["yC%d" % l] = yT
            _DBG["prC%d" % l] = pr
            wo = f32(od_w_out[i])
        po_ = get_prog("O", body_oproj)
        vecs = np.ascontiguousarray(np.stack([_bc(ml[2]), _bc(mc[2]), _bc(ln_g[l, 0]), _bc(ln_b[l, 0])], 1))
        orr = po_.run([{"yT": yT[r], "wo": wo, "x": x_loc[r], "vecs": vecs} for r in R])
        xm = [orr[r]["xo"] for r in R]
        _DBG["xm%d" % l] = xm
        pf = get_prog("F", body_ffn)
        modv2 = np.ascontiguousarray(np.stack([_fm(ml[4]), _fm(ml[3]), _fm(mc[4]), _fm(mc[3])], -1))
        vecs2 = np.ascontiguousarray(np.stack([_bc(ml[5]), _bc(mc[5]), _bc(ln_g[l, 1]), _bc(ln_b[l, 1])], 1))
        cw = np.stack([f32(ffn_conv_w[l])[0], f32(ffn_conv_w[l])[1], f32(ffn_conv_w[l])[2], f32(ffn_conv_b[l])], -1)
        cw = np.ascontiguousarray(cw.reshape(2 * NFF, 128, 4).transpose(1, 0, 2))
        ims = []
        for r in R:
            prev = xm[r - 1][LTOK - 1] if r > 0 else np.zeros(D, np.float32)
            nxt = xm[r + 1][0] if r < NCORES - 1 else np.zeros(D, np.float32)
            hm = np.zeros((128, 2), np.float32)
            hm[:, 0] = 1.0 if r > 0 else 0.0
            hm[:, 1] = 1.0 if r < NCORES - 1 else 0.0
            ims.append({"x": xm[r], "xhT": np.ascontiguousarray(np.stack([_fm(prev), _fm(nxt)], -1)), "hmask": hm,
                        "wu": f32(ffn_w_up[l]), "wd": f32(ffn_w_down[l]), "cw": cw, "modv": modv2, "vecs": vecs2, "ident": _IDENT})
        fr = pf.run(ims)
        x_lat = np.concatenate([fr[r]["xo"][:LTOK] for r in R], 0)
        x_ctx = fr[0]["xo"][LTOK:]
        _DBG["x%d" % l] = (x_lat, x_ctx)
        if _DBG.get("stop_after") == l:
            break
    return np.ascontiguousarray(x_lat[None].astype(np.float32))
```

```python
import numpy as np
import ml_dtypes
import concourse.bass as bass
import concourse.mybir as mybir
from concourse.bass_utils import run_bass_kernel_spmd

F32 = mybir.dt.float32
BF16 = mybir.dt.bfloat16
ALU = mybir.AluOpType
AF = mybir.ActivationFunctionType
AX = mybir.AxisListType
NPBF = ml_dtypes.bfloat16

NCORES = 8
D = 1024
S = 16384
DEPTH = 4
GRID_W = 64
CTX = 256
HD = 64
D_FF = 2816
LN_EPS = 1e-5
RMS_EPS = 1e-6
DN_ALPHA = (2 * DEPTH) ** 0.25
TL = 16
TC = 2
TT = TL + TC
NTOK = TT * 128
LTOK = TL * 128


class Buf:
    __slots__ = ("name", "t", "lw", "rd", "psum", "lwd")

    def __init__(self, name, t, psum=False):
        self.name = name
        self.t = t
        self.psum = psum
        self.lw = None
        self.rd = {}
        self.lwd = {}

    def __getitem__(self, k):
        return self.t[k]


class Sync:
    ENGS = ("pe", "act", "dve", "pool", "sp")

    def __init__(self):
        self.needed = set()

    def begin(self, nc, dry):
        self.nc = nc
        self.dry = dry
        self.idx = {e: 0 for e in self.ENGS}
        self.marks = {e: 0 for e in self.ENGS}
        self.markval = {}
        self.seen = {e: {o: 0 for o in self.ENGS} for e in self.ENGS}
        self.seen_dma = {}
        self.n_wait = 0
        self.ndsem = 0
        self.dsem_cnt = {}
        if not dry:
            self.eng = {"pe": nc.tensor, "act": nc.scalar, "dve": nc.vector,
                        "pool": nc.gpsimd, "sp": nc.sync}
            self.sem = {e: nc.alloc_semaphore("s_" + e) for e in self.ENGS}
            self.dsems = []

    def _deps(self, reads, writes):
        deps = set()
        for b in reads:
            if b.lw is not None:
                deps.add(b.lw)
            for w_ in b.lwd.values():
                deps.add(w_)
            if b.psum:
                for r in b.rd.values():
                    deps.add(r)
        for b in writes:
            if b.lw is not None:
                deps.add(b.lw)
            for w_ in b.lwd.values():
                deps.add(w_)
            for r in b.rd.values():
                deps.add(r)
        return deps

    def _post(self, me, reads, writes):
        key = me[0] if me[0] != "dma" else ("dma", me[1])
        for b in reads:
            b.rd[key] = me
        for b in writes:
            b.lw = me
            b.rd = {}
            if me[0] == "dma":
                b.lwd[me[1]] = me
            else:
                b.lwd = {}

    def op(self, e, fn, reads=(), writes=()):
        self._waits(e, self._deps(reads, writes))
        me = (e, self.idx[e])
        self.idx[e] += 1
        if me in self.needed:
            self.marks[e] += 1
            self.markval[me] = self.marks[e]
        if not self.dry:
            ins = fn()
            if me in self.needed:
                ins.then_inc(self.sem[e], 1)
        self._post(me, reads, writes)
        return me

    def _waits(self, e, deps):
        best = {}
        for d in deps:
            if d[0] == "dma":
                self._dma_wait(e, d)
                continue
            pe_, i_ = d
            if pe_ == e and e == "pe":
                continue
            if pe_ not in best or best[pe_] < i_:
                best[pe_] = i_
        for pe_, i_ in best.items():
            if self.dry:
                self.needed.add((pe_, i_))
                continue
            v = self.markval[(pe_, i_)]
            if self.seen[e][pe_] >= v:
                continue
            self.seen[e][pe_] = v
            self.eng[e].wait_ge(self.sem[pe_], v)
            self.n_wait += 1

    def new_dma_sem(self):
        k = self.ndsem
        self.ndsem += 1
        self.dsem_cnt[k] = 0
        if not self.dry:
            self.dsems.append(self.nc.alloc_semaphore("d%d" % k))
        return k

    def _dma_wait(self, e, d):
        k = d[1]
        v = self.dsem_cnt[k]
        if self.seen_dma.get((e, k), 0) >= v:
            return
        self.seen_dma[(e, k)] = v
        if not self.dry:
            self.eng[e].wait_ge(self.dsems[k], v)
            self.n_wait += 1

    def dma(self, e, k, fn, reads=(), writes=()):
        self._waits(e, self._deps(reads, writes))
        self.dsem_cnt[k] += 16
        me = ("dma", k, self.dsem_cnt[k])
        if not self.dry:
            fn().then_inc(self.dsems[k], 16)
        self._post(me, reads, writes)
        return me

    def wait_all(self, e, bufs):
        self._waits(e, self._deps((), bufs))


class Prog:
    def __init__(self, body):
        self.body = body
        self.sy = Sync()
        self.outs = []
        self._build(True)
        self._build(False)

    def _build(self, dry):
        self.dry = dry
        self.nc = None if dry else bass.Bass("TRN2", target_bir_lowering=False)
        self.sy.begin(self.nc, dry)
        self.outs = []
        self.body(self)
        self.sy.wait_all("sp", self.outs)
        self.sy.wait_all("pool", self.outs)

    def dram(self, name, shape, dt, kind="Internal"):
        b = Buf(name, None if self.dry else self.nc.dram_tensor(name, list(shape), dt, kind=kind).ap())
        if kind == "ExternalOutput":
            self.outs.append(b)
        return b

    def inp(self, name, shape, dt=F32):
        return self.dram(name, shape, dt, "ExternalInput")

    def out(self, name, shape, dt=F32):
        return self.dram(name, shape, dt, "ExternalOutput")

    def sb(self, name, shape, dt):
        return Buf(name, None if self.dry else self.nc.alloc_sbuf_tensor(name, list(shape), dt).ap())

    def ps(self, name, shape, dt=F32):
        return Buf(name, None if self.dry else self.nc.alloc_psum_tensor(name, list(shape), dt).ap(), psum=True)

    def sbs(self, name, n, shape, dt):
        return [self.sb("%s%d" % (name, i), shape, dt) for i in range(n)]

    def pss(self, name, n, shape, dt=F32):
        return [self.ps("%s%d" % (name, i), shape, dt) for i in range(n)]

    def run(self, in_maps):
        res = run_bass_kernel_spmd(self.nc, in_maps, core_ids=list(range(NCORES)))
        return res.results


def cap(buf, off, dims):
    return bass.AP(buf.t.tensor, off, [list(d) for d in dims])


def body_mod(p):
    nc, sy = p.nc, p.sy
    v = p.inp("v", [128, 8])
    aw = p.inp("aw", [D, 6 * D])
    ab = p.inp("ab", [1, 6 * D])
    m = p.out("m", [1, 6 * D])
    vs = p.sb("vs", [128, 8], F32)
    sv = p.sb("sv", [128, 8], F32)
    abs_ = p.sb("abs", [1, 6 * D], F32)
    ms = p.sb("ms", [1, 6 * D], F32)
    wts = p.sbs("wt", 2, [128, 8, 512], F32)
    pm = p.pss("pm", 2, [1, 512])
    d0 = sy.new_dma_sem()
    dw = [sy.new_dma_sem() for _ in range(2)]
    sy.dma("sp", d0, lambda: nc.sync.dma_start(out=vs[:, :], in_=v[:, :]), reads=[v], writes=[vs])
    sy.dma("sp", d0, lambda: nc.sync.dma_start(out=abs_[:, :], in_=ab[:, :]), reads=[ab], writes=[abs_])
    sy.op("act", lambda: nc.scalar.activation(out=sv[:, :], in_=vs[:, :], func=AF.Silu), reads=[vs], writes=[sv])
    for j in range(12):
        wt = wts[j % 2]
        sy.dma("sp", dw[j % 2], lambda j=j, wt=wt: nc.sync.dma_start(
            out=wt[:, :, :], in_=aw.t[:, j * 512:(j + 1) * 512].rearrange("(k p) n -> p k n", p=128)),
            reads=[aw], writes=[wt])
        pj = pm[j % 2]
        for k in range(8):
            sy.op("pe", lambda k=k, wt=wt, pj=pj: nc.tensor.matmul(
                pj[:, :], lhsT=sv[:, k:k + 1], rhs=wt[:, k, :], start=(k == 0), stop=(k == 7)),
                reads=[sv, wt], writes=[pj])
        sl = slice(j * 512, (j + 1) * 512)
        sy.op("dve", lambda pj=pj, sl=sl: nc.vector.tensor_tensor(
            out=ms[:, sl], in0=pj[:, :], in1=abs_[:, sl], op=ALU.add), reads=[pj, abs_], writes=[ms])
        if j in (2, 3, 8, 9):
            sy.op("dve", lambda sl=sl: nc.vector.tensor_scalar_add(out=ms[:, sl], in0=ms[:, sl], scalar1=1.0),
                  reads=[ms], writes=[ms])
    sy.dma("sp", d0, lambda: nc.sync.dma_start(out=m[:, :], in_=ms[:, :]), reads=[ms], writes=[m])


def make_body_proj(even):
    G = 4 if even else 8
    NQ = G * 128
    NW = 2304 if even else 1280
    HR = 2 * G + 2
    nblk = (NW + 511) // 512

    def body(p):
        nc, sy = p.nc, p.sy
        x = p.inp("x", [NTOK, D])
        w = p.inp("w", [D, NW])
        modv = p.inp("modv", [128, 8, 4])
        cosd = p.inp("cos", [128, TL, 64])
        sind = p.inp("sin", [128, TL, 64])
        identd = p.inp("ident", [128, 128])
        qT = p.out("qT", [128, TT, G, 128], BF16)
        kT = p.out("kT", [128, NTOK], BF16)
        vo = p.out("v", [NTOK, 128], BF16)
        if even:
            gains = p.inp("gains", [128, 2, 64])
            qbT = p.out("qbT", [128, 4, NTOK], BF16)
            kbT = p.out("kbT", [128, 4, NTOK], BF16)
            vbo = p.out("vb", [NTOK, 512], BF16)

        wb = p.sb("wb", [128, 8, NW], BF16)
        mods = p.sb("mods", [128, 8, 4], F32)
        coss = p.sb("coss", [128, TL, 64], F32)
        sins = p.sb("sins", [128, TL, 64], F32)
        ident = p.sb("ident_s", [128, 128], F32)
        identb = p.sb("identb", [128, 128], BF16)
        qT_s = p.sb("qT_s", [128, TT, G, 128], BF16)
        kT_s = p.sb("kT_s", [128, NTOK], BF16)
        if even:
            gs = p.sb("gs", [128, 2, 64], F32)
            qbT_s = p.sb("qbT_s", [128, 4, NTOK], BF16)
            kbT_s = p.sb("kbT_s", [128, 4, NTOK], BF16)
        xs = p.sbs("xs", 2, [128, D], F32)
        hT = p.sbs("hT", 2, [128, 8, 128], BF16)
        qkv = p.sbs("qkv", 2, [128, NW], F32)
        qkvb = p.sbs("qkvb", 3, [128, NW], BF16)
        t1 = p.sb("t1", [128, HR * 64], F32)
        t2 = p.sb("t2", [128, HR * 64], F32)
        ss = p.sb("ss", [128, 16], F32)
        pTs = p.pss("pT", 2, [128, 4, 128], F32)
        pq = p.pss("pq", 2, [128, 512], F32)
        ptr = p.pss("ptr", 2, [128, 8, 128], BF16)

        d0 = sy.new_dma_sem()
        dx = [sy.new_dma_sem() for _ in range(2)]
        dv = [sy.new_dma_sem() for _ in range(3)]
        dout = sy.new_dma_sem()
        sy.dma("sp", d0, lambda: nc.sync.dma_start(out=mods[:, :, :], in_=modv[:, :, :]), reads=[modv], writes=[mods])
        sy.dma("sp", d0, lambda: nc.sync.dma_start(out=coss[:, :, :], in_=cosd[:, :, :]), reads=[cosd], writes=[coss])
        sy.dma("sp", d0, lambda: nc.sync.dma_start(out=sins[:, :, :], in_=sind[:, :, :]), reads=[sind], writes=[sins])
        sy.dma("sp", d0, lambda: nc.sync.dma_start(out=ident[:, :], in_=identd[:, :]), reads=[identd], writes=[ident])
        if even:
            sy.dma("sp", d0, lambda: nc.sync.dma_start(out=gs[:, :, :], in_=gains[:, :, :]), reads=[gains], writes=[gs])
        dwt = sy.new_dma_sem()
        sy.dma("pool", dwt, lambda: nc.gpsimd.dma_start(
            out=wb[:, :, :], in_=w.t.rearrange("(k p) n -> p k n", p=128)), reads=[w], writes=[wb])
        sy.op("dve", lambda: nc.vector.tensor_copy(out=identb[:, :], in_=ident[:, :]), reads=[ident], writes=[identb])

        def load_x(t):
            b = xs[t % 2]
            sy.dma("sp", dx[t % 2], lambda: nc.sync.dma_start(out=b[:, :], in_=x[t * 128:(t + 1) * 128, :]),
                   reads=[x], writes=[b])

        import os
        LV = int(os.environ.get("PLV", "9"))
        NT_ = int(os.environ.get("PNT", str(TT)))
        load_x(0)
        for t in range(NT_):
            is_ctx = t >= TL
            mo = 2 if is_ctx else 0
            if t + 1 < NT_:
                load_x(t + 1)
            xb, hb, qb_, bb = xs[t % 2], hT[t % 2], qkv[t % 2], qkvb[t % 3]
            for c in range(8):
                sy.op("pe", lambda c=c, xb=xb: nc.tensor.transpose(pTs[c // 4][:, c % 4, :], xb[:, c * 128:(c + 1) * 128], ident[:, :]),
                      reads=[xb, ident], writes=[pTs[c // 4]])
            for c in range(8):
                pTc = pTs[c // 4]
                if c < 4:
                    sy.op("dve", lambda c=c, hb=hb, pTc=pTc: nc.vector.tensor_scalar(
                        out=hb[:, c, :], in0=pTc[:, c % 4, :], scalar1=mods[:, c, mo:mo + 1], scalar2=mods[:, c, mo + 1:mo + 2],
                        op0=ALU.mult, op1=ALU.add), reads=[pTc, mods], writes=[hb])
                else:
                    sy.op("act", lambda c=c, hb=hb, pTc=pTc: nc.scalar.activation(
                        out=hb[:, c, :], in_=pTc[:, c % 4, :], func=AF.Identity, bias=mods[:, c, mo + 1:mo + 2],
                        scale=mods[:, c, mo:mo + 1]), reads=[pTc, mods], writes=[hb])
            for j in range(nblk):
                n0, n1 = j * 512, min(NW, (j + 1) * 512)
                pj = pq[j % 2]
                for c in range(8):
                    sy.op("pe", lambda c=c, hb=hb, pj=pj, n0=n0, n1=n1: nc.tensor.matmul(
                        pj[:, 0:n1 - n0], lhsT=hb[:, c, :], rhs=wb[:, c, n0:n1], start=(c == 0), stop=(c == 7)),
                        reads=[hb, wb], writes=[pj])
                if j % 2 == 0:
                    sy.op("act", lambda pj=pj, n0=n0, n1=n1, qb_=qb_: nc.scalar.copy(out=qb_[:, n0:n1], in_=pj[:, 0:n1 - n0]),
                          reads=[pj], writes=[qb_])
                else:
                    sy.op("dve", lambda pj=pj, n0=n0, n1=n1, qb_=qb_: nc.vector.tensor_copy(out=qb_[:, n0:n1], in_=pj[:, 0:n1 - n0]),
                          reads=[pj], writes=[qb_])
            if LV < 2:
                continue
            if even and LV >= 3:
                sy.op("dve", lambda qb_=qb_: nc.vector.tensor_tensor(out=t1[:, 0:640], in0=qb_[:, 0:640], in1=qb_[:, 0:640], op=ALU.mult),
                      reads=[qb_], writes=[t1])
                sy.op("dve", lambda: nc.vector.tensor_reduce(out=ss[:, 0:10], in_=cap(t1, 0, [[HR * 64, 128], [64, 10], [1, 64]]),
                                                             axis=AX.X, op=ALU.add), reads=[t1], writes=[ss])
                sy.op("dve", lambda: nc.vector.tensor_scalar_add(out=ss[:, 0:10], in0=ss[:, 0:10], scalar1=64.0 * RMS_EPS), reads=[ss], writes=[ss])
                sy.op("act", lambda: nc.scalar.sqrt(out=ss[:, 0:10], in_=ss[:, 0:10]), reads=[ss], writes=[ss])
                sy.op("dve", lambda: nc.vector.reciprocal(out=ss[:, 0:10], in_=ss[:, 0:10]), reads=[ss], writes=[ss])
                sy.op("dve", lambda: nc.vector.tensor_scalar_mul(out=ss[:, 8:10], in0=ss[:, 8:10], scalar1=8.0), reads=[ss], writes=[ss])
                sy.op("dve", lambda qb_=qb_: nc.vector.tensor_tensor(
                    out=cap(qb_, 0, [[NW, 128], [64, 10], [1, 64]]), in0=cap(qb_, 0, [[NW, 128], [64, 10], [1, 64]]),
                    in1=cap(ss, 0, [[16, 128], [1, 10], [0, 64]]), op=ALU.mult), reads=[qb_, ss], writes=[qb_])
                sy.op("dve", lambda qb_=qb_: nc.vector.tensor_tensor(
                    out=cap(qb_, 0, [[NW, 128], [64, 8], [1, 64]]), in0=cap(qb_, 0, [[NW, 128], [64, 8], [1, 64]]),
                    in1=cap(gs, 0, [[128, 128], [0, 8], [1, 64]]), op=ALU.mult), reads=[qb_, gs], writes=[qb_])
                sy.op("dve", lambda qb_=qb_: nc.vector.tensor_tensor(
                    out=cap(qb_, 512, [[NW, 128], [64, 2], [1, 64]]), in0=cap(qb_, 512, [[NW, 128], [64, 2], [1, 64]]),
                    in1=cap(gs, 64, [[128, 128], [0, 2], [1, 64]]), op=ALU.mult), reads=[qb_, gs], writes=[qb_])
            if not is_ctx and LV >= 4:
                W_ = HR * 64
                sy.op("dve", lambda qb_=qb_, t=t: nc.vector.tensor_tensor(
                    out=cap(t1, 0, [[W_, 128], [64, HR], [1, 64]]), in0=cap(qb_, 0, [[NW, 128], [64, HR], [1, 64]]),
                    in1=cap(coss, t * 64, [[TL * 64, 128], [0, HR], [1, 64]]), op=ALU.mult), reads=[qb_, coss], writes=[t1])
                for hf in range(2):
                    sy.op("dve", lambda qb_=qb_, t=t, hf=hf: nc.vector.tensor_tensor(
                        out=cap(t2, hf * 16, [[W_, 128], [64, HR], [32, 2], [1, 16]]),
                        in0=cap(qb_, (1 - hf) * 16, [[NW, 128], [64, HR], [32, 2], [1, 16]]),
                        in1=cap(sins, t * 64 + hf * 16, [[TL * 64, 128], [0, HR], [32, 2], [1, 16]]), op=ALU.mult),
                        reads=[qb_, sins], writes=[t2])
                sy.op("dve", lambda qb_=qb_: nc.vector.tensor_tensor(out=qb_[:, 0:W_], in0=t1[:, 0:W_], in1=t2[:, 0:W_], op=ALU.add),
                      reads=[t1, t2], writes=[qb_])
            sy.op("act", lambda qb_=qb_, bb=bb: nc.scalar.activation(
                out=cap(bb, 0, [[NW, 128], [64, 2], [128, G], [1, 64]]),
                in_=cap(qb_, 0, [[NW, 128], [G * 64, 2], [64, G], [1, 64]]), func=AF.Copy, scale=(1.0 if even else 0.125)),
                reads=[qb_], writes=[bb])
            if even:
                sy.op("act", lambda qb_=qb_, bb=bb: nc.scalar.copy(out=bb[:, NQ:768], in_=qb_[:, NQ:768]), reads=[qb_], writes=[bb])
                sy.op("act", lambda qb_=qb_, bb=bb: nc.scalar.activation(out=bb[:, 768:1280], in_=qb_[:, 768:1280], func=AF.Copy, scale=0.125),
                      reads=[qb_], writes=[bb])
                sy.op("act", lambda qb_=qb_, bb=bb: nc.scalar.copy(out=bb[:, 1280:NW], in_=qb_[:, 1280:NW]), reads=[qb_], writes=[bb])
            else:
                sy.op("act", lambda qb_=qb_, bb=bb: nc.scalar.copy(out=bb[:, NQ:NW], in_=qb_[:, NQ:NW]), reads=[qb_], writes=[bb])
            dvk = dv[t % 3]
            sy.dma("sp", dvk, lambda bb=bb, t=t: nc.sync.dma_start(out=vo[t * 128:(t + 1) * 128, :], in_=bb[:, NQ + 128:NQ + 256]),
                   reads=[bb], writes=[vo])
            if even:
                sy.dma("sp", dvk, lambda bb=bb, t=t: nc.sync.dma_start(out=vbo[t * 128:(t + 1) * 128, :], in_=bb[:, 1792:2304]),
                       reads=[bb], writes=[vbo])
            if LV < 5:
                continue
            pa = ptr[t % 2]
            for g in range(G):
                sy.op("pe", lambda g=g, bb=bb, pa=pa: nc.tensor.transpose(
                    pa[:, g, :], bb[:, g * 128:(g + 1) * 128], identb[:, :]),
                    reads=[bb, identb], writes=[pa])
            sy.op("dve", lambda pa=pa, t=t: nc.vector.tensor_copy(out=qT_s[:, t, :, :], in_=pa[:, 0:G, :]),
                  reads=[pa], writes=[qT_s])
            if LV == 5:
                continue
            if even:
                pb = ptr[(t + 1) % 2]
                sy.op("pe", lambda bb=bb, pb=pb: nc.tensor.transpose(pb[:, 0, :], bb[:, NQ:NQ + 128], identb[:, :]),
                      reads=[bb, identb], writes=[pb])
                for c in range(4):
                    sy.op("pe", lambda c=c, bb=bb, pb=pb: nc.tensor.transpose(pb[:, 1 + c, :], bb[:, 768 + c * 128:768 + (c + 1) * 128], identb[:, :]),
                          reads=[bb, identb], writes=[pb])
                if LV != 7:
                    sy.op("act", lambda pb=pb, t=t: nc.scalar.copy(out=kT_s[:, t * 128:(t + 1) * 128], in_=pb[:, 0, :]),
                          reads=[pb], writes=[kT_s])
                if LV != 8:
                    sy.op("dve", lambda pb=pb, t=t: nc.vector.tensor_copy(out=qbT_s[:, :, t * 128:(t + 1) * 128], in_=pb[:, 1:5, :]),
                          reads=[pb], writes=[qbT_s])
                if LV in (7, 8):
                    continue
                if LV == 6:
                    continue
                for c in range(4):
                    sy.op("pe", lambda c=c, bb=bb, pa=pa: nc.tensor.transpose(pa[:, 4 + c, :], bb[:, 1280 + c * 128:1280 + (c + 1) * 128], identb[:, :]),
                          reads=[bb, identb], writes=[pa])
                sy.op("act", lambda pa=pa, t=t: nc.scalar.copy(out=kbT_s[:, :, t * 128:(t + 1) * 128], in_=pa[:, 4:8, :]),
                      reads=[pa], writes=[kbT_s])
            else:
                pb = ptr[(t + 1) % 2]
                sy.op("pe", lambda bb=bb, pb=pb: nc.tensor.transpose(pb[:, 0, :], bb[:, NQ:NQ + 128], identb[:, :]),
                      reads=[bb, identb], writes=[pb])
                sy.op("act", lambda pb=pb, t=t: nc.scalar.copy(out=kT_s[:, t * 128:(t + 1) * 128], in_=pb[:, 0, :]),
                      reads=[pb], writes=[kT_s])
        sy.dma("sp", dout, lambda: nc.sync.dma_start(out=qT[:, :, :, :], in_=qT_s[:, :, :, :]), reads=[qT_s], writes=[qT])
        sy.dma("sp", dout, lambda: nc.sync.dma_start(out=kT[:, :], in_=kT_s[:, :]), reads=[kT_s], writes=[kT])
        if even:
            sy.dma("sp", dout, lambda: nc.sync.dma_start(out=qbT[:, :, :], in_=qbT_s[:, :, :]), reads=[qbT_s], writes=[qbT])
            sy.dma("sp", dout, lambda: nc.sync.dma_start(out=kbT[:, :, :], in_=kbT_s[:, :, :]), reads=[kbT_s], writes=[kbT])

    return body


_PROGS = {}


def get_prog(name, body):
    if name not in _PROGS:
        _PROGS[name] = Prog(body)
    return _PROGS[name]


def make_body_attn(kind):
    G = {"A": 4, "C": 8, "B": 4}[kind]
    H = {"A": 8, "C": 16, "B": 8}[kind]
    if kind == "A":
        NKT = 128 + TC
    elif kind == "C":
        NKT = 1 + TL + 1 + TC
    else:
        NKT = 2 + TL + 2 + TC
    KC = 1 if kind != "B" else 4
    VH = 2 if kind != "B" else 8

    def body(p):
        nc, sy = p.nc, p.sy
        qT = p.inp("qT", [128, TT, G, 128], BF16)
        kT = p.inp("kT", [128, KC, NKT * 128], BF16)
        va = p.inp("va", [128, NKT, VH, 128], BF16)
        identd = p.inp("ident", [128, 128])
        yT = p.out("yT", [64, TT, H, 128], BF16)
        q_s = p.sb("q_s", [128, TT, G, 128], BF16)
        k_s = p.sb("k_s", [128, KC, NKT * 128], BF16)
        v_s = p.sb("v_s", [128, NKT, VH, 128], BF16)
        y_s = p.sb("y_s", [64, TT, H, 128], BF16)
        ident = p.sb("ident_s", [128, 128], F32)
        identb = p.sb("identb", [128, 128], BF16)
        pts = p.sbs("pt", 4, [128, 512], BF16)
        rc = p.sbs("rc", 2, [128, 512], F32)
        rs = p.sbs("rs", 2, [64, 512], F32)
        psS = p.pss("psS", 4, [128, 512], F32)
        psO = p.pss("psO", 2, [128, 512], F32)
        d0 = sy.new_dma_sem()
        dk = sy.new_dma_sem()
        dvs = sy.new_dma_sem()
        dout = sy.new_dma_sem()
        sy.dma("sp", d0, lambda: nc.sync.dma_start(out=q_s[:, :, :, :], in_=qT[:, :, :, :]), reads=[qT], writes=[q_s])
        sy.dma("sp", d0, lambda: nc.sync.dma_start(out=ident[:, :], in_=identd[:, :]), reads=[identd], writes=[ident])
        step = 26 if kind == "A" else NKT
        for k0 in range(0, NKT, step):
            k1 = min(NKT, k0 + step)
            sy.dma("sp", dk, lambda k0=k0, k1=k1: nc.sync.dma_start(out=k_s[:, :, k0 * 128:k1 * 128], in_=kT[:, :, k0 * 128:k1 * 128]),
                   reads=[kT], writes=[k_s])
            sy.dma("sp", dvs, lambda k0=k0, k1=k1: nc.sync.dma_start(out=v_s[:, k0:k1, :, :], in_=va[:, k0:k1, :, :]),
                   reads=[va], writes=[v_s])
        sy.op("dve", lambda: nc.vector.tensor_copy(out=identb[:, :], in_=ident[:, :]), reads=[ident], writes=[identb])
        if kind == "C":
            maskd = p.inp("masks", [128, 4, 512])
            sinkd = p.inp("sink", [1, 16])
            mask_s = p.sb("mask_s", [128, 4, 512], BF16)
            sk = p.sb("sk", [128, 16], F32)
            ske = p.sb("ske", [128, 16], F32)
            dm = sy.new_dma_sem()
            sy.dma("pool", dm, lambda: nc.gpsimd.dma_start(out=mask_s[:, :, :], in_=maskd[:, :, :]), reads=[maskd], writes=[mask_s])
            sy.dma("sp", d0, lambda: nc.sync.dma_start(out=sk[:, :], in_=sinkd.t.partition_broadcast(128)), reads=[sinkd], writes=[sk])
            sy.op("act", lambda: nc.scalar.activation(out=ske[:, :], in_=sk[:, :], func=AF.Exp), reads=[sk], writes=[ske])
        if kind == "B":
            nbd = p.inp("nbias", [128, 5, 8, 768])
            nb_i = p.sb("nb_i", [128, 8, 768], BF16)
            nb_e = p.sb("nb_e", [128, 8, 768], BF16)
            dm = sy.new_dma_sem()
            dme = sy.new_dma_sem()
            sy.dma("pool", dm, lambda: nc.gpsimd.dma_start(out=nb_i[:, :, :], in_=nbd[:, 2, :, :]), reads=[nbd], writes=[nb_i])

        jobs = []
        for t in range(TT):
            is_ctx = t >= TL
            if kind == "A":
                kts = [(kt, None) for kt in (range(128, 130) if is_ctx else range(130))]
                for kv in range(2):
                    jobs.append(dict(t=t, pb=kv * 64, kc=0, q=(lambda t=t: q_s[:, t, :, :]), qoff=0, N=512, kts=kts, vh=kv,
                                     h0=kv * 4, nh=4, sink=None))
            elif kind == "C":
                if is_ctx:
                    kl = [(TL + 2, None), (TL + 3, None)]
                else:
                    mp = 2 if t == 0 else 0
                    mn = 3 if t == TL - 1 else 1
                    kl = [(t, mp), (t + 1, None), (t + 2, mn), (TL + 2, None), (TL + 3, None)]
                for kv in range(2):
                    for hf in range(2):
                        jobs.append(dict(t=t, pb=kv * 64, kc=0, qoff=hf * 4, N=512, kts=kl, vh=kv,
                                         h0=kv * 8 + hf * 4, nh=4, sink=kv * 8 + hf * 4))
            else:
                if is_ctx:
                    kl = [(TL + 4, None), (TL + 5, None)]
                else:
                    cls = 0 if t == 0 else 1 if t == 1 else 3 if t == TL - 2 else 4 if t == TL - 1 else 2
                    lo = t - 1 if t == TL - 1 else t
                    nk = 6 if t in (0, TL - 1) else 5
                    kl = [(lo + i, (cls, i)) for i in range(nk)] + [(TL + 4, None), (TL + 5, None)]
                for h in range(8):
                    jobs.append(dict(t=t, pb=(h % 2) * 64, kc=h // 2, qoff=h // 2, N=128, kts=kl, vh=h,
                                     h0=h, nh=1, sink=None))

        steps = [(ji, ki) for ji, jb in enumerate(jobs) for ki in range(len(jb["kts"]))]
        LOOK = 2
        state = {"cls": None}

        def emit_qk(s_):
            ji, ki = steps[s_]
            jb = jobs[ji]
            t, pb_, N, qoff, kc = jb["t"], jb["pb"], jb["N"], jb["qoff"], jb["kc"]
            kt, bias = jb["kts"][ki]
            pS = psS[s_ % 4]
            if kind == "B" and t < TL and ki == 0:
                cls_t = jb["kts"][0][1][0]
                if cls_t != 2 and cls_t != state["cls"]:
                    state["cls"] = cls_t
                    sy.dma("pool", dme, lambda c5=cls_t: nc.gpsimd.dma_start(out=nb_e[:, :, :], in_=nbd[:, c5, :, :]),
                           reads=[nbd], writes=[nb_e])
            sy.op("pe", lambda: nc.tensor.matmul(
                pS[:, 0:N], lhsT=k_s[pb_:pb_ + 64, kc, kt * 128:(kt + 1) * 128],
                rhs=cap(q_s, pb_ * TT * G * 128 + (t * G + qoff) * 128, [[TT * G * 128, 64], [1, N]]),
                start=True, stop=(bias is None or kind == "C")), reads=[k_s, q_s], writes=[pS])
            if bias is not None and kind == "B":
                cls, wi = bias
                h = jb["vh"]
                nbb = nb_i if cls == 2 else nb_e
                sy.op("pe", lambda: nc.tensor.matmul(
                    pS[:, 0:N], lhsT=identb[:, :], rhs=nbb[:, h, wi * 128:(wi + 1) * 128], start=False, stop=True),
                    reads=[identb, nbb], writes=[pS])

        def emit_pv(s_):
            ji, ki = steps[s_]
            jb = jobs[ji]
            t, N = jb["t"], jb["N"]
            kt, bias = jb["kts"][ki]
            nkt = len(jb["kts"])
            pS, pt, po = psS[s_ % 4], pts[s_ % 4], psO[ji % 2]
            sy.op("act", lambda: nc.scalar.activation(out=pt[:, 0:N], in_=pS[:, 0:N], func=AF.Exp), reads=[pS], writes=[pt])
            if kind == "C" and bias is not None:
                sy.op("dve", lambda: nc.vector.tensor_tensor(
                    out=pt[:, 0:N], in0=pt[:, 0:N], in1=mask_s[:, bias, :], op=ALU.mult), reads=[pt, mask_s], writes=[pt])
            vh = jb["vh"]
            sy.op("pe", lambda: nc.tensor.matmul(
                po[:, 0:N], lhsT=v_s[:, kt, vh, :], rhs=pt[:, 0:N], start=(ki == 0), stop=(ki == nkt - 1)),
                reads=[v_s, pt], writes=[po])
            if ki != nkt - 1:
                return
            r1, r2 = rc[ji % 2], rs[ji % 2]
            if jb["sink"] is not None:
                s0 = jb["sink"]
                sy.op("dve", lambda: nc.vector.tensor_tensor(
                    out=cap(r1, 64 * 512, [[512, 64], [128, 4], [1, 128]]), in0=cap(po, 64 * 512, [[512, 64], [128, 4], [1, 128]]),
                    in1=cap(ske, 64 * 16 + s0, [[16, 64], [1, 4], [0, 128]]), op=ALU.add), reads=[po, ske], writes=[r1])
                sy.op("dve", lambda: nc.vector.reciprocal(out=r1[64:128, 0:N], in_=r1[64:128, 0:N]), reads=[r1], writes=[r1])
            else:
                sy.op("dve", lambda: nc.vector.reciprocal(out=r1[64:128, 0:N], in_=po[64:128, 0:N]), reads=[po], writes=[r1])
            sy.op("dve", lambda: nc.vector.tensor_copy(out=r2[0:64, 0:N], in_=r1[64:128, 0:N]), reads=[r1], writes=[r2])
            h0 = jb["h0"]
            sy.op("dve", lambda: nc.vector.tensor_tensor(
                out=cap(y_s, (t * H + h0) * 128, [[TT * H * 128, 64], [1, N]]), in0=po[0:64, 0:N], in1=r2[0:64, 0:N], op=ALU.mult),
                reads=[po, r2], writes=[y_s])

        ns = len(steps)
        for s_ in range(ns + LOOK):
            if s_ < ns:
                emit_qk(s_)
            if s_ - LOOK >= 0:
                emit_pv(s_ - LOOK)
        sy.dma("sp", dout, lambda: nc.sync.dma_start(out=yT[:, :, :, :], in_=y_s[:, :, :, :]), reads=[y_s], writes=[yT])

    return body


def ln_tile(p, nc, sy, z, tmp, st, gvec, bvec, out_t):
    sy.op("dve", lambda: nc.vector.memset(st[:, 0:2], 0.0), writes=[st])
    sy.op("act", lambda: nc.scalar.activation(out=tmp[:, :], in_=z[:, :], func=AF.Identity, accum_out=st[:, 0:1]),
          reads=[z, st], writes=[tmp, st])
    sy.op("act", lambda: nc.scalar.activation(out=tmp[:, :], in_=z[:, :], func=AF.Square, accum_out=st[:, 1:2]),
          reads=[z, st], writes=[tmp, st])
    sy.op("dve", lambda: nc.vector.tensor_scalar_mul(out=st[:, 2:3], in0=st[:, 0:1], scalar1=1.0 / D), reads=[st], writes=[st])
    sy.op("dve", lambda: nc.vector.tensor_tensor(out=st[:, 3:4], in0=st[:, 2:3], in1=st[:, 2:3], op=ALU.mult), reads=[st], writes=[st])
    sy.op("dve", lambda: nc.vector.scalar_tensor_tensor(out=st[:, 4:5], in0=st[:, 1:2], scalar=1.0 / D, in1=st[:, 3:4],
                                                        op0=ALU.mult, op1=ALU.subtract), reads=[st], writes=[st])
    sy.op("dve", lambda: nc.vector.tensor_scalar_add(out=st[:, 4:5], in0=st[:, 4:5], scalar1=LN_EPS), reads=[st], writes=[st])
    sy.op("act", lambda: nc.scalar.sqrt(out=st[:, 5:6], in_=st[:, 4:5]), reads=[st], writes=[st])
    sy.op("dve", lambda: nc.vector.reciprocal(out=st[:, 6:7], in_=st[:, 5:6]), reads=[st], writes=[st])
    sy.op("dve", lambda: nc.vector.tensor_scalar(out=tmp[:, :], in0=z[:, :], scalar1=st[:, 2:3], scalar2=st[:, 6:7],
                                                 op0=ALU.subtract, op1=ALU.mult), reads=[z, st], writes=[tmp])
    sy.op("pool", lambda: nc.gpsimd.tensor_tensor(out=tmp[:, :], in0=tmp[:, :], in1=gvec, op=ALU.mult), reads=[tmp], writes=[tmp])
    sy.op("dve", lambda: nc.vector.tensor_tensor(out=out_t[:, :], in0=tmp[:, :], in1=bvec, op=ALU.add), reads=[tmp], writes=[out_t])


def body_oproj(p):
    nc, sy = p.nc, p.sy
    yT = p.inp("yT", [64, TT, 16, 128], BF16)
    wo = p.inp("wo", [D, D])
    x = p.inp("x", [NTOK, D])
    vecs = p.inp("vecs", [128, 4, D])
    xo = p.out("xo", [NTOK, D])
    y_s = p.sb("y_s", [64, TT, 16, 128], BF16)
    wo_s = p.sb("wo_s", [64, 16, D], BF16)
    vs = p.sb("vs", [128, 4, D], F32)
    xs = p.sbs("xs", 2, [128, D], F32)
    zs = p.sbs("z", 2, [128, D], F32)
    tmps = p.sbs("tmp", 2, [128, D], F32)
    outs = p.sbs("ot", 2, [128, D], F32)
    sts = p.sbs("st", 2, [128, 8], F32)
    po = p.pss("po", 4, [128, 512], F32)
    d0 = sy.new_dma_sem()
    dw = sy.new_dma_sem()
    dx = [sy.new_dma_sem() for _ in range(2)]
    do = [sy.new_dma_sem() for _ in range(2)]
    sy.dma("sp", d0, lambda: nc.sync.dma_start(out=y_s[:, :, :, :], in_=yT[:, :, :, :]), reads=[yT], writes=[y_s])
    sy.dma("sp", d0, lambda: nc.sync.dma_start(out=vs[:, :, :], in_=vecs[:, :, :]), reads=[vecs], writes=[vs])
    sy.dma("pool", dw, lambda: nc.gpsimd.dma_start(out=wo_s[:, :, :], in_=wo.t.rearrange("(h d) n -> d h n", d=64)),
           reads=[wo], writes=[wo_s])

    def load_x(t):
        b = xs[t % 2]
        sy.dma("sp", dx[t % 2], lambda: nc.sync.dma_start(out=b[:, :], in_=x[t * 128:(t + 1) * 128, :]), reads=[x], writes=[b])

    load_x(0)
    for t in range(TT):
        if t + 1 < TT:
            load_x(t + 1)
        xb, z, tmp, ot, st = xs[t % 2], zs[t % 2], tmps[t % 2], outs[t % 2], sts[t % 2]
        gi = 1 if t >= TL else 0
        for nb in range(2):
            pj = po[(2 * t + nb) % 4]
            for h in range(16):
                sy.op("pe", lambda h=h, pj=pj, nb=nb, t=t: nc.tensor.matmul(
                    pj[:, :], lhsT=y_s[0:64, t, h, :], rhs=wo_s[0:64, h, nb * 512:(nb + 1) * 512], start=(h == 0), stop=(h == 15)),
                    reads=[y_s, wo_s], writes=[pj])
            sl = slice(nb * 512, (nb + 1) * 512)
            sy.op("dve", lambda pj=pj, sl=sl, tmp=tmp, gi=gi: nc.vector.tensor_tensor(
                out=tmp[:, sl], in0=pj[:, :], in1=vs[:, gi, sl], op=ALU.mult), reads=[pj, vs], writes=[tmp])
        sy.op("dve", lambda xb=xb, tmp=tmp, z=z: nc.vector.scalar_tensor_tensor(
            out=z[:, :], in0=xb[:, :], scalar=DN_ALPHA, in1=tmp[:, :], op0=ALU.mult, op1=ALU.add), reads=[xb, tmp], writes=[z])
        ln_tile(p, nc, sy, z, tmp, st, None if p.dry else vs[:, 2, :], None if p.dry else vs[:, 3, :], ot)
        sy.dma("sp", do[t % 2], lambda ot=ot, t=t: nc.sync.dma_start(out=xo[t * 128:(t + 1) * 128, :], in_=ot[:, :]),
               reads=[ot], writes=[xo])


NFF = D_FF // 128


def body_ffn(p):
    nc, sy = p.nc, p.sy
    x = p.inp("x", [NTOK, D])
    xhT = p.inp("xhT", [128, 8, 2])
    hmask = p.inp("hmask", [128, 2])
    wu = p.inp("wu", [D, 2 * D_FF])
    wd = p.inp("wd", [D_FF, D])
    cw = p.inp("cw", [128, 2 * NFF, 4])
    modv = p.inp("modv", [128, 8, 4])
    vecs = p.inp("vecs", [128, 4, D])
    identd = p.inp("ident", [128, 128])
    xo = p.out("xo", [NTOK, D])
    actD = p.dram("actD", [128, NFF, NTOK], BF16)

    GT = 1024
    hT_l = p.sb("hT_l", [128, 8, LTOK + 2], BF16)
    hT_c = p.sb("hT_c", [128, 8, CTX + 2], BF16)
    wd_s = p.sb("wd_s", [128, NFF, D], BF16)
    wus = p.sbs("wu_s", 2, [128, 8, 256], BF16)
    uas = p.sbs("ua", 2, [128, GT + 2], F32)
    ugs = p.sbs("ug", 2, [128, GT + 2], F32)
    tas = p.sbs("ta", 2, [128, GT], F32)
    tgs = p.sbs("tg", 2, [128, GT], F32)
    acts = p.sbs("acs", 2, [128, GT], BF16)
    actin = p.sbs("actin", 2, [128, NFF, 256], BF16)
    cw_s = p.sb("cw_s", [128, 2 * NFF, 4], F32)
    mods = p.sb("mods", [128, 8, 4], F32)
    vs = p.sb("vs", [128, 4, D], F32)
    ident = p.sb("ident_s", [128, 128], F32)
    xh_s = p.sb("xh_s", [128, 8, 2], F32)
    hm_s = p.sb("hm_s", [128, 2], F32)
    sts = p.sbs("st", 2, [128, 8], F32)
    pu = p.pss("pu", 4, [128, 512], F32)
    pus = p.pss("pus", 2, [128, 512], F32)
    pd = p.pss("pd", 2, [128, 512], F32)
    d0 = sy.new_dma_sem()
    dwd = sy.new_dma_sem()
    dwu = [sy.new_dma_sem() for _ in range(2)]
    dx = [sy.new_dma_sem() for _ in range(2)]
    do = [sy.new_dma_sem() for _ in range(2)]
    dact = [sy.new_dma_sem() for _ in range(2)]
    dain = [sy.new_dma_sem() for _ in range(2)]
    for dst, src in ((cw_s, cw), (mods, modv), (vs, vecs), (xh_s, xhT)):
        sy.dma("sp", d0, lambda dst=dst, src=src: nc.sync.dma_start(out=dst.t, in_=src.t), reads=[src], writes=[dst])
    sy.dma("sp", d0, lambda: nc.sync.dma_start(out=ident[:, :], in_=identd[:, :]), reads=[identd], writes=[ident])
    sy.dma("sp", d0, lambda: nc.sync.dma_start(out=hm_s[:, :], in_=hmask[:, :]), reads=[hmask], writes=[hm_s])

    def load_w(j):
        ws = wus[j % 2]
        dk = dwu[j % 2]
        sy.dma("pool", dk, lambda: nc.gpsimd.dma_start(
            out=ws[:, :, 0:128], in_=wu.t[:, j * 128:(j + 1) * 128].rearrange("(k p) n -> p k n", p=128)), reads=[wu], writes=[ws])
        sy.dma("pool", dk, lambda: nc.gpsimd.dma_start(
            out=ws[:, :, 128:256], in_=wu.t[:, D_FF + j * 128:D_FF + (j + 1) * 128].rearrange("(k p) n -> p k n", p=128)),
            reads=[wu], writes=[ws])

    load_w(0)
    load_w(1)
    sy.dma("pool", dwd, lambda: nc.gpsimd.dma_start(out=wd_s[:, :, :], in_=wd.t.rearrange("(j p) n -> p j n", p=128)),
           reads=[wd], writes=[wd_s])

    xs = [uas[0], uas[1]]

    def load_x(t):
        b = xs[t % 2]
        sy.dma("sp", dx[t % 2], lambda: nc.sync.dma_start(out=b[:, 0:D], in_=x[t * 128:(t + 1) * 128, :]), reads=[x], writes=[b])

    load_x(0)
    for t in range(TT):
        if t + 1 < TT:
            load_x(t + 1)
        xb = xs[t % 2]
        is_ctx = t >= TL
        mo = 2 if is_ctx else 0
        dstb = hT_c if is_ctx else hT_l
        c0 = 1 + (t - TL if is_ctx else t) * 128
        for c in range(8):
            pT = pu[c // 4 + 2 * (t % 2)]
            sy.op("pe", lambda c=c, xb=xb, pT=pT: nc.tensor.transpose(pT[:, (c % 4) * 128:(c % 4 + 1) * 128], xb[:, c * 128:(c + 1) * 128], ident[:, :]),
                  reads=[xb, ident], writes=[pT])
        for c in range(8):
            pT = pu[c // 4 + 2 * (t % 2)]
            if c < 4:
                sy.op("dve", lambda c=c, pT=pT, dstb=dstb, c0=c0, mo=mo: nc.vector.tensor_scalar(
                    out=dstb[:, c, c0:c0 + 128], in0=pT[:, (c % 4) * 128:(c % 4 + 1) * 128], scalar1=mods[:, c, mo:mo + 1],
                    scalar2=mods[:, c, mo + 1:mo + 2], op0=ALU.mult, op1=ALU.add), reads=[pT, mods], writes=[dstb])
            else:
                sy.op("act", lambda c=c, pT=pT, dstb=dstb, c0=c0, mo=mo: nc.scalar.activation(
                    out=dstb[:, c, c0:c0 + 128], in_=pT[:, (c % 4) * 128:(c % 4 + 1) * 128], func=AF.Identity,
                    bias=mods[:, c, mo + 1:mo + 2], scale=mods[:, c, mo:mo + 1]), reads=[pT, mods], writes=[dstb])
    for j, col in ((0, 0), (1, LTOK + 1)):
        sy.op("dve", lambda j=j: nc.vector.tensor_tensor(out=xh_s[:, :, j], in0=xh_s[:, :, j], in1=mods[:, :, 0], op=ALU.mult),
              reads=[xh_s, mods], writes=[xh_s])
        sy.op("dve", lambda j=j: nc.vector.tensor_tensor(out=xh_s[:, :, j], in0=xh_s[:, :, j], in1=mods[:, :, 1], op=ALU.add),
              reads=[xh_s, mods], writes=[xh_s])
        sy.op("dve", lambda j=j, col=col: nc.vector.tensor_scalar(out=hT_l[:, :, col], in0=xh_s[:, :, j], scalar1=hm_s[:, j:j + 1],
                                                                scalar2=None, op0=ALU.mult), reads=[xh_s, hm_s], writes=[hT_l])
    sy.op("dve", lambda: nc.vector.memset(hT_c[:, :, 0], 0.0), reads=[hT_c], writes=[hT_c])
    sy.op("dve", lambda: nc.vector.memset(hT_c[:, :, CTX + 1], 0.0), reads=[hT_c], writes=[hT_c])

    groups = [(hT_l, g * GT, GT, g * GT) for g in range(LTOK // GT)] + [(hT_c, 0, CTX, LTOK)]
    ui = 0
    for j in range(NFF):
        ws = wus[j % 2]
        for (hb, c0, gt, tok0) in groups:
            ua, ug, ta, tg, ac = uas[ui % 2], ugs[ui % 2], tas[ui % 2], tgs[ui % 2], acts[ui % 2]
            ui += 1
            blocks = [(b0, min(b0 + 512, gt + 2)) for b0 in range(0, gt + 2, 512)]
            for bi, (b0, b1) in enumerate(blocks):
                for br, (ub, woff) in enumerate(((ua, 0), (ug, 128))):
                    pp = pus[br] if (b1 - b0) < 16 else pu[(2 * bi + br) % 4]
                    for k in range(8):
                        sy.op("pe", lambda k=k, pp=pp, woff=woff, b0=b0, b1=b1: nc.tensor.matmul(
                            pp[:, 0:b1 - b0], lhsT=ws[:, k, woff:woff + 128], rhs=hb[:, k, c0 + b0:c0 + b1],
                            start=(k == 0), stop=(k == 7)), reads=[ws, hb], writes=[pp])
                    sy.op("act", lambda pp=pp, ub=ub, b0=b0, b1=b1: nc.scalar.copy(out=ub[:, b0:b1], in_=pp[:, 0:b1 - b0]),
                          reads=[pp], writes=[ub])
            ch = j
            sy.op("dve", lambda ch=ch: nc.vector.tensor_scalar(
                out=ta[:, 0:gt], in0=ua[:, 1:gt + 1], scalar1=cw_s[:, ch, 1:2], scalar2=cw_s[:, ch, 3:4],
                op0=ALU.mult, op1=ALU.add), reads=[ua, cw_s], writes=[ta])
            for tap in (0, 2):
                sy.op("dve", lambda ch=ch, tap=tap: nc.vector.scalar_tensor_tensor(
                    out=ta[:, 0:gt], in0=ua[:, tap:gt + tap], scalar=cw_s[:, ch, tap:tap + 1], in1=ta[:, 0:gt],
                    op0=ALU.mult, op1=ALU.add), reads=[ua, cw_s, ta], writes=[ta])
            ch = NFF + j
            sy.op("pool", lambda ch=ch: nc.gpsimd.tensor_scalar(
                out=tg[:, 0:gt], in0=ug[:, 1:gt + 1], scalar1=cw_s[:, ch, 1:2], scalar2=cw_s[:, ch, 3:4],
                op0=ALU.mult, op1=ALU.add), reads=[ug, cw_s], writes=[tg])
            for tap in (0, 2):
                sy.op("dve", lambda ch=ch, tap=tap: nc.vector.scalar_tensor_tensor(
                    out=tg[:, 0:gt], in0=ug[:, tap:gt + tap], scalar=cw_s[:, ch, tap:tap + 1], in1=tg[:, 0:gt],
                    op0=ALU.mult, op1=ALU.add), reads=[ug, cw_s, tg], writes=[tg])
            sy.op("act", lambda: nc.scalar.activation(out=tg[:, 0:gt], in_=tg[:, 0:gt], func=AF.Silu), reads=[tg], writes=[tg])
            sy.op("dve", lambda: nc.vector.tensor_tensor(out=ac[:, 0:gt], in0=tg[:, 0:gt], in1=ta[:, 0:gt], op=ALU.mult),
                  reads=[tg, ta], writes=[ac])
            sy.dma("sp", dact[ui % 2], lambda: nc.sync.dma_start(out=actD[:, j, tok0:tok0 + gt], in_=ac[:, 0:gt]),
                   reads=[ac], writes=[actD])
        if j + 2 < NFF:
            load_w(j + 2)

    def load_act(gi):
        b = actin[gi % 2]
        sy.dma("sp", dain[gi % 2], lambda: nc.sync.dma_start(out=b[:, :, :], in_=actD[:, :, gi * 256:(gi + 1) * 256]),
               reads=[actD], writes=[b])

    load_act(0)
    for gi in range(TT // 2):
        if gi + 1 < TT // 2:
            load_act(gi + 1)
        ab = actin[gi % 2]
        for ti in range(2):
            t = gi * 2 + ti
            load_x(t)
            xb, z, tmp, ot, st = xs[t % 2], tas[0], tgs[0], (tas[1] if t % 2 == 0 else tgs[1]), sts[t % 2]
            gsel = 1 if t >= TL else 0
            for nb in range(2):
                pj = pd[nb]
                for j in range(NFF):
                    sy.op("pe", lambda j=j, pj=pj, nb=nb, ti=ti: nc.tensor.matmul(
                        pj[:, :], lhsT=ab[:, j, ti * 128:(ti + 1) * 128], rhs=wd_s[:, j, nb * 512:(nb + 1) * 512],
                        start=(j == 0), stop=(j == NFF - 1)), reads=[ab, wd_s], writes=[pj])
                sl = slice(nb * 512, (nb + 1) * 512)
                sy.op("dve", lambda pj=pj, sl=sl, gsel=gsel: nc.vector.tensor_tensor(
                    out=tmp[:, sl], in0=pj[:, :], in1=vs[:, gsel, sl], op=ALU.mult), reads=[pj, vs], writes=[tmp])
            sy.op("dve", lambda xb=xb: nc.vector.scalar_tensor_tensor(
                out=z[:, :], in0=xb[:, 0:D], scalar=DN_ALPHA, in1=tmp[:, :], op0=ALU.mult, op1=ALU.add), reads=[xb, tmp], writes=[z])
            ln_tile(p, nc, sy, z, tmp, st, None if p.dry else vs[:, 2, :], None if p.dry else vs[:, 3, :], ot)
            sy.dma("sp", do[t % 2], lambda ot=ot, t=t: nc.sync.dma_start(out=xo[t * 128:(t + 1) * 128, :], in_=ot[:, :]),
                   reads=[ot], writes=[xo])


def _fm(v):
    return np.ascontiguousarray(np.asarray(v, np.float32).reshape(8, 128).T)


def _bc(v):
    return np.broadcast_to(np.asarray(v, np.float32)[None, :], (128, v.shape[-1]))


def _rope_tables(base):
    t = np.arange(base, base + LTOK)
    row = (t // GRID_W).astype(np.float32)
    col = (t % GRID_W).astype(np.float32)
    half = HD // 2
    inv = (np.float32(10000.0) ** (-np.arange(0, half, 2, dtype=np.float32) / np.float32(half))).astype(np.float32)
    ar = row[:, None] * inv
    ac = col[:, None] * inv
    ang = np.concatenate([ar, ar, ac, ac], -1).astype(np.float32)
    cos = np.cos(ang).astype(np.float32)
    sin = np.sin(ang).astype(np.float32)
    sgn = np.concatenate([-np.ones(16), np.ones(16), -np.ones(16), np.ones(16)]).astype(np.float32)
    pm = lambda a: np.ascontiguousarray(a.reshape(TL, 128, 64).transpose(1, 0, 2))
    return pm(cos), pm(sin * sgn)


NEGM = -30000.0


def _nbias_core(rpb, r):
    out = np.full((128, 5, 8, 6, 128), NEGM, np.float32)
    for cls, t in enumerate((0, 1, 5, TL - 2, TL - 1)):
        b = TL * r + t
        lo = t - 1 if t == TL - 1 else t
        nk = 6 if t in (0, TL - 1) else 5
        b0 = TL * r + lo - 2
        j = np.arange(128)[:, None, None]
        wi = np.arange(nk)[None, :, None]
        i = np.arange(128)[None, None, :]
        ktok = (b0 + wi) * 128 + j
        qtok = b * 128 + i
        krow, kcol = ktok // GRID_W, ktok % GRID_W
        row, col = qtok // GRID_W, qtok % GRID_W
        rs_ = np.clip(row - 4, 0, S // GRID_W - 8)
        cs_ = np.clip(col - 8, 0, GRID_W - 16)
        valid = (krow >= rs_) & (krow < rs_ + 8) & (kcol >= cs_) & (kcol < cs_ + 16) & (ktok >= 0) & (ktok < S)
        dr = np.clip(krow - row + 7, 0, 14)
        dc = np.clip(kcol - col + 15, 0, 30)
        vals = rpb[:, dr, dc]
        vals = np.where(valid[None], vals, np.float32(NEGM))
        out[:, cls, :, :nk, :] = vals.transpose(1, 0, 2, 3)
    return np.ascontiguousarray(out.reshape(128, 5, 8, 768))


def _cmasks_core(r):
    j = np.arange(128)[:, None]
    i = np.arange(128)[None, :]
    prev = np.where(j >= i, 1.0, 0.0).astype(np.float32)
    nxt = np.where(j <= i, 1.0, 0.0).astype(np.float32)
    allm = np.zeros((128, 128), np.float32)
    m = np.stack([prev, nxt, allm if r == 0 else prev, allm if r == NCORES - 1 else nxt], 1)
    return np.ascontiguousarray(np.broadcast_to(m[:, :, None, :], (128, 4, 4, 128)).reshape(128, 4, 512))


def _aug_v(v_tok, nh):
    n = v_tok.shape[0]
    a = np.ones((n, nh, 128), NPBF)
    a[:, :, :64] = v_tok.reshape(n, nh, 64)
    return np.ascontiguousarray(a.reshape(n // 128, 128, nh, 128).transpose(1, 0, 2, 3))


_IDENT = np.eye(128, dtype=np.float32)
_DBG = {}


def kernel(x, c, ctx, c_ctx, ada_w, ada_b, ln_g, ln_b, ev_w_in, ev_w_out, ev_q_gain, ev_k_gain, ev_rpb,
           od_w_in, od_w_out, od_sink, ffn_w_up, ffn_conv_w, ffn_conv_b, ffn_w_down):
    f32 = lambda a: np.ascontiguousarray(np.asarray(a, np.float32))
    x, c, ctx, c_ctx = f32(x), f32(c), f32(ctx), f32(c_ctx)
    ada_w, ada_b, ln_g, ln_b = f32(ada_w), f32(ada_b), f32(ln_g), f32(ln_b)
    R = range(NCORES)

    pm = get_prog("M", body_mod)
    res = pm.run([{"v": _fm((c[0] if r % 2 == 0 else c_ctx)), "aw": ada_w[r // 2], "ab": ada_b[r // 2][None, :]} for r in R])
    mod = [[res[2 * l + s]["m"].reshape(6, D) for s in range(2)] for l in range(DEPTH)]

    x_lat = x[0]
    x_ctx = ctx[0]
    for l in range(DEPTH):
        i = l // 2
        even = (l % 2 == 0)
        ml, mc = mod[l]
        x_loc = [np.concatenate([x_lat[r * LTOK:(r + 1) * LTOK], x_ctx], 0) for r in R]
        pp = get_prog("P%d" % even, make_body_proj(even))
        modv = np.ascontiguousarray(np.stack([_fm(ml[1]), _fm(ml[0]), _fm(mc[1]), _fm(mc[0])], -1))
        ims = []
        for r in R:
            cs, sn = _rope_tables(r * LTOK)
            im = {"x": x_loc[r], "w": f32(ev_w_in[i] if even else od_w_in[i]), "modv": modv, "cos": cs, "sin": sn, "ident": _IDENT}
            if even:
                im["gains"] = np.ascontiguousarray(np.broadcast_to(
                    np.stack([f32(ev_q_gain[i]), f32(ev_k_gain[i])], 0)[None], (128, 2, 64)))
            ims.append(im)
        pr = pp.run(ims)

        def halo_tok(key, nh_tiles, tokmajor):
            outl = []
            for r in R:
                ax = 0 if tokmajor else -1
                own = pr[r][key]
                take = lambda a, s0, s1: (a[s0:s1] if tokmajor else a[..., s0:s1])
                hw = nh_tiles * 128
                prev = take(pr[r - 1][key], LTOK - hw, LTOK) if r > 0 else np.zeros_like(take(own, 0, hw))
                nxt = take(pr[r + 1][key], 0, hw) if r < NCORES - 1 else np.zeros_like(take(own, 0, hw))
                outl.append(np.concatenate([prev, take(own, 0, LTOK), nxt, take(own, LTOK, NTOK)], ax))
            return outl

        if even:
            pa = get_prog("ATTA", make_body_attn("A"))
            k_all = np.concatenate([pr[r]["kT"][:, :LTOK] for r in R] + [pr[0]["kT"][:, LTOK:]], 1)[:, None, :]
            v_all = np.concatenate([pr[r]["v"][:LTOK] for r in R] + [pr[0]["v"][LTOK:]], 0)
            va = _aug_v(v_all, 2)
            ar = pa.run([{"qT": pr[r]["qT"], "kT": np.ascontiguousarray(k_all), "va": va, "ident": _IDENT} for r in R])
            pb = get_prog("ATTB", make_body_attn("B"))
            kh = halo_tok("kbT", 2, False)
            vh = halo_tok("vb", 2, True)
            rpb = f32(ev_rpb[i])
            br = pb.run([{"qT": np.ascontiguousarray(pr[r]["qbT"].reshape(128, 4, TT, 128).transpose(0, 2, 1, 3)),
                          "kT": np.ascontiguousarray(kh[r]), "va": _aug_v(vh[r], 8), "ident": _IDENT,
                          "nbias": _nbias_core(rpb, r)} for r in R])
            yT = [np.ascontiguousarray(np.concatenate([ar[r]["yT"], br[r]["yT"]], 2)) for r in R]
            wo = f32(ev_w_out[i])
        else:
            pc = get_prog("ATTC", make_body_attn("C"))
            kh = halo_tok("kT", 1, False)
            vh = halo_tok("v", 1, True)
            cims = [{"qT": pr[r]["qT"], "kT": np.ascontiguousarray(kh[r][:, None, :]), "va": _aug_v(vh[r], 2),
                     "ident": _IDENT, "masks": _cmasks_core(r), "sink": f32(od_sink[i])[None, :]} for r in R]
            _DBG["cims%d" % l] = cims
            cr = pc.run(cims)
            yT = [cr[r]["yT"] for r in R]
            _DBG["yC%d" % l] = yT
            _DBG["prC%d" % l] = pr
            wo = f32(od_w_out[i])
        po_ = get_prog("O", body_oproj)
        vecs = np.ascontiguousarray(np.stack([_bc(ml[2]), _bc(mc[2]), _bc(ln_g[l, 0]), _bc(ln_b[l, 0])], 1))
        orr = po_.run([{"yT": yT[r], "wo": wo, "x": x_loc[r], "vecs": vecs} for r in R])
        xm = [orr[r]["xo"] for r in R]
        _DBG["xm%d" % l] = xm
        pf = get_prog("F", body_ffn)
        modv2 = np.ascontiguousarray(np.stack([_fm(ml[4]), _fm(ml[3]), _fm(mc[4]), _fm(mc[3])], -1))
        vecs2 = np.ascontiguousarray(np.stack([_bc(ml[5]), _bc(mc[5]), _bc(ln_g[l, 1]), _bc(ln_b[l, 1])], 1))
        cw = np.stack([f32(ffn_conv_w[l])[0], f32(ffn_conv_w[l])[1], f32(ffn_conv_w[l])[2], f32(ffn_conv_b[l])], -1)
        cw = np.ascontiguousarray(cw.reshape(2 * NFF, 128, 4).transpose(1, 0, 2))
        ims = []
        for r in R:
            prev = xm[r - 1][LTOK - 1] if r > 0 else np.zeros(D, np.float32)
            nxt = xm[r + 1][0] if r < NCORES - 1 else np.zeros(D, np.float32)
            hm = np.zeros((128, 2), np.float32)
            hm[:, 0] = 1.0 if r > 0 else 0.0
            hm[:, 1] = 1.0 if r < NCORES - 1 else 0.0
            ims.append({"x": xm[r], "xhT": np.ascontiguousarray(np.stack([_fm(prev), _fm(nxt)], -1)), "hmask": hm,
                        "wu": f32(ffn_w_up[l]), "wd": f32(ffn_w_down[l]), "cw": cw, "modv": modv2, "vecs": vecs2, "ident": _IDENT})
        fr = pf.run(ims)
        x_lat = np.concatenate([fr[r]["xo"][:LTOK] for r in R], 0)
        x_ctx = fr[0]["xo"][LTOK:]
        _DBG["x%d" % l] = (x_lat, x_ctx)
        if _DBG.get("stop_after") == l:
            break
    return np.ascontiguousarray(x_lat[None].astype(np.float32))
```

```python
import numpy as np
import ml_dtypes
import concourse.bass as bass
import concourse.mybir as mybir
from concourse.bass_utils import run_bass_kernel_spmd

F32 = mybir.dt.float32
BF16 = mybir.dt.bfloat16
ALU = mybir.AluOpType
AF = mybir.ActivationFunctionType
AX = mybir.AxisListType
NPBF = ml_dtypes.bfloat16

NCORES = 8
D = 1024
S = 16384
DEPTH = 4
GRID_W = 64
CTX = 256
HD = 64
D_FF = 2816
LN_EPS = 1e-5
RMS_EPS = 1e-6
DN_ALPHA = (2 * DEPTH) ** 0.25
TL = 16
TC = 2
TT = TL + TC
NTOK = TT * 128
LTOK = TL * 128


class Buf:
    __slots__ = ("name", "t", "lw", "rd", "psum", "lwd")

    def __init__(self, name, t, psum=False):
        self.name = name
        self.t = t
        self.psum = psum
        self.lw = None
        self.rd = {}
        self.lwd = {}

    def __getitem__(self, k):
        return self.t[k]


class Sync:
    ENGS = ("pe", "act", "dve", "pool", "sp")

    def __init__(self):
        self.needed = set()

    def begin(self, nc, dry):
        self.nc = nc
        self.dry = dry
        self.idx = {e: 0 for e in self.ENGS}
        self.marks = {e: 0 for e in self.ENGS}
        self.markval = {}
        self.seen = {e: {o: 0 for o in self.ENGS} for e in self.ENGS}
        self.seen_dma = {}
        self.n_wait = 0
        self.ndsem = 0
        self.dsem_cnt = {}
        if not dry:
            self.eng = {"pe": nc.tensor, "act": nc.scalar, "dve": nc.vector,
                        "pool": nc.gpsimd, "sp": nc.sync}
            self.sem = {e: nc.alloc_semaphore("s_" + e) for e in self.ENGS}
            self.dsems = []

    def _deps(self, reads, writes):
        deps = set()
        for b in reads:
            if b.lw is not None:
                deps.add(b.lw)
            for w_ in b.lwd.values():
                deps.add(w_)
            if b.psum:
                for r in b.rd.values():
                    deps.add(r)
        for b in writes:
            if b.lw is not None:
                deps.add(b.lw)
            for w_ in b.lwd.values():
                deps.add(w_)
            for r in b.rd.values():
                deps.add(r)
        return deps

    def _post(self, me, reads, writes):
        key = me[0] if me[0] != "dma" else ("dma", me[1])
        for b in reads:
            b.rd[key] = me
        for b in writes:
            b.lw = me
            b.rd = {}
            if me[0] == "dma":
                b.lwd[me[1]] = me
            else:
                b.lwd = {}

    def op(self, e, fn, reads=(), writes=()):
        self._waits(e, self._deps(reads, writes))
        me = (e, self.idx[e])
        self.idx[e] += 1
        if me in self.needed:
            self.marks[e] += 1
            self.markval[me] = self.marks[e]
        if not self.dry:
            ins = fn()
            if me in self.needed:
                ins.then_inc(self.sem[e], 1)
        self._post(me, reads, writes)
        return me

    def _waits(self, e, deps):
        best = {}
        for d in deps:
            if d[0] == "dma":
                self._dma_wait(e, d)
                continue
            pe_, i_ = d
            if pe_ == e and e == "pe":
                continue
            if pe_ not in best or best[pe_] < i_:
                best[pe_] = i_
        for pe_, i_ in best.items():
            if self.dry:
                self.needed.add((pe_, i_))
                continue
            v = self.markval[(pe_, i_)]
            if self.seen[e][pe_] >= v:
                continue
            self.seen[e][pe_] = v
            self.eng[e].wait_ge(self.sem[pe_], v)
            self.n_wait += 1

    def new_dma_sem(self):
        k = self.ndsem
        self.ndsem += 1
        self.dsem_cnt[k] = 0
        if not self.dry:
            self.dsems.append(self.nc.alloc_semaphore("d%d" % k))
        return k

    def _dma_wait(self, e, d):
        k = d[1]
        v = self.dsem_cnt[k]
        if self.seen_dma.get((e, k), 0) >= v:
            return
        self.seen_dma[(e, k)] = v
        if not self.dry:
            self.eng[e].wait_ge(self.dsems[k], v)
            self.n_wait += 1

    def dma(self, e, k, fn, reads=(), writes=()):
        self._waits(e, self._deps(reads, writes))
        self.dsem_cnt[k] += 16
        me = ("dma", k, self.dsem_cnt[k])
        if not self.dry:
            fn().then_inc(self.dsems[k], 16)
        self._post(me, reads, writes)
        return me

    def wait_all(self, e, bufs):
        self._waits(e, self._deps((), bufs))


class Prog:
    def __init__(self, body):
        self.body = body
        self.sy = Sync()
        self.outs = []
        self._build(True)
        self._build(False)

    def _build(self, dry):
        self.dry = dry
        self.nc = None if dry else bass.Bass("TRN2", target_bir_lowering=False)
        self.sy.begin(self.nc, dry)
        self.outs = []
        self.body(self)
        self.sy.wait_all("sp", self.outs)
        self.sy.wait_all("pool", self.outs)

    def dram(self, name, shape, dt, kind="Internal"):
        b = Buf(name, None if self.dry else self.nc.dram_tensor(name, list(shape), dt, kind=kind).ap())
        if kind == "ExternalOutput":
            self.outs.append(b)
        return b

    def inp(self, name, shape, dt=F32):
        return self.dram(name, shape, dt, "ExternalInput")

    def out(self, name, shape, dt=F32):
        return self.dram(name, shape, dt, "ExternalOutput")

    def sb(self, name, shape, dt):
        return Buf(name, None if self.dry else self.nc.alloc_sbuf_tensor(name, list(shape), dt).ap())

    def ps(self, name, shape, dt=F32):
        return Buf(name, None if self.dry else self.nc.alloc_psum_tensor(name, list(shape), dt).ap(), psum=True)

    def sbs(self, name, n, shape, dt):
        return [self.sb("%s%d" % (name, i), shape, dt) for i in range(n)]

    def pss(self, name, n, shape, dt=F32):
        return [self.ps("%s%d" % (name, i), shape, dt) for i in range(n)]

    def run(self, in_maps):
        res = run_bass_kernel_spmd(self.nc, in_maps, core_ids=list(range(NCORES)))
        return res.results


def cap(buf, off, dims):
    return bass.AP(buf.t.tensor, off, [list(d) for d in dims])


def body_mod(p):
    nc, sy = p.nc, p.sy
    v = p.inp("v", [128, 8])
    aw = p.inp("aw", [D, 6 * D])
    ab = p.inp("ab", [1, 6 * D])
    m = p.out("m", [1, 6 * D])
    vs = p.sb("vs", [128, 8], F32)
    sv = p.sb("sv", [128, 8], F32)
    abs_ = p.sb("abs", [1, 6 * D], F32)
    ms = p.sb("ms", [1, 6 * D], F32)
    wts = p.sbs("wt", 2, [128, 8, 512], F32)
    pm = p.pss("pm", 2, [1, 512])
    d0 = sy.new_dma_sem()
    dw = [sy.new_dma_sem() for _ in range(2)]
    sy.dma("sp", d0, lambda: nc.sync.dma_start(out=vs[:, :], in_=v[:, :]), reads=[v], writes=[vs])
    sy.dma("sp", d0, lambda: nc.sync.dma_start(out=abs_[:, :], in_=ab[:, :]), reads=[ab], writes=[abs_])
    sy.op("act", lambda: nc.scalar.activation(out=sv[:, :], in_=vs[:, :], func=AF.Silu), reads=[vs], writes=[sv])
    for j in range(12):
        wt = wts[j % 2]
        sy.dma("sp", dw[j % 2], lambda j=j, wt=wt: nc.sync.dma_start(
            out=wt[:, :, :], in_=aw.t[:, j * 512:(j + 1) * 512].rearrange("(k p) n -> p k n", p=128)),
            reads=[aw], writes=[wt])
        pj = pm[j % 2]
        for k in range(8):
            sy.op("pe", lambda k=k, wt=wt, pj=pj: nc.tensor.matmul(
                pj[:, :], lhsT=sv[:, k:k + 1], rhs=wt[:, k, :], start=(k == 0), stop=(k == 7)),
                reads=[sv, wt], writes=[pj])
        sl = slice(j * 512, (j + 1) * 512)
        sy.op("dve", lambda pj=pj, sl=sl: nc.vector.tensor_tensor(
            out=ms[:, sl], in0=pj[:, :], in1=abs_[:, sl], op=ALU.add), reads=[pj, abs_], writes=[ms])
        if j in (2, 3, 8, 9):
            sy.op("dve", lambda sl=sl: nc.vector.tensor_scalar_add(out=ms[:, sl], in0=ms[:, sl], scalar1=1.0),
                  reads=[ms], writes=[ms])
    sy.dma("sp", d0, lambda: nc.sync.dma_start(out=m[:, :], in_=ms[:, :]), reads=[ms], writes=[m])


def make_body_proj(even):
    G = 4 if even else 8
    NQ = G * 128
    NW = 2304 if even else 1280
    HR = 2 * G + 2
    nblk = (NW + 511) // 512

    def body(p):
        nc, sy = p.nc, p.sy
        x = p.inp("x", [NTOK, D])
        w = p.inp("w", [D, NW])
        modv = p.inp("modv", [128, 8, 4])
        cosd = p.inp("cos", [128, TL, 64])
        sind = p.inp("sin", [128, TL, 64])
        identd = p.inp("ident", [128, 128])
        qT = p.out("qT", [128, TT, G, 128], BF16)
        kT = p.out("kT", [128, NTOK], BF16)
        vo = p.out("v", [NTOK, 128], BF16)
        if even:
            gains = p.inp("gains", [128, 2, 64])
            qbT = p.out("qbT", [128, 4, NTOK], BF16)
            kbT = p.out("kbT", [128, 4, NTOK], BF16)
            vbo = p.out("vb", [NTOK, 512], BF16)

        wb = p.sb("wb", [128, 8, NW], BF16)
        mods = p.sb("mods", [128, 8, 4], F32)
        coss = p.sb("coss", [128, TL, 64], F32)
        sins = p.sb("sins", [128, TL, 64], F32)
        ident = p.sb("ident_s", [128, 128], F32)
        identb = p.sb("identb", [128, 128], BF16)
        qT_s = p.sb("qT_s", [128, TT, G, 128], BF16)
        kT_s = p.sb("kT_s", [128, NTOK], BF16)
        if even:
            gs = p.sb("gs", [128, 2, 64], F32)
            qbT_s = p.sb("qbT_s", [128, 4, NTOK], BF16)
            kbT_s = p.sb("kbT_s", [128, 4, NTOK], BF16)
        xs = p.sbs("xs", 2, [128, D], F32)
        hT = p.sbs("hT", 2, [128, 8, 128], BF16)
        qkv = p.sbs("qkv", 2, [128, NW], F32)
        qkvb = p.sbs("qkvb", 3, [128, NW], BF16)
        t1 = p.sb("t1", [128, HR * 64], F32)
        t2 = p.sb("t2", [128, HR * 64], F32)
        ss = p.sb("ss", [128, 16], F32)
        pTs = p.pss("pT", 2, [128, 4, 128], F32)
        pq = p.pss("pq", 2, [128, 512], F32)
        ptr = p.pss("ptr", 2, [128, 8, 128], BF16)

        d0 = sy.new_dma_sem()
        dx = [sy.new_dma_sem() for _ in range(2)]
        dv = [sy.new_dma_sem() for _ in range(3)]
        dout = sy.new_dma_sem()
        sy.dma("sp", d0, lambda: nc.sync.dma_start(out=mods[:, :, :], in_=modv[:, :, :]), reads=[modv], writes=[mods])
        sy.dma("sp", d0, lambda: nc.sync.dma_start(out=coss[:, :, :], in_=cosd[:, :, :]), reads=[cosd], writes=[coss])
        sy.dma("sp", d0, lambda: nc.sync.dma_start(out=sins[:, :, :], in_=sind[:, :, :]), reads=[sind], writes=[sins])
        sy.dma("sp", d0, lambda: nc.sync.dma_start(out=ident[:, :], in_=identd[:, :]), reads=[identd], writes=[ident])
        if even:
            sy.dma("sp", d0, lambda: nc.sync.dma_start(out=gs[:, :, :], in_=gains[:, :, :]), reads=[gains], writes=[gs])
        dwt = sy.new_dma_sem()
        sy.dma("pool", dwt, lambda: nc.gpsimd.dma_start(
            out=wb[:, :, :], in_=w.t.rearrange("(k p) n -> p k n", p=128)), reads=[w], writes=[wb])
        sy.op("dve", lambda: nc.vector.tensor_copy(out=identb[:, :], in_=ident[:, :]), reads=[ident], writes=[identb])

        def load_x(t):
            b = xs[t % 2]
            sy.dma("sp", dx[t % 2], lambda: nc.sync.dma_start(out=b[:, :], in_=x[t * 128:(t + 1) * 128, :]),
                   reads=[x], writes=[b])

        LV = 9
        NT_ = TT
        load_x(0)
        for t in range(NT_):
            is_ctx = t >= TL
            mo = 2 if is_ctx else 0
            if t + 1 < NT_:
                load_x(t + 1)
            xb, hb, qb_, bb = xs[t % 2], hT[t % 2], qkv[t % 2], qkvb[t % 3]
            for c in range(8):
                sy.op("pe", lambda c=c, xb=xb: nc.tensor.transpose(pTs[c // 4][:, c % 4, :], xb[:, c * 128:(c + 1) * 128], ident[:, :]),
                      reads=[xb, ident], writes=[pTs[c // 4]])
            for c in range(8):
                pTc = pTs[c // 4]
                if c < 4:
                    sy.op("dve", lambda c=c, hb=hb, pTc=pTc: nc.vector.tensor_scalar(
                        out=hb[:, c, :], in0=pTc[:, c % 4, :], scalar1=mods[:, c, mo:mo + 1], scalar2=mods[:, c, mo + 1:mo + 2],
                        op0=ALU.mult, op1=ALU.add), reads=[pTc, mods], writes=[hb])
                else:
                    sy.op("act", lambda c=c, hb=hb, pTc=pTc: nc.scalar.activation(
                        out=hb[:, c, :], in_=pTc[:, c % 4, :], func=AF.Identity, bias=mods[:, c, mo + 1:mo + 2],
                        scale=mods[:, c, mo:mo + 1]), reads=[pTc, mods], writes=[hb])
            for j in range(nblk):
                n0, n1 = j * 512, min(NW, (j + 1) * 512)
                pj = pq[j % 2]
                for c in range(8):
                    sy.op("pe", lambda c=c, hb=hb, pj=pj, n0=n0, n1=n1: nc.tensor.matmul(
                        pj[:, 0:n1 - n0], lhsT=hb[:, c, :], rhs=wb[:, c, n0:n1], start=(c == 0), stop=(c == 7)),
                        reads=[hb, wb], writes=[pj])
                if j % 2 == 0:
                    sy.op("act", lambda pj=pj, n0=n0, n1=n1, qb_=qb_: nc.scalar.copy(out=qb_[:, n0:n1], in_=pj[:, 0:n1 - n0]),
                          reads=[pj], writes=[qb_])
                else:
                    sy.op("dve", lambda pj=pj, n0=n0, n1=n1, qb_=qb_: nc.vector.tensor_copy(out=qb_[:, n0:n1], in_=pj[:, 0:n1 - n0]),
                          reads=[pj], writes=[qb_])
            if LV < 2:
                continue
            if even and LV >= 3:
                sy.op("dve", lambda qb_=qb_: nc.vector.tensor_tensor(out=t1[:, 0:640], in0=qb_[:, 0:640], in1=qb_[:, 0:640], op=ALU.mult),
                      reads=[qb_], writes=[t1])
                sy.op("dve", lambda: nc.vector.tensor_reduce(out=ss[:, 0:10], in_=cap(t1, 0, [[HR * 64, 128], [64, 10], [1, 64]]),
                                                             axis=AX.X, op=ALU.add), reads=[t1], writes=[ss])
                sy.op("dve", lambda: nc.vector.tensor_scalar_add(out=ss[:, 0:10], in0=ss[:, 0:10], scalar1=64.0 * RMS_EPS), reads=[ss], writes=[ss])
                sy.op("act", lambda: nc.scalar.sqrt(out=ss[:, 0:10], in_=ss[:, 0:10]), reads=[ss], writes=[ss])
                sy.op("dve", lambda: nc.vector.reciprocal(out=ss[:, 0:10], in_=ss[:, 0:10]), reads=[ss], writes=[ss])
                sy.op("dve", lambda: nc.vector.tensor_scalar_mul(out=ss[:, 8:10], in0=ss[:, 8:10], scalar1=8.0), reads=[ss], writes=[ss])
                sy.op("dve", lambda qb_=qb_: nc.vector.tensor_tensor(
                    out=cap(qb_, 0, [[NW, 128], [64, 10], [1, 64]]), in0=cap(qb_, 0, [[NW, 128], [64, 10], [1, 64]]),
                    in1=cap(ss, 0, [[16, 128], [1, 10], [0, 64]]), op=ALU.mult), reads=[qb_, ss], writes=[qb_])
                sy.op("dve", lambda qb_=qb_: nc.vector.tensor_tensor(
                    out=cap(qb_, 0, [[NW, 128], [64, 8], [1, 64]]), in0=cap(qb_, 0, [[NW, 128], [64, 8], [1, 64]]),
                    in1=cap(gs, 0, [[128, 128], [0, 8], [1, 64]]), op=ALU.mult), reads=[qb_, gs], writes=[qb_])
                sy.op("dve", lambda qb_=qb_: nc.vector.tensor_tensor(
                    out=cap(qb_, 512, [[NW, 128], [64, 2], [1, 64]]), in0=cap(qb_, 512, [[NW, 128], [64, 2], [1, 64]]),
                    in1=cap(gs, 64, [[128, 128], [0, 2], [1, 64]]), op=ALU.mult), reads=[qb_, gs], writes=[qb_])
            if not is_ctx and LV >= 4:
                W_ = HR * 64
                sy.op("dve", lambda qb_=qb_, t=t: nc.vector.tensor_tensor(
                    out=cap(t1, 0, [[W_, 128], [64, HR], [1, 64]]), in0=cap(qb_, 0, [[NW, 128], [64, HR], [1, 64]]),
                    in1=cap(coss, t * 64, [[TL * 64, 128], [0, HR], [1, 64]]), op=ALU.mult), reads=[qb_, coss], writes=[t1])
                for hf in range(2):
                    sy.op("dve", lambda qb_=qb_, t=t, hf=hf: nc.vector.tensor_tensor(
                        out=cap(t2, hf * 16, [[W_, 128], [64, HR], [32, 2], [1, 16]]),
                        in0=cap(qb_, (1 - hf) * 16, [[NW, 128], [64, HR], [32, 2], [1, 16]]),
                        in1=cap(sins, t * 64 + hf * 16, [[TL * 64, 128], [0, HR], [32, 2], [1, 16]]), op=ALU.mult),
                        reads=[qb_, sins], writes=[t2])
                sy.op("dve", lambda qb_=qb_: nc.vector.tensor_tensor(out=qb_[:, 0:W_], in0=t1[:, 0:W_], in1=t2[:, 0:W_], op=ALU.add),
                      reads=[t1, t2], writes=[qb_])
            sy.op("act", lambda qb_=qb_, bb=bb: nc.scalar.activation(
                out=cap(bb, 0, [[NW, 128], [64, 2], [128, G], [1, 64]]),
                in_=cap(qb_, 0, [[NW, 128], [G * 64, 2], [64, G], [1, 64]]), func=AF.Copy, scale=(1.0 if even else 0.125)),
                reads=[qb_], writes=[bb])
            if even:
                sy.op("act", lambda qb_=qb_, bb=bb: nc.scalar.copy(out=bb[:, NQ:768], in_=qb_[:, NQ:768]), reads=[qb_], writes=[bb])
                sy.op("act", lambda qb_=qb_, bb=bb: nc.scalar.activation(out=bb[:, 768:1280], in_=qb_[:, 768:1280], func=AF.Copy, scale=0.125),
                      reads=[qb_], writes=[bb])
                sy.op("act", lambda qb_=qb_, bb=bb: nc.scalar.copy(out=bb[:, 1280:NW], in_=qb_[:, 1280:NW]), reads=[qb_], writes=[bb])
            else:
                sy.op("act", lambda qb_=qb_, bb=bb: nc.scalar.copy(out=bb[:, NQ:NW], in_=qb_[:, NQ:NW]), reads=[qb_], writes=[bb])
            dvk = dv[t % 3]
            sy.dma("sp", dvk, lambda bb=bb, t=t: nc.sync.dma_start(out=vo[t * 128:(t + 1) * 128, :], in_=bb[:, NQ + 128:NQ + 256]),
                   reads=[bb], writes=[vo])
            if even:
                sy.dma("sp", dvk, lambda bb=bb, t=t: nc.sync.dma_start(out=vbo[t * 128:(t + 1) * 128, :], in_=bb[:, 1792:2304]),
                       reads=[bb], writes=[vbo])
            if LV < 5:
                continue
            pa = ptr[t % 2]
            for g in range(G):
                sy.op("pe", lambda g=g, bb=bb, pa=pa: nc.tensor.transpose(
                    pa[:, g, :], bb[:, g * 128:(g + 1) * 128], identb[:, :]),
                    reads=[bb, identb], writes=[pa])
            sy.op("dve", lambda pa=pa, t=t: nc.vector.tensor_copy(out=qT_s[:, t, :, :], in_=pa[:, 0:G, :]),
                  reads=[pa], writes=[qT_s])
            if LV == 5:
                continue
            if even:
                pb = ptr[(t + 1) % 2]
                sy.op("pe", lambda bb=bb, pb=pb: nc.tensor.transpose(pb[:, 0, :], bb[:, NQ:NQ + 128], identb[:, :]),
                      reads=[bb, identb], writes=[pb])
                for c in range(4):
                    sy.op("pe", lambda c=c, bb=bb, pb=pb: nc.tensor.transpose(pb[:, 1 + c, :], bb[:, 768 + c * 128:768 + (c + 1) * 128], identb[:, :]),
                          reads=[bb, identb], writes=[pb])
                if LV != 7:
                    sy.op("act", lambda pb=pb, t=t: nc.scalar.copy(out=kT_s[:, t * 128:(t + 1) * 128], in_=pb[:, 0, :]),
                          reads=[pb], writes=[kT_s])
                if LV != 8:
                    sy.op("dve", lambda pb=pb, t=t: nc.vector.tensor_copy(out=qbT_s[:, :, t * 128:(t + 1) * 128], in_=pb[:, 1:5, :]),
                          reads=[pb], writes=[qbT_s])
                if LV in (7, 8):
                    continue
                if LV == 6:
                    continue
                for c in range(4):
                    sy.op("pe", lambda c=c, bb=bb, pa=pa: nc.tensor.transpose(pa[:, 4 + c, :], bb[:, 1280 + c * 128:1280 + (c + 1) * 128], identb[:, :]),
                          reads=[bb, identb], writes=[pa])
                sy.op("act", lambda pa=pa, t=t: nc.scalar.copy(out=kbT_s[:, :, t * 128:(t + 1) * 128], in_=pa[:, 4:8, :]),
                      reads=[pa], writes=[kbT_s])
            else:
                pb = ptr[(t + 1) % 2]
                sy.op("pe", lambda bb=bb, pb=pb: nc.tensor.transpose(pb[:, 0, :], bb[:, NQ:NQ + 128], identb[:, :]),
                      reads=[bb, identb], writes=[pb])
                sy.op("act", lambda pb=pb, t=t: nc.scalar.copy(out=kT_s[:, t * 128:(t + 1) * 128], in_=pb[:, 0, :]),
                      reads=[pb], writes=[kT_s])
        sy.dma("sp", dout, lambda: nc.sync.dma_start(out=qT[:, :, :, :], in_=qT_s[:, :, :, :]), reads=[qT_s], writes=[qT])
        sy.dma("sp", dout, lambda: nc.sync.dma_start(out=kT[:, :], in_=kT_s[:, :]), reads=[kT_s], writes=[kT])
        if even:
            sy.dma("sp", dout, lambda: nc.sync.dma_start(out=qbT[:, :, :], in_=qbT_s[:, :, :]), reads=[qbT_s], writes=[qbT])
            sy.dma("sp", dout, lambda: nc.sync.dma_start(out=kbT[:, :, :], in_=kbT_s[:, :, :]), reads=[kbT_s], writes=[kbT])

    return body


_PROGS = {}


def get_prog(name, body):
    if name not in _PROGS:
        _PROGS[name] = Prog(body)
    return _PROGS[name]


def make_body_attn(kind):
    G = {"A": 4, "C": 8, "B": 4}[kind]
    H = {"A": 8, "C": 16, "B": 8}[kind]
    if kind == "A":
        NKT = 128 + TC
    elif kind == "C":
        NKT = 1 + TL + 1 + TC
    else:
        NKT = 2 + TL + 2 + TC
    KC = 1 if kind != "B" else 4
    VH = 2 if kind != "B" else 8

    def body(p):
        nc, sy = p.nc, p.sy
        qT = p.inp("qT", [128, TT, G, 128], BF16)
        kT = p.inp("kT", [128, KC, NKT * 128], BF16)
        va = p.inp("va", [128, NKT, VH, 128], BF16)
        identd = p.inp("ident", [128, 128])
        yT = p.out("yT", [64, TT, H, 128], BF16)
        q_s = p.sb("q_s", [128, TT, G, 128], BF16)
        k_s = p.sb("k_s", [128, KC, NKT * 128], BF16)
        v_s = p.sb("v_s", [128, NKT, VH, 128], BF16)
        y_s = p.sb("y_s", [64, TT, H, 128], BF16)
        ident = p.sb("ident_s", [128, 128], F32)
        identb = p.sb("identb", [128, 128], BF16)
        pts = p.sbs("pt", 3, [128, 1024], BF16)
        NPT = 3
        rc = p.sbs("rc", 2, [128, 512], F32)
        rs = p.sbs("rs", 2, [64, 512], F32)
        NPS = 2 if kind == "C" else 3
        psS = p.pss("psS", NPS, [128, 1024], F32)
        psO = p.pss("psO", 4 if kind == "C" else 2, [128, 512], F32)
        d0 = sy.new_dma_sem()
        dk = sy.new_dma_sem()
        dvs = sy.new_dma_sem()
        dout = sy.new_dma_sem()
        sy.dma("sp", d0, lambda: nc.sync.dma_start(out=q_s[:, :, :, :], in_=qT[:, :, :, :]), reads=[qT], writes=[q_s])
        sy.dma("sp", d0, lambda: nc.sync.dma_start(out=ident[:, :], in_=identd[:, :]), reads=[identd], writes=[ident])
        step = 26 if kind == "A" else NKT
        for k0 in range(0, NKT, step):
            k1 = min(NKT, k0 + step)
            sy.dma("sp", dk, lambda k0=k0, k1=k1: nc.sync.dma_start(out=k_s[:, :, k0 * 128:k1 * 128], in_=kT[:, :, k0 * 128:k1 * 128]),
                   reads=[kT], writes=[k_s])
            sy.dma("sp", dvs, lambda k0=k0, k1=k1: nc.sync.dma_start(out=v_s[:, k0:k1, :, :], in_=va[:, k0:k1, :, :]),
                   reads=[va], writes=[v_s])
        sy.op("dve", lambda: nc.vector.tensor_copy(out=identb[:, :], in_=ident[:, :]), reads=[ident], writes=[identb])
        if kind == "C":
            maskd = p.inp("masks", [128, 4, 512])
            sinkd = p.inp("sink", [1, 16])
            mask_s = p.sb("mask_s", [128, 4, 512], BF16)
            sk = p.sb("sk", [128, 16], F32)
            ske = p.sb("ske", [128, 16], F32)
            dm = sy.new_dma_sem()
            sy.dma("pool", dm, lambda: nc.gpsimd.dma_start(out=mask_s[:, :, :], in_=maskd[:, :, :]), reads=[maskd], writes=[mask_s])
            sy.dma("sp", d0, lambda: nc.sync.dma_start(out=sk[:, :], in_=sinkd.t.partition_broadcast(128)), reads=[sinkd], writes=[sk])
            sy.op("act", lambda: nc.scalar.activation(out=ske[:, :], in_=sk[:, :], func=AF.Exp), reads=[sk], writes=[ske])
        if kind == "B":
            nbd = p.inp("nbias", [128, 5, 8, 768])
            nb_i = p.sb("nb_i", [128, 8, 768], BF16)
            nb_e = p.sb("nb_e", [128, 8, 768], BF16)
            dm = sy.new_dma_sem()
            dme = sy.new_dma_sem()
            sy.dma("pool", dm, lambda: nc.gpsimd.dma_start(out=nb_i[:, :, :], in_=nbd[:, 2, :, :]), reads=[nbd], writes=[nb_i])

        jobs = []
        for t in range(TT):
            is_ctx = t >= TL
            if kind == "A":
                kts = [(kt, None) for kt in (range(128, 130) if is_ctx else range(130))]
                jobs.append(dict(t=t, N=512, kts=kts, subs=[dict(pb=kv * 64, kc=0, qoff=0, vh=kv, h0=kv * 4, sink=None) for kv in range(2)]))
            elif kind == "C":
                if is_ctx:
                    kl = [(TL + 2, None), (TL + 3, None)]
                else:
                    mp = 2 if t == 0 else 0
                    mn = 3 if t == TL - 1 else 1
                    kl = [(t, mp), (t + 1, None), (t + 2, mn), (TL + 2, None), (TL + 3, None)]
                for hf in range(2):
                    jobs.append(dict(t=t, N=512, kts=kl, subs=[dict(pb=kv * 64, kc=0, qoff=hf * 4, vh=kv, h0=kv * 8 + hf * 4,
                                                                    sink=kv * 8 + hf * 4) for kv in range(2)]))
            else:
                if is_ctx:
                    kl = [(TL + 4, None), (TL + 5, None)]
                else:
                    cls = 0 if t == 0 else 1 if t == 1 else 3 if t == TL - 2 else 4 if t == TL - 1 else 2
                    lo = t - 1 if t == TL - 1 else t
                    nk = 6 if t in (0, TL - 1) else 5
                    kl = [(lo + i, (cls, i)) for i in range(nk)] + [(TL + 4, None), (TL + 5, None)]
                for h in range(8):
                    jobs.append(dict(t=t, N=128, kts=kl, subs=[dict(pb=(h % 2) * 64, kc=h // 2, qoff=h // 2, vh=h, h0=h, sink=None)]))

        steps = [(ji, ki) for ji, jb in enumerate(jobs) for ki in range(len(jb["kts"]))]
        LOOK = NPS - 1
        state = {"cls": None}

        def emit_qk(s_):
            ji, ki = steps[s_]
            jb = jobs[ji]
            t, N = jb["t"], jb["N"]
            kt, bias = jb["kts"][ki]
            pS = psS[s_ % NPS]
            if kind == "B" and t < TL and ki == 0:
                cls_t = jb["kts"][0][1][0]
                if cls_t != 2 and cls_t != state["cls"]:
                    state["cls"] = cls_t
                    sy.dma("pool", dme, lambda c5=cls_t: nc.gpsimd.dma_start(out=nb_e[:, :, :], in_=nbd[:, c5, :, :]),
                           reads=[nbd], writes=[nb_e])
            for ui_, sub in enumerate(jb["subs"]):
                pb_, kc, qoff = sub["pb"], sub["kc"], sub["qoff"]
                o0 = ui_ * 512
                sy.op("pe", lambda pb_=pb_, kc=kc, qoff=qoff, o0=o0: nc.tensor.matmul(
                    pS[:, o0:o0 + N], lhsT=k_s[pb_:pb_ + 64, kc, kt * 128:(kt + 1) * 128],
                    rhs=cap(q_s, pb_ * TT * G * 128 + (t * G + qoff) * 128, [[TT * G * 128, 64], [1, N]]),
                    start=True, stop=(bias is None or kind == "C")), reads=[k_s, q_s], writes=[pS])
                if bias is not None and kind == "B":
                    cls, wi = bias
                    h = sub["vh"]
                    nbb = nb_i if cls == 2 else nb_e
                    sy.op("pe", lambda nbb=nbb, h=h, wi=wi, o0=o0: nc.tensor.matmul(
                        pS[:, o0:o0 + N], lhsT=identb[:, :], rhs=nbb[:, h, wi * 128:(wi + 1) * 128], start=False, stop=True),
                        reads=[identb, nbb], writes=[pS])

        def emit_pv(s_):
            ji, ki = steps[s_]
            jb = jobs[ji]
            t, N = jb["t"], jb["N"]
            kt, bias = jb["kts"][ki]
            nkt = len(jb["kts"])
            nsub = len(jb["subs"])
            Wd = 512 * (nsub - 1) + N
            pS, pt = psS[s_ % NPS], pts[s_ % NPT]
            sy.op("act", lambda: nc.scalar.activation(out=pt[:, 0:Wd], in_=pS[:, 0:Wd], func=AF.Exp), reads=[pS], writes=[pt])
            if kind == "C" and bias is not None:
                sy.op("dve", lambda: nc.vector.tensor_tensor(
                    out=cap(pt, 0, [[1024, 128], [512, nsub], [1, 512]]), in0=cap(pt, 0, [[1024, 128], [512, nsub], [1, 512]]),
                    in1=cap(mask_s, bias * 512, [[4 * 512, 128], [0, nsub], [1, 512]]), op=ALU.mult), reads=[pt, mask_s], writes=[pt])
            for ui_, sub in enumerate(jb["subs"]):
                po = (psO[(2 * (ji % 2) + ui_) % len(psO)] if nsub == 2 else psO[ji % 2])
                vh = sub["vh"]
                o0 = ui_ * 512
                sy.op("pe", lambda po=po, vh=vh, o0=o0: nc.tensor.matmul(
                    po[:, 0:N], lhsT=v_s[:, kt, vh, :], rhs=pt[:, o0:o0 + N], start=(ki == 0), stop=(ki == nkt - 1)),
                    reads=[v_s, pt], writes=[po])
            if ki != nkt - 1:
                return
            for ui_, sub in enumerate(jb["subs"]):
                po = (psO[(2 * (ji % 2) + ui_) % len(psO)] if nsub == 2 else psO[ji % 2])
                r1, r2 = rc[(ji + ui_) % 2], rs[(ji + ui_) % 2]
                if sub["sink"] is not None:
                    s0 = sub["sink"]
                    sy.op("dve", lambda po=po, r1=r1, s0=s0: nc.vector.tensor_tensor(
                        out=cap(r1, 64 * 512, [[512, 64], [128, 4], [1, 128]]), in0=cap(po, 64 * 512, [[512, 64], [128, 4], [1, 128]]),
                        in1=cap(ske, 64 * 16 + s0, [[16, 64], [1, 4], [0, 128]]), op=ALU.add), reads=[po, ske], writes=[r1])
                    sy.op("dve", lambda r1=r1: nc.vector.reciprocal(out=r1[64:128, 0:N], in_=r1[64:128, 0:N]), reads=[r1], writes=[r1])
                else:
                    sy.op("dve", lambda po=po, r1=r1: nc.vector.reciprocal(out=r1[64:128, 0:N], in_=po[64:128, 0:N]), reads=[po], writes=[r1])
                sy.op("dve", lambda r1=r1, r2=r2: nc.vector.tensor_copy(out=r2[0:64, 0:N], in_=r1[64:128, 0:N]), reads=[r1], writes=[r2])
                h0 = sub["h0"]
                sy.op("dve", lambda po=po, r2=r2, h0=h0: nc.vector.tensor_tensor(
                    out=cap(y_s, (t * H + h0) * 128, [[TT * H * 128, 64], [1, N]]), in0=po[0:64, 0:N], in1=r2[0:64, 0:N], op=ALU.mult),
                    reads=[po, r2], writes=[y_s])

        ns = len(steps)
        for s_ in range(ns + LOOK):
            if s_ < ns:
                emit_qk(s_)
            if s_ - LOOK >= 0:
                emit_pv(s_ - LOOK)
        sy.dma("sp", dout, lambda: nc.sync.dma_start(out=yT[:, :, :, :], in_=y_s[:, :, :, :]), reads=[y_s], writes=[yT])

    return body


def ln_tile(p, nc, sy, z, tmp, st, gvec, bvec, out_t):
    sy.op("dve", lambda: nc.vector.memset(st[:, 0:2], 0.0), writes=[st])
    sy.op("act", lambda: nc.scalar.activation(out=tmp[:, :], in_=z[:, :], func=AF.Identity, accum_out=st[:, 0:1]),
          reads=[z, st], writes=[tmp, st])
    sy.op("act", lambda: nc.scalar.activation(out=tmp[:, :], in_=z[:, :], func=AF.Square, accum_out=st[:, 1:2]),
          reads=[z, st], writes=[tmp, st])
    sy.op("dve", lambda: nc.vector.tensor_scalar_mul(out=st[:, 2:3], in0=st[:, 0:1], scalar1=1.0 / D), reads=[st], writes=[st])
    sy.op("dve", lambda: nc.vector.tensor_tensor(out=st[:, 3:4], in0=st[:, 2:3], in1=st[:, 2:3], op=ALU.mult), reads=[st], writes=[st])
    sy.op("dve", lambda: nc.vector.scalar_tensor_tensor(out=st[:, 4:5], in0=st[:, 1:2], scalar=1.0 / D, in1=st[:, 3:4],
                                                        op0=ALU.mult, op1=ALU.subtract), reads=[st], writes=[st])
    sy.op("dve", lambda: nc.vector.tensor_scalar_add(out=st[:, 4:5], in0=st[:, 4:5], scalar1=LN_EPS), reads=[st], writes=[st])
    sy.op("act", lambda: nc.scalar.sqrt(out=st[:, 5:6], in_=st[:, 4:5]), reads=[st], writes=[st])
    sy.op("dve", lambda: nc.vector.reciprocal(out=st[:, 6:7], in_=st[:, 5:6]), reads=[st], writes=[st])
    sy.op("dve", lambda: nc.vector.tensor_scalar(out=tmp[:, :], in0=z[:, :], scalar1=st[:, 2:3], scalar2=st[:, 6:7],
                                                 op0=ALU.subtract, op1=ALU.mult), reads=[z, st], writes=[tmp])
    sy.op("pool", lambda: nc.gpsimd.tensor_tensor(out=tmp[:, :], in0=tmp[:, :], in1=gvec, op=ALU.mult), reads=[tmp], writes=[tmp])
    sy.op("dve", lambda: nc.vector.tensor_tensor(out=out_t[:, :], in0=tmp[:, :], in1=bvec, op=ALU.add), reads=[tmp], writes=[out_t])


def body_oproj(p):
    nc, sy = p.nc, p.sy
    yT = p.inp("yT", [64, TT, 16, 128], BF16)
    wo = p.inp("wo", [D, D])
    x = p.inp("x", [NTOK, D])
    vecs = p.inp("vecs", [128, 4, D])
    xo = p.out("xo", [NTOK, D])
    y_s = p.sb("y_s", [64, TT, 16, 128], BF16)
    wo_s = p.sb("wo_s", [64, 16, D], BF16)
    vs = p.sb("vs", [128, 4, D], F32)
    xs = p.sbs("xs", 2, [128, D], F32)
    zs = p.sbs("z", 2, [128, D], F32)
    tmps = p.sbs("tmp", 2, [128, D], F32)
    outs = p.sbs("ot", 2, [128, D], F32)
    sts = p.sbs("st", 2, [128, 8], F32)
    po = p.pss("po", 4, [128, 512], F32)
    d0 = sy.new_dma_sem()
    dw = sy.new_dma_sem()
    dx = [sy.new_dma_sem() for _ in range(2)]
    do = [sy.new_dma_sem() for _ in range(2)]
    sy.dma("sp", d0, lambda: nc.sync.dma_start(out=y_s[:, :, :, :], in_=yT[:, :, :, :]), reads=[yT], writes=[y_s])
    sy.dma("sp", d0, lambda: nc.sync.dma_start(out=vs[:, :, :], in_=vecs[:, :, :]), reads=[vecs], writes=[vs])
    sy.dma("pool", dw, lambda: nc.gpsimd.dma_start(out=wo_s[:, :, :], in_=wo.t.rearrange("(h d) n -> d h n", d=64)),
           reads=[wo], writes=[wo_s])

    def load_x(t):
        b = xs[t % 2]
        sy.dma("sp", dx[t % 2], lambda: nc.sync.dma_start(out=b[:, :], in_=x[t * 128:(t + 1) * 128, :]), reads=[x], writes=[b])

    load_x(0)
    for t in range(TT):
        if t + 1 < TT:
            load_x(t + 1)
        xb, z, tmp, ot, st = xs[t % 2], zs[t % 2], tmps[t % 2], outs[t % 2], sts[t % 2]
        gi = 1 if t >= TL else 0
        for nb in range(2):
            pj = po[(2 * t + nb) % 4]
            for h in range(16):
                sy.op("pe", lambda h=h, pj=pj, nb=nb, t=t: nc.tensor.matmul(
                    pj[:, :], lhsT=y_s[0:64, t, h, :], rhs=wo_s[0:64, h, nb * 512:(nb + 1) * 512], start=(h == 0), stop=(h == 15)),
                    reads=[y_s, wo_s], writes=[pj])
            sl = slice(nb * 512, (nb + 1) * 512)
            sy.op("dve", lambda pj=pj, sl=sl, tmp=tmp, gi=gi: nc.vector.tensor_tensor(
                out=tmp[:, sl], in0=pj[:, :], in1=vs[:, gi, sl], op=ALU.mult), reads=[pj, vs], writes=[tmp])
        sy.op("dve", lambda xb=xb, tmp=tmp, z=z: nc.vector.scalar_tensor_tensor(
            out=z[:, :], in0=xb[:, :], scalar=DN_ALPHA, in1=tmp[:, :], op0=ALU.mult, op1=ALU.add), reads=[xb, tmp], writes=[z])
        ln_tile(p, nc, sy, z, tmp, st, None if p.dry else vs[:, 2, :], None if p.dry else vs[:, 3, :], ot)
        sy.dma("sp", do[t % 2], lambda ot=ot, t=t: nc.sync.dma_start(out=xo[t * 128:(t + 1) * 128, :], in_=ot[:, :]),
               reads=[ot], writes=[xo])


NFF = D_FF // 128


def body_ffn(p):
    nc, sy = p.nc, p.sy
    x = p.inp("x", [NTOK, D])
    xhT = p.inp("xhT", [128, 8, 2])
    hmask = p.inp("hmask", [128, 2])
    wu = p.inp("wu", [D, 2 * D_FF])
    wd = p.inp("wd", [D_FF, D])
    cw = p.inp("cw", [128, 2 * NFF, 4])
    modv = p.inp("modv", [128, 8, 4])
    vecs = p.inp("vecs", [128, 4, D])
    identd = p.inp("ident", [128, 128])
    xo = p.out("xo", [NTOK, D])
    actD = p.dram("actD", [128, NFF, NTOK], BF16)

    GT = 1024
    hT_l = p.sb("hT_l", [128, 8, LTOK + 2], BF16)
    hT_c = p.sb("hT_c", [128, 8, CTX + 2], BF16)
    wd_s = p.sb("wd_s", [128, NFF, D], BF16)
    wus = p.sbs("wu_s", 2, [128, 8, 256], BF16)
    uas = p.sbs("ua", 2, [128, GT + 2], F32)
    ugs = p.sbs("ug", 2, [128, GT + 2], F32)
    tas = p.sbs("ta", 2, [128, GT], F32)
    tgs = p.sbs("tg", 2, [128, GT], F32)
    acts = p.sbs("acs", 2, [128, GT], BF16)
    actin = p.sbs("actin", 2, [128, NFF, 256], BF16)
    cw_s = p.sb("cw_s", [128, 2 * NFF, 4], F32)
    mods = p.sb("mods", [128, 8, 4], F32)
    vs = p.sb("vs", [128, 4, D], F32)
    ident = p.sb("ident_s", [128, 128], F32)
    xh_s = p.sb("xh_s", [128, 8, 2], F32)
    hm_s = p.sb("hm_s", [128, 2], F32)
    sts = p.sbs("st", 2, [128, 8], F32)
    pu = p.pss("pu", 4, [128, 512], F32)
    pus = p.pss("pus", 2, [128, 512], F32)
    pd = p.pss("pd", 2, [128, 512], F32)
    d0 = sy.new_dma_sem()
    dwd = sy.new_dma_sem()
    dwu = [sy.new_dma_sem() for _ in range(2)]
    dx = [sy.new_dma_sem() for _ in range(2)]
    do = [sy.new_dma_sem() for _ in range(2)]
    dact = [sy.new_dma_sem() for _ in range(2)]
    dain = [sy.new_dma_sem() for _ in range(2)]
    for dst, src in ((cw_s, cw), (mods, modv), (vs, vecs), (xh_s, xhT)):
        sy.dma("sp", d0, lambda dst=dst, src=src: nc.sync.dma_start(out=dst.t, in_=src.t), reads=[src], writes=[dst])
    sy.dma("sp", d0, lambda: nc.sync.dma_start(out=ident[:, :], in_=identd[:, :]), reads=[identd], writes=[ident])
    sy.dma("sp", d0, lambda: nc.sync.dma_start(out=hm_s[:, :], in_=hmask[:, :]), reads=[hmask], writes=[hm_s])

    def load_w(j):
        ws = wus[j % 2]
        dk = dwu[j % 2]
        sy.dma("pool", dk, lambda: nc.gpsimd.dma_start(
            out=ws[:, :, 0:128], in_=wu.t[:, j * 128:(j + 1) * 128].rearrange("(k p) n -> p k n", p=128)), reads=[wu], writes=[ws])
        sy.dma("pool", dk, lambda: nc.gpsimd.dma_start(
            out=ws[:, :, 128:256], in_=wu.t[:, D_FF + j * 128:D_FF + (j + 1) * 128].rearrange("(k p) n -> p k n", p=128)),
            reads=[wu], writes=[ws])

    load_w(0)
    load_w(1)
    sy.dma("pool", dwd, lambda: nc.gpsimd.dma_start(out=wd_s[:, :, :], in_=wd.t.rearrange("(j p) n -> p j n", p=128)),
           reads=[wd], writes=[wd_s])

    xs = [uas[0], uas[1]]

    def load_x(t):
        b = xs[t % 2]
        sy.dma("sp", dx[t % 2], lambda: nc.sync.dma_start(out=b[:, 0:D], in_=x[t * 128:(t + 1) * 128, :]), reads=[x], writes=[b])

    load_x(0)
    for t in range(TT):
        if t + 1 < TT:
            load_x(t + 1)
        xb = xs[t % 2]
        is_ctx = t >= TL
        mo = 2 if is_ctx else 0
        dstb = hT_c if is_ctx else hT_l
        c0 = 1 + (t - TL if is_ctx else t) * 128
        for c in range(8):
            pT = pu[c // 4 + 2 * (t % 2)]
            sy.op("pe", lambda c=c, xb=xb, pT=pT: nc.tensor.transpose(pT[:, (c % 4) * 128:(c % 4 + 1) * 128], xb[:, c * 128:(c + 1) * 128], ident[:, :]),
                  reads=[xb, ident], writes=[pT])
        for c in range(8):
            pT = pu[c // 4 + 2 * (t % 2)]
            if c < 4:
                sy.op("dve", lambda c=c, pT=pT, dstb=dstb, c0=c0, mo=mo: nc.vector.tensor_scalar(
                    out=dstb[:, c, c0:c0 + 128], in0=pT[:, (c % 4) * 128:(c % 4 + 1) * 128], scalar1=mods[:, c, mo:mo + 1],
                    scalar2=mods[:, c, mo + 1:mo + 2], op0=ALU.mult, op1=ALU.add), reads=[pT, mods], writes=[dstb])
            else:
                sy.op("act", lambda c=c, pT=pT, dstb=dstb, c0=c0, mo=mo: nc.scalar.activation(
                    out=dstb[:, c, c0:c0 + 128], in_=pT[:, (c % 4) * 128:(c % 4 + 1) * 128], func=AF.Identity,
                    bias=mods[:, c, mo + 1:mo + 2], scale=mods[:, c, mo:mo + 1]), reads=[pT, mods], writes=[dstb])
    for j, col in ((0, 0), (1, LTOK + 1)):
        sy.op("dve", lambda j=j: nc.vector.tensor_tensor(out=xh_s[:, :, j], in0=xh_s[:, :, j], in1=mods[:, :, 0], op=ALU.mult),
              reads=[xh_s, mods], writes=[xh_s])
        sy.op("dve", lambda j=j: nc.vector.tensor_tensor(out=xh_s[:, :, j], in0=xh_s[:, :, j], in1=mods[:, :, 1], op=ALU.add),
              reads=[xh_s, mods], writes=[xh_s])
        sy.op("dve", lambda j=j, col=col: nc.vector.tensor_scalar(out=hT_l[:, :, col], in0=xh_s[:, :, j], scalar1=hm_s[:, j:j + 1],
                                                                scalar2=None, op0=ALU.mult), reads=[xh_s, hm_s], writes=[hT_l])
    sy.op("dve", lambda: nc.vector.memset(hT_c[:, :, 0], 0.0), reads=[hT_c], writes=[hT_c])
    sy.op("dve", lambda: nc.vector.memset(hT_c[:, :, CTX + 1], 0.0), reads=[hT_c], writes=[hT_c])

    groups = [(hT_l, g * GT, GT, g * GT) for g in range(LTOK // GT)] + [(hT_c, 0, CTX, LTOK)]
    ui = 0
    for j in range(NFF):
        ws = wus[j % 2]
        for (hb, c0, gt, tok0) in groups:
            ua, ug, ta, tg, ac = uas[ui % 2], ugs[ui % 2], tas[ui % 2], tgs[ui % 2], acts[ui % 2]
            ui += 1
            blocks = [(b0, min(b0 + 512, gt + 2)) for b0 in range(0, gt + 2, 512)]
            for bi, (b0, b1) in enumerate(blocks):
                for br, (ub, woff) in enumerate(((ua, 0), (ug, 128))):
                    pp = pus[br] if (b1 - b0) < 16 else pu[(2 * bi + br) % 4]
                    for k in range(8):
                        sy.op("pe", lambda k=k, pp=pp, woff=woff, b0=b0, b1=b1: nc.tensor.matmul(
                            pp[:, 0:b1 - b0], lhsT=ws[:, k, woff:woff + 128], rhs=hb[:, k, c0 + b0:c0 + b1],
                            start=(k == 0), stop=(k == 7)), reads=[ws, hb], writes=[pp])
                    sy.op("act", lambda pp=pp, ub=ub, b0=b0, b1=b1: nc.scalar.copy(out=ub[:, b0:b1], in_=pp[:, 0:b1 - b0]),
                          reads=[pp], writes=[ub])
            ch = j
            sy.op("dve", lambda ch=ch: nc.vector.tensor_scalar(
                out=ta[:, 0:gt], in0=ua[:, 1:gt + 1], scalar1=cw_s[:, ch, 1:2], scalar2=cw_s[:, ch, 3:4],
                op0=ALU.mult, op1=ALU.add), reads=[ua, cw_s], writes=[ta])
            for tap in (0, 2):
                sy.op("dve", lambda ch=ch, tap=tap: nc.vector.scalar_tensor_tensor(
                    out=ta[:, 0:gt], in0=ua[:, tap:gt + tap], scalar=cw_s[:, ch, tap:tap + 1], in1=ta[:, 0:gt],
                    op0=ALU.mult, op1=ALU.add), reads=[ua, cw_s, ta], writes=[ta])
            ch = NFF + j
            sy.op("pool", lambda ch=ch: nc.gpsimd.tensor_scalar(
                out=tg[:, 0:gt], in0=ug[:, 1:gt + 1], scalar1=cw_s[:, ch, 1:2], scalar2=cw_s[:, ch, 3:4],
                op0=ALU.mult, op1=ALU.add), reads=[ug, cw_s], writes=[tg])
            for tap in (0, 2):
                sy.op("dve", lambda ch=ch, tap=tap: nc.vector.scalar_tensor_tensor(
                    out=tg[:, 0:gt], in0=ug[:, tap:gt + tap], scalar=cw_s[:, ch, tap:tap + 1], in1=tg[:, 0:gt],
                    op0=ALU.mult, op1=ALU.add), reads=[ug, cw_s, tg], writes=[tg])
            sy.op("act", lambda: nc.scalar.activation(out=tg[:, 0:gt], in_=tg[:, 0:gt], func=AF.Silu), reads=[tg], writes=[tg])
            sy.op("dve", lambda: nc.vector.tensor_tensor(out=ac[:, 0:gt], in0=tg[:, 0:gt], in1=ta[:, 0:gt], op=ALU.mult),
                  reads=[tg, ta], writes=[ac])
            sy.dma("sp", dact[ui % 2], lambda: nc.sync.dma_start(out=actD[:, j, tok0:tok0 + gt], in_=ac[:, 0:gt]),
                   reads=[ac], writes=[actD])
        if j + 2 < NFF:
            load_w(j + 2)

    def load_act(gi):
        b = actin[gi % 2]
        sy.dma("sp", dain[gi % 2], lambda: nc.sync.dma_start(out=b[:, :, :], in_=actD[:, :, gi * 256:(gi + 1) * 256]),
               reads=[actD], writes=[b])

    load_act(0)
    for gi in range(TT // 2):
        if gi + 1 < TT // 2:
            load_act(gi + 1)
        ab = actin[gi % 2]
        for ti in range(2):
            t = gi * 2 + ti
            load_x(t)
            xb, z, tmp, ot, st = xs[t % 2], tas[0], tgs[0], (tas[1] if t % 2 == 0 else tgs[1]), sts[t % 2]
            gsel = 1 if t >= TL else 0
            for nb in range(2):
                pj = pd[nb]
                for j in range(NFF):
                    sy.op("pe", lambda j=j, pj=pj, nb=nb, ti=ti: nc.tensor.matmul(
                        pj[:, :], lhsT=ab[:, j, ti * 128:(ti + 1) * 128], rhs=wd_s[:, j, nb * 512:(nb + 1) * 512],
                        start=(j == 0), stop=(j == NFF - 1)), reads=[ab, wd_s], writes=[pj])
                sl = slice(nb * 512, (nb + 1) * 512)
                sy.op("dve", lambda pj=pj, sl=sl, gsel=gsel: nc.vector.tensor_tensor(
                    out=tmp[:, sl], in0=pj[:, :], in1=vs[:, gsel, sl], op=ALU.mult), reads=[pj, vs], writes=[tmp])
            sy.op("dve", lambda xb=xb: nc.vector.scalar_tensor_tensor(
                out=z[:, :], in0=xb[:, 0:D], scalar=DN_ALPHA, in1=tmp[:, :], op0=ALU.mult, op1=ALU.add), reads=[xb, tmp], writes=[z])
            ln_tile(p, nc, sy, z, tmp, st, None if p.dry else vs[:, 2, :], None if p.dry else vs[:, 3, :], ot)
            sy.dma("sp", do[t % 2], lambda ot=ot, t=t: nc.sync.dma_start(out=xo[t * 128:(t + 1) * 128, :], in_=ot[:, :]),
                   reads=[ot], writes=[xo])


def _fm(v):
    return np.ascontiguousarray(np.asarray(v, np.float32).reshape(8, 128).T)


def _bc(v):
    return np.broadcast_to(np.asarray(v, np.float32)[None, :], (128, v.shape[-1]))


def _rope_tables(base):
    t = np.arange(base, base + LTOK)
    row = (t // GRID_W).astype(np.float32)
    col = (t % GRID_W).astype(np.float32)
    half = HD // 2
    inv = (np.float32(10000.0) ** (-np.arange(0, half, 2, dtype=np.float32) / np.float32(half))).astype(np.float32)
    ar = row[:, None] * inv
    ac = col[:, None] * inv
    ang = np.concatenate([ar, ar, ac, ac], -1).astype(np.float32)
    cos = np.cos(ang).astype(np.float32)
    sin = np.sin(ang).astype(np.float32)
    sgn = np.concatenate([-np.ones(16), np.ones(16), -np.ones(16), np.ones(16)]).astype(np.float32)
    pm = lambda a: np.ascontiguousarray(a.reshape(TL, 128, 64).transpose(1, 0, 2))
    return pm(cos), pm(sin * sgn)


NEGM = -30000.0


def _nbias_core(rpb, r):
    out = np.full((128, 5, 8, 6, 128), NEGM, np.float32)
    for cls, t in enumerate((0, 1, 5, TL - 2, TL - 1)):
        b = TL * r + t
        lo = t - 1 if t == TL - 1 else t
        nk = 6 if t in (0, TL - 1) else 5
        b0 = TL * r + lo - 2
        j = np.arange(128)[:, None, None]
        wi = np.arange(nk)[None, :, None]
        i = np.arange(128)[None, None, :]
        ktok = (b0 + wi) * 128 + j
        qtok = b * 128 + i
        krow, kcol = ktok // GRID_W, ktok % GRID_W
        row, col = qtok // GRID_W, qtok % GRID_W
        rs_ = np.clip(row - 4, 0, S // GRID_W - 8)
        cs_ = np.clip(col - 8, 0, GRID_W - 16)
        valid = (krow >= rs_) & (krow < rs_ + 8) & (kcol >= cs_) & (kcol < cs_ + 16) & (ktok >= 0) & (ktok < S)
        dr = np.clip(krow - row + 7, 0, 14)
        dc = np.clip(kcol - col + 15, 0, 30)
        vals = rpb[:, dr, dc]
        vals = np.where(valid[None], vals, np.float32(NEGM))
        out[:, cls, :, :nk, :] = vals.transpose(1, 0, 2, 3)
    return np.ascontiguousarray(out.reshape(128, 5, 8, 768))


def _cmasks_core(r):
    j = np.arange(128)[:, None]
    i = np.arange(128)[None, :]
    prev = np.where(j >= i, 1.0, 0.0).astype(np.float32)
    nxt = np.where(j <= i, 1.0, 0.0).astype(np.float32)
    allm = np.zeros((128, 128), np.float32)
    m = np.stack([prev, nxt, allm if r == 0 else prev, allm if r == NCORES - 1 else nxt], 1)
    return np.ascontiguousarray(np.broadcast_to(m[:, :, None, :], (128, 4, 4, 128)).reshape(128, 4, 512))


def _aug_v(v_tok, nh):
    n = v_tok.shape[0]
    a = np.ones((n, nh, 128), NPBF)
    a[:, :, :64] = v_tok.reshape(n, nh, 64)
    return np.ascontiguousarray(a.reshape(n // 128, 128, nh, 128).transpose(1, 0, 2, 3))


_IDENT = np.eye(128, dtype=np.float32)
_DBG = {}


def kernel(x, c, ctx, c_ctx, ada_w, ada_b, ln_g, ln_b, ev_w_in, ev_w_out, ev_q_gain, ev_k_gain, ev_rpb,
           od_w_in, od_w_out, od_sink, ffn_w_up, ffn_conv_w, ffn_conv_b, ffn_w_down):
    f32 = lambda a: np.ascontiguousarray(np.asarray(a, np.float32))
    x, c, ctx, c_ctx = f32(x), f32(c), f32(ctx), f32(c_ctx)
    ada_w, ada_b, ln_g, ln_b = f32(ada_w), f32(ada_b), f32(ln_g), f32(ln_b)
    R = range(NCORES)

    pm = get_prog("M", body_mod)
    res = pm.run([{"v": _fm((c[0] if r % 2 == 0 else c_ctx)), "aw": ada_w[r // 2], "ab": ada_b[r // 2][None, :]} for r in R])
    mod = [[res[2 * l + s]["m"].reshape(6, D) for s in range(2)] for l in range(DEPTH)]

    x_lat = x[0]
    x_ctx = ctx[0]
    for l in range(DEPTH):
        i = l // 2
        even = (l % 2 == 0)
        ml, mc = mod[l]
        x_loc = [np.concatenate([x_lat[r * LTOK:(r + 1) * LTOK], x_ctx], 0) for r in R]
        pp = get_prog("P%d" % even, make_body_proj(even))
        modv = np.ascontiguousarray(np.stack([_fm(ml[1]), _fm(ml[0]), _fm(mc[1]), _fm(mc[0])], -1))
        ims = []
        for r in R:
            cs, sn = _rope_tables(r * LTOK)
            im = {"x": x_loc[r], "w": f32(ev_w_in[i] if even else od_w_in[i]), "modv": modv, "cos": cs, "sin": sn, "ident": _IDENT}
            if even:
                im["gains"] = np.ascontiguousarray(np.broadcast_to(
                    np.stack([f32(ev_q_gain[i]), f32(ev_k_gain[i])], 0)[None], (128, 2, 64)))
            ims.append(im)
        pr = pp.run(ims)

        def halo_tok(key, nh_tiles, tokmajor):
            outl = []
            for r in R:
                ax = 0 if tokmajor else -1
                own = pr[r][key]
                take = lambda a, s0, s1: (a[s0:s1] if tokmajor else a[..., s0:s1])
                hw = nh_tiles * 128
                prev = take(pr[r - 1][key], LTOK - hw, LTOK) if r > 0 else np.zeros_like(take(own, 0, hw))
                nxt = take(pr[r + 1][key], 0, hw) if r < NCORES - 1 else np.zeros_like(take(own, 0, hw))
                outl.append(np.concatenate([prev, take(own, 0, LTOK), nxt, take(own, LTOK, NTOK)], ax))
            return outl

        if even:
            pa = get_prog("ATTA", make_body_attn("A"))
            k_all = np.concatenate([pr[r]["kT"][:, :LTOK] for r in R] + [pr[0]["kT"][:, LTOK:]], 1)[:, None, :]
            v_all = np.concatenate([pr[r]["v"][:LTOK] for r in R] + [pr[0]["v"][LTOK:]], 0)
            va = _aug_v(v_all, 2)
            ar = pa.run([{"qT": pr[r]["qT"], "kT": np.ascontiguousarray(k_all), "va": va, "ident": _IDENT} for r in R])
            pb = get_prog("ATTB", make_body_attn("B"))
            kh = halo_tok("kbT", 2, False)
            vh = halo_tok("vb", 2, True)
            rpb = f32(ev_rpb[i])
            br = pb.run([{"qT": np.ascontiguousarray(pr[r]["qbT"].reshape(128, 4, TT, 128).transpose(0, 2, 1, 3)),
                          "kT": np.ascontiguousarray(kh[r]), "va": _aug_v(vh[r], 8), "ident": _IDENT,
                          "nbias": _nbias_core(rpb, r)} for r in R])
            yT = [np.ascontiguousarray(np.concatenate([ar[r]["yT"], br[r]["yT"]], 2)) for r in R]
            wo = f32(ev_w_out[i])
        else:
            pc = get_prog("ATTC", make_body_attn("C"))
            kh = halo_tok("kT", 1, False)
            vh = halo_tok("v", 1, True)
            cims = [{"qT": pr[r]["qT"], "kT": np.ascontiguousarray(kh[r][:, None, :]), "va": _aug_v(vh[r], 2),
                     "ident": _IDENT, "masks": _cmasks_core(r), "sink": f32(od_sink[i])[None, :]} for r in R]
            _DBG["cims%d" % l] = cims
            cr = pc.run(cims)
            yT = [cr[r]["yT"] for r in R]
            _DBG["yC%d" % l] = yT
            _DBG["prC%d" % l] = pr
            wo = f32(od_w_out[i])
        po_ = get_prog("O", body_oproj)
        vecs = np.ascontiguousarray(np.stack([_bc(ml[2]), _bc(mc[2]), _bc(ln_g[l, 0]), _bc(ln_b[l, 0])], 1))
        orr = po_.run([{"yT": yT[r], "wo": wo, "x": x_loc[r], "vecs": vecs} for r in R])
        xm = [orr[r]["xo"] for r in R]
        _DBG["xm%d" % l] = xm
        pf = get_prog("F", body_ffn)
        modv2 = np.ascontiguousarray(np.stack([_fm(ml[4]), _fm(ml[3]), _fm(mc[4]), _fm(mc[3])], -1))
        vecs2 = np.ascontiguousarray(np.stack([_bc(ml[5]), _bc(mc[5]), _bc(ln_g[l, 1]), _bc(ln_b[l, 1])], 1))
        cw = np.stack([f32(ffn_conv_w[l])[0], f32(ffn_conv_w[l])[1], f32(ffn_conv_w[l])[2], f32(ffn_conv_b[l])], -1)
        cw = np.ascontiguousarray(cw.reshape(2 * NFF, 128, 4).transpose(1, 0, 2))
        ims = []
        for r in R:
            prev = xm[r - 1][LTOK - 1] if r > 0 else np.zeros(D, np.float32)
            nxt = xm[r + 1][0] if r < NCORES - 1 else np.zeros(D, np.float32)
            hm = np.zeros((128, 2), np.float32)
            hm[:, 0] = 1.0 if r > 0 else 0.0
            hm[:, 1] = 1.0 if r < NCORES - 1 else 0.0
            ims.append({"x": xm[r], "xhT": np.ascontiguousarray(np.stack([_fm(prev), _fm(nxt)], -1)), "hmask": hm,
                        "wu": f32(ffn_w_up[l]), "wd": f32(ffn_w_down[l]), "cw": cw, "modv": modv2, "vecs": vecs2, "ident": _IDENT})
        fr = pf.run(ims)
        x_lat = np.concatenate([fr[r]["xo"][:LTOK] for r in R], 0)
        x_ctx = fr[0]["xo"][LTOK:]
        _DBG["x%d" % l] = (x_lat, x_ctx)
        if _DBG.get("stop_after") == l:
            break
    return np.ascontiguousarray(x_lat[None].astype(np.float32))
```
